# Optimizing a Trainium2 kernel written in Bass

```python
import jax, jax.numpy as jnp
from jax import lax
import numpy as np

D_MODEL = 1024
BATCH = 8
SEQ = 8192
DEPTH = 4

GRID_W = 64
CTX_LEN = 256
Q_BLOCK = 128
ROPE_THETA = 10000.0
NORM_EPS = 1e-6

GQA_HEADS = 6
GQA_KV_HEADS = 2
GQA_GROUP = GQA_HEADS // GQA_KV_HEADS
HEAD_DIM = 64
MLA_HEADS = 6
MLA_NOPE = 64
MLA_ROPE = 32
MLA_QK = MLA_NOPE + MLA_ROPE
MLA_V = 64
MLA_Q_RANK = 256
MLA_KV_RANK = 128
POOL_WINDOWS = (2, 4, 8, 16)
POOL_GROUPS = len(POOL_WINDOWS)
POOL_GROUP_DIM = 64
POOL_WIDTH = POOL_GROUPS * POOL_GROUP_DIM

MIX_WIDTH = GQA_HEADS * HEAD_DIM + MLA_HEADS * MLA_V + POOL_WIDTH
IN_SIZES = (GQA_HEADS * HEAD_DIM, GQA_KV_HEADS * HEAD_DIM, GQA_KV_HEADS * HEAD_DIM,
            MLA_Q_RANK, MLA_KV_RANK, MLA_ROPE, POOL_WIDTH)
IN_WIDTH = sum(IN_SIZES)
D_FF = 4 * D_MODEL
N_MOD = 6

kernel_name = "hybrid_headgroup_diffusion_prefix_trunk"


def rms_norm(x, g):
    xf = x.astype(jnp.float32)
    y = xf * lax.rsqrt(jnp.mean(xf * xf, axis=-1, keepdims=True) + NORM_EPS)
    return (y * g.astype(jnp.float32)).astype(x.dtype)


def modulate(h, shift, scale):
    return h * (1.0 + scale) + shift


def axial_rope_tables(length, rot_dim):
    rows = length // GRID_W
    row = jnp.repeat(jnp.arange(rows, dtype=jnp.float32), GRID_W)
    col = jnp.tile(jnp.arange(GRID_W, dtype=jnp.float32), rows)
    n = rot_dim // 4
    inv = ROPE_THETA ** (-jnp.arange(n, dtype=jnp.float32) / n)
    ang = jnp.concatenate([row[:, None] * inv, col[:, None] * inv], axis=-1)
    return jnp.cos(ang)[:, None, :], jnp.sin(ang)[:, None, :]


def apply_rope(x, cos, sin):
    half = x.shape[-1] // 2
    xf = x.astype(jnp.float32)
    x1, x2 = xf[..., :half], xf[..., half:]
    return jnp.concatenate([x1 * cos - x2 * sin, x1 * sin + x2 * cos], axis=-1).astype(x.dtype)


def attention(q, k, v, scale):
    b, hk, g, lq, dk = q.shape
    nb = lq // Q_BLOCK
    qb = jnp.moveaxis(q.reshape(b, hk, g, nb, Q_BLOCK, dk), 3, 0)

    def one_block(qi):
        s = jnp.einsum('bhgqd,bhkd->bhgqk', qi, k, preferred_element_type=jnp.float32) * scale
        p = jax.nn.softmax(s, axis=-1)
        return jnp.einsum('bhgqk,bhkd->bhgqd', p.astype(v.dtype), v)

    ob = lax.map(one_block, qb)
    return jnp.moveaxis(ob, 0, 3).reshape(b, hk, g, lq, v.shape[-1])


def merge_heads(o):
    b, hk, g, l, d = o.shape
    return o.transpose(0, 3, 1, 2, 4).reshape(b, l, hk * g * d)


def mixer_inputs(h, w_in, g_q_gqa, g_k_gqa, g_cq, g_ckv, w_uq, w_ukv, g_q_mla, g_k_mla, rope):
    b, l, _ = h.shape
    u = h @ w_in
    offs = np.cumsum(IN_SIZES)[:-1].tolist()
    u_q, u_k, u_v, u_cq, u_ckv, u_kr, pool_in = jnp.split(u, offs, axis=-1)
    qg = rms_norm(u_q.reshape(b, l, GQA_HEADS, HEAD_DIM), g_q_gqa)
    kg = rms_norm(u_k.reshape(b, l, GQA_KV_HEADS, HEAD_DIM), g_k_gqa)
    vg = u_v.reshape(b, l, GQA_KV_HEADS, HEAD_DIM)
    qm = (rms_norm(u_cq, g_cq) @ w_uq).reshape(b, l, MLA_HEADS, MLA_QK)
    kv = (rms_norm(u_ckv, g_ckv) @ w_ukv).reshape(b, l, MLA_HEADS, MLA_NOPE + MLA_V)
    k_rope = jnp.broadcast_to(u_kr[:, :, None, :], (b, l, MLA_HEADS, MLA_ROPE))
    km = jnp.concatenate([kv[..., :MLA_NOPE], k_rope], axis=-1)
    vm = kv[..., MLA_NOPE:]
    qm = rms_norm(qm, g_q_mla)
    km = rms_norm(km, g_k_mla)
    if rope is not None:
        cos_g, sin_g, cos_m, sin_m = rope
        qg = apply_rope(qg, cos_g, sin_g)
        kg = apply_rope(kg, cos_g, sin_g)
        qm = jnp.concatenate([qm[..., :MLA_NOPE], apply_rope(qm[..., MLA_NOPE:], cos_m, sin_m)], axis=-1)
        km = jnp.concatenate([km[..., :MLA_NOPE], apply_rope(km[..., MLA_NOPE:], cos_m, sin_m)], axis=-1)
    qg = qg.reshape(b, l, GQA_KV_HEADS, GQA_GROUP, HEAD_DIM).transpose(0, 2, 3, 1, 4)
    kg = kg.transpose(0, 2, 1, 3)
    vg = vg.transpose(0, 2, 1, 3)
    qm = qm.transpose(0, 2, 1, 3)[:, :, None]
    km = km.transpose(0, 2, 1, 3)
    vm = vm.transpose(0, 2, 1, 3)
    return qg, kg, vg, qm, km, vm, pool_in


def multiscale_pool(u, w_pool, ls_pool):
    b, l, _ = u.shape
    uf = u.astype(jnp.float32)
    cs = jnp.concatenate([jnp.zeros((b, 1, POOL_WIDTH), jnp.float32), jnp.cumsum(uf, axis=1)], axis=1)
    t = jnp.arange(l)
    outs = []
    for gi, w in enumerate(POOL_WINDOWS):
        lo = jnp.maximum(t - w // 2, 0)
        hi = jnp.minimum(t + w // 2, l)
        csg = cs[..., gi * POOL_GROUP_DIM:(gi + 1) * POOL_GROUP_DIM]
        s = jnp.take(csg, hi, axis=1) - jnp.take(csg, lo, axis=1)
        mean = s / (hi - lo).astype(jnp.float32)[None, :, None]
        outs.append(mean - uf[..., gi * POOL_GROUP_DIM:(gi + 1) * POOL_GROUP_DIM])
    y = jnp.stack(outs, axis=2).astype(u.dtype)
    y = jnp.einsum('blgc,gcd->blgd', y, w_pool).reshape(b, l, POOL_WIDTH)
    return y * ls_pool


def token_mix(qg, kg, vg, qm, km, vm, pool_in, w_pool, ls_pool, w_out):
    og = attention(qg, kg, vg, HEAD_DIM ** -0.5)
    om = attention(qm, km, vm, MLA_QK ** -0.5)
    op = multiscale_pool(pool_in, w_pool, ls_pool)
    return jnp.concatenate([merge_heads(og), merge_heads(om), op], axis=-1) @ w_out


def sq_relu_mlp(h, w1, w2):
    return jnp.square(jax.nn.relu(h @ w1)) @ w2


def setup_inputs(seed: int = 0) -> dict:
    key = jax.random.key(seed)
    ks = jax.random.split(key, 24)
    f32 = jnp.float32

    def dense(k, shape, fan_in, s=1.0):
        return jax.random.normal(k, shape, f32) * (s * fan_in ** -0.5)

    def gain(k, shape):
        return 1.0 + 0.1 * jax.random.normal(k, shape, f32)

    return {
        "x": jax.random.normal(ks[0], (BATCH, SEQ, D_MODEL), f32),
        "c": jax.random.normal(ks[1], (BATCH, D_MODEL), f32),
        "ctx": jax.random.normal(ks[2], (BATCH, CTX_LEN, D_MODEL), f32),
        "c_ctx": jax.random.normal(ks[3], (D_MODEL,), f32),
        "w_mod": dense(ks[4], (DEPTH, D_MODEL, N_MOD * D_MODEL), D_MODEL, 0.5),
        "b_mod": 0.02 * jax.random.normal(ks[5], (DEPTH, N_MOD * D_MODEL), f32),
        "g_norm1": gain(ks[6], (DEPTH, D_MODEL)),
        "g_norm2": gain(ks[7], (DEPTH, D_MODEL)),
        "w_in": dense(ks[8], (DEPTH, D_MODEL, IN_WIDTH), D_MODEL),
        "g_q_gqa": gain(ks[9], (DEPTH, HEAD_DIM)),
        "g_k_gqa": gain(ks[10], (DEPTH, HEAD_DIM)),
        "g_cq": gain(ks[11], (DEPTH, MLA_Q_RANK)),
        "g_ckv": gain(ks[12], (DEPTH, MLA_KV_RANK)),
        "w_uq": dense(ks[13], (DEPTH, MLA_Q_RANK, MLA_HEADS * MLA_QK), MLA_Q_RANK),
        "w_ukv": dense(ks[14], (DEPTH, MLA_KV_RANK, MLA_HEADS * (MLA_NOPE + MLA_V)), MLA_KV_RANK),
        "g_q_mla": gain(ks[15], (DEPTH, MLA_QK)),
        "g_k_mla": gain(ks[16], (DEPTH, MLA_QK)),
        "w_pool": dense(ks[17], (DEPTH, POOL_GROUPS, POOL_GROUP_DIM, POOL_GROUP_DIM), POOL_GROUP_DIM),
        "ls_pool": gain(ks[18], (DEPTH, POOL_WIDTH)),
        "w_out": dense(ks[19], (DEPTH, MIX_WIDTH, D_MODEL), MIX_WIDTH),
        "w_mlp1": dense(ks[20], (DEPTH, D_MODEL, D_FF), D_MODEL),
        "w_mlp2": dense(ks[21], (DEPTH, D_FF, D_MODEL), D_FF),
    }


def reference(x, c, ctx, c_ctx, w_mod, b_mod, g_norm1, g_norm2, w_in, g_q_gqa, g_k_gqa, g_cq, g_ckv,
              w_uq, w_ukv, g_q_mla, g_k_mla, w_pool, ls_pool, w_out, w_mlp1, w_mlp2):
    length = x.shape[1]
    cos_g, sin_g = axial_rope_tables(length, HEAD_DIM)
    cos_m, sin_m = axial_rope_tables(length, MLA_ROPE)
    rope = (cos_g.astype(x.dtype), sin_g.astype(x.dtype), cos_m.astype(x.dtype), sin_m.astype(x.dtype))
    x_lat, x_ctx = x, ctx
    for i in range(DEPTH):
        last = i == DEPTH - 1
        mod_lat = (jax.nn.silu(c) @ w_mod[i] + b_mod[i])[:, None, :]
        mod_ctx = (jax.nn.silu(c_ctx) @ w_mod[i] + b_mod[i])[None, None, :]
        sh1, sc1, gt1, sh2, sc2, gt2 = jnp.split(mod_lat, N_MOD, axis=-1)
        csh1, csc1, cgt1, csh2, csc2, cgt2 = jnp.split(mod_ctx, N_MOD, axis=-1)
        layer_w = (w_in[i], g_q_gqa[i], g_k_gqa[i], g_cq[i], g_ckv[i], w_uq[i], w_ukv[i], g_q_mla[i], g_k_mla[i])
        hc = modulate(rms_norm(x_ctx, g_norm1[i]), csh1, csc1)
        qg_c, kg_c, vg_c, qm_c, km_c, vm_c, pool_c = mixer_inputs(hc, *layer_w, None)
        h = modulate(rms_norm(x_lat, g_norm1[i]), sh1, sc1)
        qg, kg, vg, qm, km, vm, pool_l = mixer_inputs(h, *layer_w, rope)
        mix = token_mix(qg, jnp.concatenate([kg_c, kg], axis=2), jnp.concatenate([vg_c, vg], axis=2),
                        qm, jnp.concatenate([km_c, km], axis=2), jnp.concatenate([vm_c, vm], axis=2),
                        pool_l, w_pool[i], ls_pool[i], w_out[i])
        x_lat = x_lat + gt1 * mix
        x_lat = x_lat + gt2 * sq_relu_mlp(modulate(rms_norm(x_lat, g_norm2[i]), sh2, sc2), w_mlp1[i], w_mlp2[i])
        if not last:
            mix_c = token_mix(qg_c, kg_c, vg_c, qm_c, km_c, vm_c, pool_c, w_pool[i], ls_pool[i], w_out[i])
            x_ctx = x_ctx + cgt1 * mix_c
            x_ctx = x_ctx + cgt2 * sq_relu_mlp(modulate(rms_norm(x_ctx, g_norm2[i]), csh2, csc2),
                                               w_mlp1[i], w_mlp2[i])
    return x_lat
```

```python
import numpy as np
from contextlib import ExitStack
import concourse.bass as bass
import concourse.mybir as mybir
from concourse.bass_utils import run_bass_kernel_spmd

F32 = mybir.dt.float32
BF16 = mybir.dt.bfloat16
AF = mybir.ActivationFunctionType
ALU = mybir.AluOpType
AX = mybir.AxisListType

CFG = dict(L=8192, C=256, DEPTH=4, NCORES=8, DEBUG=False)
D = 1024
KD = 8
DFF = 4096
IN_W = 1312
EPS = 1e-6
POOL_W = (2, 4, 8, 16)


class SemC:
    def __init__(self, h):
        self.h = h
        self.n = 0


class Dep:
    __slots__ = ("name", "writers", "readers", "dsem")

    def __init__(self, name=""):
        self.name = name
        self.writers = []
        self.readers = []
        self.dsem = None


class DDep:
    def __init__(self, name=""):
        self.name = name
        self.pending = {}


class Eng:
    def __init__(self, name, h):
        self.name = name
        self.h = h
        self.sem = None
        self.waited = {}


class KB:
    def __init__(self, nc, es):
        self.nc = nc
        self.es = es
        self.pe = Eng("pe", nc.tensor)
        self.act = Eng("act", nc.scalar)
        self.dve = Eng("dve", nc.vector)
        self.pool = Eng("pool", nc.gpsimd)
        self.sp = Eng("sp", nc.sync)
        self.engines = [self.pe, self.act, self.dve, self.pool, self.sp]
        self.all_sems = []
        self.dsem_free = []
        self.dsem_all = []
        self.phase_deps = []
        for i in range(44):
            s = SemC(es.enter_context(nc.semaphore(f"d{i}")))
            self.dsem_free.append(s)
            self.dsem_all.append(s)
        self.epoch_id = 0
        self.new_epoch()

    def new_epoch(self):
        for E in self.engines:
            if E is self.sp:
                continue
            E.sem = SemC(self.es.enter_context(self.nc.semaphore(f"e{self.epoch_id}_{E.name}")))
            self.all_sems.append(E.sem)
        self.epoch_id += 1

    def dep(self, name=""):
        d = Dep(name)
        self.phase_deps.append(d)
        return d

    def _wait(self, E, sem, cnt):
        if E.waited.get(sem, 0) >= cnt:
            return
        E.h.wait_ge(sem.h, cnt)
        E.waited[sem] = cnt

    def op(self, E, fn, r=(), w=(), sig=True):
        need = {}

        def add(s, c):
            if E.waited.get(s, 0) >= c:
                return
            if need.get(s, 0) < c:
                need[s] = c

        for d in r:
            for (we, s, c) in d.writers:
                add(s, c)
        for d in w:
            for (we, s, c) in d.writers:
                if we is not E:
                    add(s, c)
            for (re, s, c) in d.readers:
                if re is not E:
                    add(s, c)
        items = list(need.items())
        for (s, c) in items[:-1]:
            E.h.wait_ge(s.h, c)
            E.waited[s] = c
        ins = fn()
        if items:
            s, c = items[-1]
            ins._wait_ge(s.h, c)
            E.waited[s] = c
        if sig:
            E.sem.n += 1
            ins.then_inc(E.sem.h, 1)
            rec = (E, E.sem, E.sem.n)
        else:
            rec = (E, E.sem, E.sem.n + 1)
        for d in r:
            d.readers = [x for x in d.readers if x[0] is not E] + [rec]
        for d in w:
            d.writers = [rec]
            d.readers = []
        return ins

    def dma(self, Q, out, in_, r=(), w=(), dr=(), dw=()):
        d0 = (list(w) + list(r))[0]
        if d0.dsem is None:
            d0.dsem = self.dsem_free.pop(0)
        ds = d0.dsem
        for d in r:
            for (we, s, c) in d.writers:
                self._wait(Q, s, c)
        for d in w:
            for (we, s, c) in d.writers:
                if s is not ds:
                    self._wait(Q, s, c)
            for (re, s, c) in d.readers:
                self._wait(Q, s, c)
        for dd in dr:
            for s, c in dd.pending.items():
                self._wait(Q, s, c)
        ins = Q.h.dma_start(out=out, in_=in_)
        ds.n += 16
        ins.then_inc(ds.h, 16)
        rec = (None, ds, ds.n)
        for d in r:
            d.readers = [x for x in d.readers if x[1] is not ds] + [rec]
        for d in w:
            d.writers = [rec]
            d.readers = []
        for dd in dw:
            dd.pending[ds] = ds.n
        return ins

    def barrier(self):
        for E in self.engines:
            for F in self.engines:
                if F is E or F.sem is None:
                    continue
                if F.sem.n > 0:
                    self._wait(E, F.sem, F.sem.n)
            for s in self.dsem_all:
                if s.n > 0:
                    self._wait(E, s, s.n)

    def end_phase(self):
        self.barrier()
        for d in self.phase_deps:
            if d.dsem is not None:
                self.dsem_free.append(d.dsem)
                d.dsem = None
        self.phase_deps = []


class T:
    _uid = [0]

    def __init__(self, kb, es, name, shape, dt, psum=False):
        T._uid[0] += 1
        name = f"{name}_{T._uid[0]}"
        if psum:
            self.t = es.enter_context(kb.nc.psum_tensor(name, shape, dt))
        else:
            self.t = es.enter_context(kb.nc.sbuf_tensor(name, shape, dt))
        self.d = kb.dep(name)

    def __getitem__(self, k):
        return self.t[k]


def _rope_tables(L, C):
    rows = L // 64
    row = np.repeat(np.arange(rows, dtype=np.float32), 64)
    col = np.tile(np.arange(64, dtype=np.float32), rows)

    def tab(rot):
        n = rot // 4
        inv = (10000.0 ** (-np.arange(n, dtype=np.float32) / n)).astype(np.float32)
        ang = np.concatenate([row[:, None] * inv, col[:, None] * inv], axis=-1).astype(np.float32)
        cos = np.concatenate([np.ones((C, rot // 2), np.float32), np.cos(ang).astype(np.float32)], 0)
        sin = np.concatenate([np.zeros((C, rot // 2), np.float32), np.sin(ang).astype(np.float32)], 0)
        return cos, sin

    cg, sg = tab(64)
    cm, sm = tab(32)
    T_ = L + C
    nt = T_ // 128

    def lay(a):
        return np.ascontiguousarray(a.reshape(nt, 128, -1).transpose(1, 0, 2))

    return lay(cg), lay(sg), lay(cm), lay(sm)


def _pool_bands():
    B = np.zeros((128, 5, 4, 128), np.float32)
    for g, w in enumerate(POOL_W):
        h = w // 2
        for t in range(128):
            for tp in range(t - h, t + h):
                if 0 <= tp < 128:
                    B[tp, 1, g, t] += 1.0 / w
                elif tp < 0:
                    B[tp + 128, 0, g, t] += 1.0 / w
                else:
                    B[tp - 128, 2, g, t] += 1.0 / w
            B[t, 1, g, t] -= 1.0
            lo = max(t - h, 0)
            cnt = (t + h) - lo
            for tp in range(lo, min(t + h, 128)):
                B[tp, 3, g, t] += 1.0 / cnt
            B[t, 3, g, t] -= 1.0
            hi = min(t + h, 128)
            cnt = hi - (t - h)
            for tp in range(max(t - h, 0), hi):
                B[tp, 4, g, t] += 1.0 / cnt
            B[t, 4, g, t] -= 1.0
    return B


def build(cfg):
    L, C, DEPTH, DEBUG = cfg["L"], cfg["C"], cfg["DEPTH"], cfg["DEBUG"]
    TT = L + C
    NT = TT // 128
    NCT = C // 128
    nc = bass.Bass("TRN2", target_bir_lowering=False)

    def din(name, shape, dt=F32):
        return nc.dram_tensor(name, shape, dt, kind="ExternalInput").ap()

    def dscr(name, shape, dt):
        kind = "ExternalOutput" if DEBUG else "Internal"
        return nc.dram_tensor(name, shape, dt, kind=kind).ap()

    x_in = din("x", [L, D])
    ctx_in = din("ctx", [C, D])
    cc_in = din("cc", [128, KD, 2])
    w_mod = din("w_mod", [DEPTH, D, 6 * D])
    b_mod = din("b_mod", [DEPTH, 6 * D])
    g_norm1 = din("g_norm1", [DEPTH, D])
    g_norm2 = din("g_norm2", [DEPTH, D])
    w_in = din("w_in", [DEPTH, D, IN_W])
    g_q_gqa = din("g_q_gqa", [DEPTH, 64])
    g_k_gqa = din("g_k_gqa", [DEPTH, 64])
    g_cq = din("g_cq", [DEPTH, 256])
    g_ckv = din("g_ckv", [DEPTH, 128])
    w_uq = din("w_uq", [DEPTH, 256, 576])
    w_ukv = din("w_ukv", [DEPTH, 128, 768])
    g_q_mla = din("g_q_mla", [DEPTH, 96])
    g_k_mla = din("g_k_mla", [DEPTH, 96])
    w_pool = din("w_pool", [DEPTH, 4, 64, 64])
    ls_pool = din("ls_pool", [DEPTH, 256])
    w_out = din("w_out", [DEPTH, D, D])
    w_mlp1 = din("w_mlp1", [DEPTH, D, DFF])
    w_mlp2 = din("w_mlp2", [DEPTH, DFF, D])
    ident_in = din("ident", [128, 128])
    cosg_in = din("cosg", [128, NT, 32])
    sing_in = din("sing", [128, NT, 32])
    cosm_in = din("cosm", [128, NT, 16])
    sinm_in = din("sinm", [128, NT, 16])
    bands_in = din("bands", [128, 5, 4, 128])
    out = nc.dram_tensor("out", [L, D], F32, kind="ExternalOutput").ap()

    xs = dscr("xs", [TT, D], F32)
    modv = dscr("modv", [DEPTH, 2, 6 * D], F32)
    QTg = dscr("QTg", [6, 64, TT], BF16)
    KTg = dscr("KTg", [2, 64, TT], BF16)
    QTm = dscr("QTm", [6, 96, TT], BF16)
    KTm = dscr("KTm", [6, 96, TT], BF16)
    Vs = dscr("Vs", [TT, 512], BF16)
    pin = dscr("pin", [TT, 256], F32)
    mixT = dscr("mixT", [D, TT], BF16)
    h2T = dscr("h2T", [D, TT], BF16)

    d_xs, d_modv, d_q, d_v, d_pin, d_mix, d_h2, d_out = [DDep(n) for n in
                                                       ("xs", "modv", "q", "v", "pin", "mix", "h2", "out")]

    ges = ExitStack()
    with ges:
        kb = KB(nc, ges)
        pe, act, dve, pool, sp = kb.pe, kb.act, kb.dve, kb.pool, kb.sp

        ident = T(kb, ges, "ident", [128, 128], BF16)
        kb.dma(pool, ident[:], ident_in[:, :], w=[ident.d])
        epsc = T(kb, ges, "epsc", [128, 1], F32)
        kb.op(dve, lambda: nc.vector.memset(epsc[:], EPS), w=[epsc.d])

        def rstd_chain(ssq, rstd, n, inv_dim):
            kb.op(act, lambda: nc.scalar.activation(out=rstd[:, 0:n], in_=ssq[:, 0:n], func=AF.Ln,
                                                    bias=epsc[:, 0:1], scale=inv_dim),
                  r=[ssq.d, epsc.d], w=[rstd.d])
            kb.op(act, lambda: nc.scalar.activation(out=rstd[:, 0:n], in_=rstd[:, 0:n], func=AF.Exp, scale=-0.5),
                  r=[rstd.d], w=[rstd.d])

        def x_src(l, t0, n):
            if l == 0:
                if t0 < NCT:
                    a = ctx_in[t0 * 128:(t0 + n) * 128, :]
                else:
                    a = x_in[(t0 - NCT) * 128:(t0 - NCT + n) * 128, :]
            else:
                a = xs[t0 * 128:(t0 + n) * 128, :]
            return a.rearrange("(i p) c -> p i c", p=128)

        def bc_row(ap1d):
            return ap1d.partition_broadcast(128)

        with ExitStack() as es:
            cs_raw = T(kb, es, "cs_raw", [128, KD, 2], F32)
            cs = T(kb, es, "cs", [128, KD, 2], F32)
            kb.dma(sp, cs_raw[:], cc_in[:, :, :], w=[cs_raw.d])
            kb.op(act, lambda: nc.scalar.activation(out=cs[:], in_=cs_raw[:], func=AF.Silu), r=[cs_raw.d], w=[cs.d])
            wm = [T(kb, es, f"wm{i}", [128, KD, 512], F32) for i in range(3)]
            pm = [T(kb, es, f"pm{i}", [128, 512], F32, psum=True) for i in range(2)]
            mo = [T(kb, es, f"mo{i}", [2, 512], F32) for i in range(2)]
            bm = [T(kb, es, f"bm{i}", [2, 6 * D], F32) for i in range(2)]
            it = 0
            for l in range(DEPTH):
                bml = bm[l % 2]
                for j in range(2):
                    kb.dma(sp, bml[j:j + 1, :], b_mod[l:l + 1, :], w=[bml.d])
                wv = w_mod[l].rearrange("(k p) n -> p k n", p=128)
                for n in range(12):
                    wt = wm[it % 3]
                    kb.dma(sp, wt[:], wv[:, :, n * 512:(n + 1) * 512], w=[wt.d])
                    pt = pm[it % 2]
                    for k in range(KD):
                        kb.op(pe, lambda k=k: nc.tensor.matmul(pt[0:2, :], lhsT=cs[:, k, :], rhs=wt[:, k, :],
                                                               start=(k == 0), stop=(k == KD - 1)),
                              r=[cs.d, wt.d], w=[pt.d], sig=(k == KD - 1))
                    mt = mo[it % 2]
                    kb.op(dve, lambda: nc.vector.tensor_tensor(out=mt[:], in0=pt[0:2, :],
                                                               in1=bml[:, n * 512:(n + 1) * 512], op=ALU.add),
                          r=[pt.d, bml.d], w=[mt.d])
                    kb.dma(sp, modv[l, :, n * 512:(n + 1) * 512], mt[:], r=[mt.d], dw=[d_modv])
                    it += 1
            kb.end_phase()

        def load_mod_tile(tile, l, j, idx, tmp=None):
            kb.dma(sp, tile[:], bc_row(modv[l, j, idx * D:(idx + 1) * D]), w=[tile.d], dr=[d_modv])

        def make_scale_tile(tile, gt, l, j, idx):
            load_mod_tile(tile, l, j, idx)
            kb.op(dve, lambda: nc.vector.scalar_tensor_tensor(out=tile[:], in0=tile[:], scalar=1.0, in1=gt[:],
                                                             op0=ALU.add, op1=ALU.mult),
                  r=[tile.d, gt.d], w=[tile.d])

        for l in range(DEPTH):
            last = (l == DEPTH - 1)
            if l > 0:
                kb.new_epoch()
            with ExitStack() as es:
                NB = 2
                win = T(kb, es, "win", [128, KD, IN_W], BF16)
                wv = w_in[l].rearrange("(k p) n -> p k n", p=128)
                for k in range(KD):
                    kb.dma(pool, win[:, k, :], wv[:, k, :], w=[win.d])
                wuq_f = T(kb, es, "wuq_f", [128, 2, 576], F32)
                wuq = T(kb, es, "wuq", [128, 2, 576], BF16)
                wukv_f = T(kb, es, "wukv_f", [128, 768], F32)
                wukv = T(kb, es, "wukv", [128, 768], BF16)
                gcq = T(kb, es, "gcq", [128, 2], F32)
                gckv = T(kb, es, "gckv", [128, 1], F32)
                kb.dma(sp, wuq_f[:], w_uq[l].rearrange("(k p) n -> p k n", p=128), w=[wuq_f.d])
                kb.dma(sp, wukv_f[:], w_ukv[l], w=[wukv_f.d])
                for k in range(2):
                    kb.dma(sp, gcq[:, k:k + 1], g_cq[l, k * 128:(k + 1) * 128].rearrange("(p o) -> p o", o=1),
                           w=[gcq.d])
                kb.dma(sp, gckv[:, 0:1], g_ckv[l].rearrange("(p o) -> p o", o=1), w=[gckv.d])
                for k in range(2):
                    kb.op(dve, lambda k=k: nc.vector.tensor_scalar(out=wuq[:, k, :], in0=wuq_f[:, k, :],
                                                                   scalar1=gcq[:, k:k + 1], scalar2=None,
                                                                   op0=ALU.mult),
                          r=[wuq_f.d, gcq.d], w=[wuq.d])
                kb.op(dve, lambda: nc.vector.tensor_scalar(out=wukv[:], in0=wukv_f[:], scalar1=gckv[:, 0:1],
                                                           scalar2=None, op0=ALU.mult),
                      r=[wukv_f.d, gckv.d], w=[wukv.d])
                GQK = T(kb, es, "GQK", [128, 8, 64], F32)
                for h in range(8):
                    src = g_q_gqa[l] if h < 6 else g_k_gqa[l]
                    kb.dma(sp, GQK[:, h, :], bc_row(src), w=[GQK.d])
                GQM = T(kb, es, "GQM", [128, 6, 96], F32)
                GKM = T(kb, es, "GKM", [128, 6, 96], F32)
                for h in range(6):
                    kb.dma(sp, GQM[:, h, :], bc_row(g_q_mla[l]), w=[GQM.d])
                    kb.dma(sp, GKM[:, h, :], bc_row(g_k_mla[l]), w=[GKM.d])
                cosg = T(kb, es, "cosg", [128, NT, 32], F32)
                sing = T(kb, es, "sing", [128, NT, 32], F32)
                cosm = T(kb, es, "cosm", [128, NT, 16], F32)
                sinm = T(kb, es, "sinm", [128, NT, 16], F32)
                kb.dma(sp, cosg[:], cosg_in[:, :, :], w=[cosg.d])
                kb.dma(sp, sing[:], sing_in[:, :, :], w=[sing.d])
                kb.dma(sp, cosm[:], cosm_in[:, :, :], w=[cosm.d])
                kb.dma(sp, sinm[:], sinm_in[:, :, :], w=[sinm.d])
                g1t = T(kb, es, "g1t", [128, D], F32)
                kb.dma(sp, g1t[:], bc_row(g_norm1[l]), w=[g1t.d])
                A1 = T(kb, es, "A1", [128, D], F32)
                B1 = T(kb, es, "B1", [128, D], F32)

                xblk = [T(kb, es, f"xblk{i}", [128, NB, D], F32) for i in range(2)]
                junk = T(kb, es, "junkA", [128, D], BF16)
                ssq1 = T(kb, es, "ssq1", [128, NB], F32)
                rstd1 = T(kb, es, "rstd1", [128, NB], F32)
                t1 = T(kb, es, "t1", [128, NB, D], F32)
                hb = T(kb, es, "hb", [128, NB, D], BF16)
                hT = T(kb, es, "hT", [128, NB, KD, 128], BF16)
                u_sb = T(kb, es, "u_sb", [128, NB, IN_W], F32)
                sq = T(kb, es, "sq", [128, NB, 576], F32)
                ssq8 = T(kb, es, "ssq8", [128, NB * 8], F32)
                rstd8 = T(kb, es, "rstd8", [128, NB * 8], F32)
                tq = T(kb, es, "tq", [128, NB, 8, 64], F32)
                tq2 = T(kb, es, "tq2", [128, NB, 8, 64], F32)
                tq3 = T(kb, es, "tq3", [128, NB, 8, 64], F32)
                qk_bf = T(kb, es, "qk_bf", [128, NB, 8, 64], BF16)
                ssqc = T(kb, es, "ssqc", [128, NB * 3], F32)
                ssqcq = T(kb, es, "ssqcq", [128, NB], F32)
                ssqckv = T(kb, es, "ssqckv", [128, NB], F32)
                rstdcq = T(kb, es, "rstdcq", [128, NB], F32)
                rstdckv = T(kb, es, "rstdckv", [128, NB], F32)
                cb = T(kb, es, "cb", [128, NB, 384], BF16)
                cT = T(kb, es, "cT", [128, NB, 3, 128], BF16)
                qm_sb = T(kb, es, "qm_sb", [128, NB, 6, 96], F32)
                kv_sb = T(kb, es, "kv_sb", [128, NB, 6, 128], F32)
                kcat = T(kb, es, "kcat", [128, NB, 6, 96], F32)
                ssq6 = T(kb, es, "ssq6", [128, NB * 6], F32)
                rstd6 = T(kb, es, "rstd6", [128, NB * 6], F32)
                qg = T(kb, es, "qg", [128, NB, 6, 96], F32)
                rt = [T(kb, es, f"rt{i}", [128, NB, 6, 16], F32) for i in range(4)]
                qm_bf = T(kb, es, "qm_bf", [128, NB, 6, 96], BF16)
                km_bf = T(kb, es, "km_bf", [128, NB, 6, 96], BF16)
                vblk = T(kb, es, "vblk", [128, NB, 512], BF16)
                qkT_blk = T(kb, es, "qkT_blk", [64, 8, NB * 128], BF16)
                qmT_blk = T(kb, es, "qmT_blk", [96, 6, NB * 128], BF16)
                kmT_blk = T(kb, es, "kmT_blk", [96, 6, NB * 128], BF16)
                pT = [T(kb, es, f"pT{i}", [128, 1024], BF16, psum=True) for i in range(2)]
                pu = T(kb, es, "pu", [128, 3, 512], F32, psum=True)
                pkv = T(kb, es, "pkv", [128, 2, 512], F32, psum=True)
                pti = [0]

                def next_pT():
                    pti[0] += 1
                    return pT[pti[0] % 2]

                nblk = NT // NB
                cur_stream = [None]

                def load_x(b):
                    xb_ = xblk[b % 2]
                    kb.dma(sp, xb_[:], x_src(l, b * NB, NB), w=[xb_.d], dr=[d_xs])

                load_x(0)
                for b in range(nblk):
                    t0 = b * NB
                    is_ctx = t0 < NCT
                    j = 1 if is_ctx else 0
                    if cur_stream[0] != j:
                        make_scale_tile(A1, g1t, l, j, 1)
                        load_mod_tile(B1, l, j, 0)
                        cur_stream[0] = j
                    if b + 1 < nblk:
                        load_x(b + 1)
                    xb_ = xblk[b % 2]
                    for i in range(NB):
                        kb.op(act, lambda i=i: nc.scalar.activation(out=junk[:], in_=xb_[:, i, :], func=AF.Square,
                                                                    accum_out=ssq1[:, i:i + 1]),
                              r=[xb_.d], w=[junk.d, ssq1.d])
                    rstd_chain(ssq1, rstd1, NB, 1.0 / D)
                    for i in range(NB):
                        kb.op(dve, lambda i=i: nc.vector.scalar_tensor_tensor(
                            out=t1[:, i, :], in0=xb_[:, i, :], scalar=rstd1[:, i:i + 1], in1=A1[:],
                            op0=ALU.mult, op1=ALU.mult), r=[xb_.d, rstd1.d, A1.d], w=[t1.d])
                    for i in range(NB):
                        kb.op(pool, lambda i=i: nc.gpsimd.tensor_tensor(out=hb[:, i, :], in0=t1[:, i, :], in1=B1[:],
                                                                        op=ALU.add), r=[t1.d, B1.d], w=[hb.d])
                    for i in range(NB):
                        p_ = next_pT()
                        for k in range(KD):
                            kb.op(pe, lambda i=i, k=k: nc.tensor.transpose(p_[:, k * 128:(k + 1) * 128],
                                                                           hb[:, i, k * 128:(k + 1) * 128], ident[:]),
                                  r=[hb.d, ident.d], w=[p_.d], sig=(k == KD - 1))
                        kb.op(act, lambda i=i: nc.scalar.copy(out=hT[:, i, :, :].rearrange("p k t -> p (k t)"),
                                                              in_=p_[:, :]), r=[p_.d], w=[hT.d])
                    for i in range(NB):
                        for c, (c0, cw) in enumerate(((0, 512), (512, 512), (1024, 288))):
                            for k in range(KD):
                                kb.op(pe, lambda i=i, c=c, c0=c0, cw=cw, k=k: nc.tensor.matmul(
                                    pu[:, c, 0:cw], lhsT=hT[:, i, k, :], rhs=win[:, k, c0:c0 + cw],
                                    start=(k == 0), stop=(k == KD - 1)),
                                    r=[hT.d, win.d], w=[pu.d], sig=(k == KD - 1))
                        kb.op(act, lambda i=i: nc.scalar.copy(out=u_sb[:, i, 0:1024].rearrange("p (c n) -> p c n", c=2),
                                                              in_=pu[:, 0:2, :]), r=[pu.d], w=[u_sb.d])
                        kb.op(dve, lambda i=i: nc.vector.tensor_copy(out=u_sb[:, i, 1024:IN_W], in_=pu[:, 2, 0:288]),
                              r=[pu.d], w=[u_sb.d])
                    uqk = u_sb[:, :, 0:512]
                    kb.op(dve, lambda: nc.vector.tensor_tensor(out=sq[:, :, 0:512], in0=uqk, in1=uqk, op=ALU.mult),
                          r=[u_sb.d], w=[sq.d])
                    kb.op(dve, lambda: nc.vector.tensor_reduce(
                        out=ssq8[:, :].rearrange("p (n h) -> p n h", h=8),
                        in_=sq[:, :, 0:512].rearrange("p n (h d) -> p n h d", d=64),
                        axis=AX.X, op=ALU.add), r=[sq.d], w=[ssq8.d])
                    rstd_chain(ssq8, rstd8, NB * 8, 1.0 / 64)
                    kb.op(dve, lambda: nc.vector.tensor_tensor(
                        out=tq[:],
                        in0=uqk.rearrange("p n (h d) -> p n h d", d=64),
                        in1=rstd8[:, :].rearrange("p (n h) -> p n h", h=8).unsqueeze(3).to_broadcast([128, NB, 8, 64]),
                        op=ALU.mult),
                        r=[u_sb.d, rstd8.d], w=[tq.d])
                    for i in range(NB):
                        kb.op(pool, lambda i=i: nc.gpsimd.tensor_tensor(out=tq[:, i, :, :], in0=tq[:, i, :, :],
                                                                        in1=GQK[:], op=ALU.mult),
                              r=[tq.d, GQK.d], w=[tq.d])
                    for i in range(NB):
                        cg = cosg[:, t0 + i, :].unsqueeze(1).to_broadcast([128, 8, 32])
                        sg = sing[:, t0 + i, :].unsqueeze(1).to_broadcast([128, 8, 32])
                        lo = tq[:, i, :, 0:32]
                        hi = tq[:, i, :, 32:64]
                        kb.op(dve, lambda: nc.vector.tensor_tensor(out=tq2[:, i, :, 0:32], in0=lo, in1=cg, op=ALU.mult),
                              r=[tq.d, cosg.d], w=[tq2.d])
                        kb.op(dve, lambda: nc.vector.tensor_tensor(out=tq2[:, i, :, 32:64], in0=hi, in1=cg, op=ALU.mult),
                              r=[tq.d, cosg.d], w=[tq2.d])
                        kb.op(pool, lambda: nc.gpsimd.tensor_tensor(out=tq3[:, i, :, 0:32], in0=hi, in1=sg, op=ALU.mult),
                              r=[tq.d, sing.d], w=[tq3.d])
                        kb.op(pool, lambda: nc.gpsimd.tensor_tensor(out=tq3[:, i, :, 32:64], in0=lo, in1=sg, op=ALU.mult),
                              r=[tq.d, sing.d], w=[tq3.d])
                        kb.op(dve, lambda: nc.vector.tensor_tensor(out=qk_bf[:, i, :, 0:32], in0=tq2[:, i, :, 0:32],
                                                                   in1=tq3[:, i, :, 0:32], op=ALU.subtract),
                              r=[tq2.d, tq3.d], w=[qk_bf.d])
                        kb.op(dve, lambda: nc.vector.tensor_tensor(out=qk_bf[:, i, :, 32:64], in0=tq2[:, i, :, 32:64],
                                                                   in1=tq3[:, i, :, 32:64], op=ALU.add),
                              r=[tq2.d, tq3.d], w=[qk_bf.d])
                    kb.op(act, lambda: nc.scalar.copy(out=vblk[:, :, 0:128], in_=u_sb[:, :, 512:640]),
                          r=[u_sb.d], w=[vblk.d])
                    ucc = u_sb[:, :, 640:1024]
                    kb.op(dve, lambda: nc.vector.tensor_tensor(out=sq[:, :, 0:384], in0=ucc, in1=ucc, op=ALU.mult),
                          r=[u_sb.d], w=[sq.d])
                    kb.op(dve, lambda: nc.vector.tensor_reduce(
                        out=ssqc[:, :].rearrange("p (n c) -> p n c", c=3),
                        in_=sq[:, :, 0:384].rearrange("p n (c d) -> p n c d", d=128),
                        axis=AX.X, op=ALU.add), r=[sq.d], w=[ssqc.d])
                    sv = ssqc[:, :].rearrange("p (n c) -> p n c", c=3)
                    kb.op(dve, lambda: nc.vector.tensor_tensor(out=ssqcq[:, :], in0=sv[:, :, 0], in1=sv[:, :, 1],
                                                               op=ALU.add), r=[ssqc.d], w=[ssqcq.d])
                    kb.op(dve, lambda: nc.vector.tensor_copy(out=ssqckv[:, :], in_=sv[:, :, 2]),
                          r=[ssqc.d], w=[ssqckv.d])
                    rstd_chain(ssqcq, rstdcq, NB, 1.0 / 256)
                    rstd_chain(ssqckv, rstdckv, NB, 1.0 / 128)
                    kb.op(pool, lambda: nc.gpsimd.tensor_copy(out=cb[:], in_=ucc), r=[u_sb.d], w=[cb.d])
                    for i in range(NB):
                        p_ = next_pT()
                        for k in range(3):
                            kb.op(pe, lambda i=i, k=k: nc.tensor.transpose(p_[:, k * 128:(k + 1) * 128],
                                                                           cb[:, i, k * 128:(k + 1) * 128], ident[:]),
                                  r=[cb.d, ident.d], w=[p_.d], sig=(k == 2))
                        kb.op(act, lambda i=i: nc.scalar.copy(out=cT[:, i, :, :].rearrange("p k t -> p (k t)"),
                                                              in_=p_[:, 0:384]), r=[p_.d], w=[cT.d])
                    for i in range(NB):
                        for (c, c0, cw) in ((0, 0, 512), (1, 512, 64)):
                            for k in range(2):
                                kb.op(pe, lambda i=i, c=c, c0=c0, cw=cw, k=k: nc.tensor.matmul(
                                    pu[:, c, 0:cw], lhsT=cT[:, i, k, :], rhs=wuq[:, k, c0:c0 + cw],
                                    start=(k == 0), stop=(k == 1)), r=[cT.d, wuq.d], w=[pu.d], sig=(k == 1))
                        for (c, c0, cw) in ((0, 0, 512), (1, 512, 256)):
                            kb.op(pe, lambda i=i, c=c, c0=c0, cw=cw: nc.tensor.matmul(
                                pkv[:, c, 0:cw], lhsT=cT[:, i, 2, :], rhs=wukv[:, c0:c0 + cw],
                                start=True, stop=True), r=[cT.d, wukv.d], w=[pkv.d])
                        qmv = qm_sb[:, i, :, :].rearrange("p h d -> p (h d)")
                        kb.op(dve, lambda i=i: nc.vector.tensor_scalar(out=qmv[:, 0:512], in0=pu[:, 0, :],
                                                                       scalar1=rstdcq[:, i:i + 1], scalar2=None,
                                                                       op0=ALU.mult),
                              r=[pu.d, rstdcq.d], w=[qm_sb.d])
                        kb.op(dve, lambda i=i: nc.vector.tensor_scalar(out=qmv[:, 512:576], in0=pu[:, 1, 0:64],
                                                                       scalar1=rstdcq[:, i:i + 1], scalar2=None,
                                                                       op0=ALU.mult),
                              r=[pu.d, rstdcq.d], w=[qm_sb.d])
                        kvv = kv_sb[:, i, :, :].rearrange("p h d -> p (h d)")
                        kb.op(act, lambda i=i: nc.scalar.activation(out=kvv[:, 0:512], in_=pkv[:, 0, :], func=AF.Identity,
                                                                    scale=rstdckv[:, i:i + 1]),
                              r=[pkv.d, rstdckv.d], w=[kv_sb.d])
                        kb.op(act, lambda i=i: nc.scalar.activation(out=kvv[:, 512:768], in_=pkv[:, 1, 0:256],
                                                                    func=AF.Identity, scale=rstdckv[:, i:i + 1]),
                              r=[pkv.d, rstdckv.d], w=[kv_sb.d])
                    kb.op(pool, lambda: nc.gpsimd.tensor_copy(out=kcat[:, :, :, 0:64], in_=kv_sb[:, :, :, 0:64]),
                          r=[kv_sb.d], w=[kcat.d])
                    kb.op(pool, lambda: nc.gpsimd.tensor_copy(
                        out=kcat[:, :, :, 64:96],
                        in_=u_sb[:, :, 1024:1056].unsqueeze(2).to_broadcast([128, NB, 6, 32])),
                        r=[u_sb.d], w=[kcat.d])
                    kb.op(act, lambda: nc.scalar.copy(out=vblk[:, :, 128:512].rearrange("p n (h d) -> p n h d", d=64),
                                                      in_=kv_sb[:, :, :, 64:128]), r=[kv_sb.d], w=[vblk.d])

                    def mla_norm_rope(src, G, dst):
                        s3 = src[:].rearrange("p n h d -> p (n h) d")
                        kb.op(dve, lambda: nc.vector.tensor_tensor(
                            out=sq[:].rearrange("p n (h d) -> p (n h) d", d=96), in0=s3, in1=s3, op=ALU.mult),
                            r=[src.d], w=[sq.d])
                        kb.op(dve, lambda: nc.vector.tensor_reduce(
                            out=ssq6[:, :], in_=sq[:].rearrange("p n (h d) -> p (n h) d", d=96),
                            axis=AX.X, op=ALU.add), r=[sq.d], w=[ssq6.d])
                        rstd_chain(ssq6, rstd6, NB * 6, 1.0 / 96)
                        kb.op(dve, lambda: nc.vector.tensor_tensor(
                            out=qg[:].rearrange("p n h d -> p (n h) d"), in0=s3,
                            in1=rstd6[:, :].unsqueeze(2).to_broadcast([128, NB * 6, 96]), op=ALU.mult),
                            r=[src.d, rstd6.d], w=[qg.d])
                        for i in range(NB):
                            kb.op(pool, lambda i=i: nc.gpsimd.tensor_tensor(out=qg[:, i, :, :], in0=qg[:, i, :, :],
                                                                            in1=G[:], op=ALU.mult),
                                  r=[qg.d, G.d], w=[qg.d])
                        kb.op(act, lambda: nc.scalar.copy(out=dst[:, :, :, 0:64], in_=qg[:, :, :, 0:64]),
                              r=[qg.d], w=[dst.d])
                        for i in range(NB):
                            cm_ = cosm[:, t0 + i, :].unsqueeze(1).to_broadcast([128, 6, 16])
                            sm_ = sinm[:, t0 + i, :].unsqueeze(1).to_broadcast([128, 6, 16])
                            lo = qg[:, i, :, 64:80]
                            hi = qg[:, i, :, 80:96]
                            kb.op(dve, lambda: nc.vector.tensor_tensor(out=rt[0][:, i, :, :], in0=lo, in1=cm_, op=ALU.mult),
                                  r=[qg.d, cosm.d], w=[rt[0].d])
                            kb.op(dve, lambda: nc.vector.tensor_tensor(out=rt[1][:, i, :, :], in0=hi, in1=sm_, op=ALU.mult),
                                  r=[qg.d, sinm.d], w=[rt[1].d])
                            kb.op(pool, lambda: nc.gpsimd.tensor_tensor(out=rt[2][:, i, :, :], in0=lo, in1=sm_, op=ALU.mult),
                                  r=[qg.d, sinm.d], w=[rt[2].d])
                            kb.op(pool, lambda: nc.gpsimd.tensor_tensor(out=rt[3][:, i, :, :], in0=hi, in1=cm_, op=ALU.mult),
                                  r=[qg.d, cosm.d], w=[rt[3].d])
                        kb.op(dve, lambda: nc.vector.tensor_tensor(out=dst[:, :, :, 64:80], in0=rt[0][:], in1=rt[1][:],
                                                                   op=ALU.subtract), r=[rt[0].d, rt[1].d], w=[dst.d])
                        kb.op(dve, lambda: nc.vector.tensor_tensor(out=dst[:, :, :, 80:96], in0=rt[2][:], in1=rt[3][:],
                                                                   op=ALU.add), r=[rt[2].d, rt[3].d], w=[dst.d])

                    mla_norm_rope(qm_sb, GQM, qm_bf)
                    mla_norm_rope(kcat, GKM, km_bf)
                    tsl = slice(t0 * 128, (t0 + NB) * 128)
                    for i in range(NB):
                        p_ = next_pT()
                        for h in range(8):
                            kb.op(pe, lambda i=i, h=h: nc.tensor.transpose(p_[0:64, h * 128:(h + 1) * 128],
                                                                           qk_bf[:, i, h, :], ident[:]),
                                  r=[qk_bf.d, ident.d], w=[p_.d], sig=(h == 7))
                        kb.op(act, lambda i=i: nc.scalar.copy(out=qkT_blk[:, :, i * 128:(i + 1) * 128],
                                                              in_=p_[0:64, :].rearrange("p (h t) -> p h t", t=128)),
                              r=[p_.d], w=[qkT_blk.d])
                    kb.dma(sp, QTg[:, :, tsl].rearrange("h d t -> d h t"), qkT_blk[:, 0:6, :], r=[qkT_blk.d], dw=[d_q])
                    kb.dma(sp, KTg[:, :, tsl].rearrange("h d t -> d h t"), qkT_blk[:, 6:8, :], r=[qkT_blk.d], dw=[d_q])
                    for (srcb, dstb, dram) in ((qm_bf, qmT_blk, QTm), (km_bf, kmT_blk, KTm)):
                        for i in range(NB):
                            p_ = next_pT()
                            for h in range(6):
                                kb.op(pe, lambda i=i, h=h: nc.tensor.transpose(p_[0:96, h * 128:(h + 1) * 128],
                                                                               srcb[:, i, h, :], ident[:]),
                                      r=[srcb.d, ident.d], w=[p_.d], sig=(h == 5))
                            kb.op(dve, lambda i=i: nc.vector.tensor_copy(
                                out=dstb[:, :, i * 128:(i + 1) * 128],
                                in_=p_[0:96, 0:768].rearrange("p (h t) -> p h t", t=128)), r=[p_.d], w=[dstb.d])
                        kb.dma(sp, dram[:, :, tsl].rearrange("h d t -> d h t"), dstb[:], r=[dstb.d], dw=[d_q])
                    kb.dma(sp, Vs[tsl, :].rearrange("(i p) c -> p i c", p=128), vblk[:], r=[vblk.d], dw=[d_v])
                    kb.dma(sp, pin[tsl, :].rearrange("(i p) c -> p i c", p=128), u_sb[:, :, 1056:IN_W],
                           r=[u_sb.d], dw=[d_pin])
                kb.end_phase()

            with ExitStack() as es:
                bands = T(kb, es, "bands", [128, 5, 4, 128], F32)
                kb.dma(sp, bands[:], bands_in[:, :, :, :], w=[bands.d])
                wpool = T(kb, es, "wpool", [64, 4, 64], BF16)
                kb.dma(pool, wpool[:], w_pool[l].rearrange("g c d -> c g d"), w=[wpool.d])
                lsp = T(kb, es, "lsp", [64, 4], F32)
                for g in range(4):
                    kb.dma(sp, lsp[:, g:g + 1], ls_pool[l, g * 64:(g + 1) * 64].rearrange("(p o) -> p o", o=1),
                           w=[lsp.d])
                pw = [T(kb, es, f"pw{i}", [128, 3, 256], F32) for i in range(3)]
                py = [T(kb, es, f"py{i}", [64, 512], F32, psum=True) for i in range(2)]
                po = [T(kb, es, f"po{i}", [64, 512], F32, psum=True) for i in range(2)]
                yT = [T(kb, es, f"yT{i}", [64, 4, 128], BF16) for i in range(2)]
                oT = [T(kb, es, f"oT{i}", [64, 4, 128], BF16) for i in range(2)]
                seqs = [(NCT, NT)] if last else [(0, NCT), (NCT, NT)]
                it = 0
                for (s0, s1) in seqs:
                    for tt in range(s0, s1):
                        first, lastt = (tt == s0), (tt == s1 - 1)
                        j0 = 1 if first else 0
                        j1 = 2 if lastt else 3
                        pw_ = pw[it % 3]
                        kb.dma(sp, pw_[:, j0:j1, :],
                               pin[(tt - 1 + j0) * 128:(tt - 1 + j1) * 128, :].rearrange("(j p) c -> p j c", p=128),
                               w=[pw_.d], dr=[d_pin])
                        py_, po_, yT_, oT_ = py[it % 2], po[it % 2], yT[it % 2], oT[it % 2]
                        for g in range(4):
                            js = list(range(j0, j1))
                            for jj in js:
                                var = jj if jj != 1 else (3 if first else (4 if lastt else 1))
                                kb.op(pe, lambda g=g, jj=jj, var=var: nc.tensor.matmul(
                                    py_[:, g * 128:(g + 1) * 128], lhsT=pw_[:, jj, g * 64:(g + 1) * 64],
                                    rhs=bands[:, var, g, :], start=(jj == js[0]), stop=(jj == js[-1])),
                                    r=[pw_.d, bands.d], w=[py_.d], sig=(jj == js[-1]))
                        kb.op(act, lambda: nc.scalar.copy(out=yT_[:].rearrange("p g t -> p (g t)"), in_=py_[:, :]),
                              r=[py_.d], w=[yT_.d])
                        for g in range(4):
                            kb.op(pe, lambda g=g: nc.tensor.matmul(po_[:, g * 128:(g + 1) * 128], lhsT=wpool[:, g, :],
                                                                   rhs=yT_[:, g, :], start=True, stop=True),
                                  r=[wpool.d, yT_.d], w=[po_.d], sig=(g == 3))
                        kb.op(dve, lambda: nc.vector.tensor_tensor(
                            out=oT_[:], in0=po_[:, :].rearrange("p (g t) -> p g t", t=128),
                            in1=lsp[:, :].unsqueeze(2).to_broadcast([64, 4, 128]), op=ALU.mult),
                            r=[po_.d, lsp.d], w=[oT_.d])
                        kb.dma(sp, mixT[768:1024, tt * 128:(tt + 1) * 128].rearrange("(g d) t -> d g t", d=64), oT_[:],
                               r=[oT_.d], dw=[d_mix])
                        it += 1
                kb.end_phase()

            with ExitStack() as es:
                KS = 3
                S = [T(kb, es, f"S{i}", [128, KS * 512], F32, psum=True) for i in range(2)]
                O = [T(kb, es, f"O{i}", [128, 512], F32, psum=True) for i in range(2)]
                kT = [T(kb, es, f"kT{i}", [128, TT], BF16) for i in range(2)]
                vA = [T(kb, es, f"vA{i}", [128, NT, 128], BF16) for i in range(2)]
                qT = [T(kb, es, f"qT{i}", [128, TT], BF16) for i in range(2)]
                P = [T(kb, es, f"P{i}", [128, KS * 512], BF16) for i in range(3)]
                for z_ in kT + qT:
                    kb.op(pool, lambda z_=z_: nc.gpsimd.memset(z_[64:128, :], 0.0), w=[z_.d])
                rec = [T(kb, es, f"rec{i}", [128, 512], F32) for i in range(2)]
                ob = [T(kb, es, f"ob{i}", [64, 512], BF16) for i in range(3)]
                for v_ in vA:
                    kb.op(pool, lambda v_=v_: nc.gpsimd.memset(v_[:, :, 64:128], 1.0), w=[v_.d])
                groups = []
                for g in range(2):
                    groups.append((64, KTg[g], g * 64, [(QTg[3 * g + i], (3 * g + i) * 64) for i in range(3)]))
                for h in range(6):
                    groups.append((96, KTm[h], 128 + h * 64, [(QTm[h], 384 + h * 64)]))
                heads = []
                for gi, (dk, ksrc, vcol, qs) in enumerate(groups):
                    for (qsrc, row0) in qs:
                        heads.append((gi, dk, ksrc, vcol, qsrc, row0))

                loaded_g = {}

                def load_group(gi):
                    dk, ksrc, vcol, qs = groups[gi]
                    kt_, va_ = kT[gi % 2], vA[gi % 2]
                    kb.dma(sp, kt_[0:dk, :], ksrc[:, :], w=[kt_.d], dr=[d_q])
                    kb.dma(sp, va_[:, :, 0:64], Vs[:, vcol:vcol + 64].rearrange("(i p) c -> p i c", p=128),
                           w=[va_.d], dr=[d_v])

                def load_q(hi):
                    gi, dk, ksrc, vcol, qsrc, row0 = heads[hi]
                    kb.dma(sp, qT[hi % 2][0:dk, :], qsrc[:, :], w=[qT[hi % 2].d], dr=[d_q])

                steps = []
                for hi, (gi, dk, ksrc, vcol, qsrc, row0) in enumerate(heads):
                    blocks = []
                    if not last:
                        blocks.append((0, C, 0, NCT))
                    for qb in range(L // 512):
                        blocks.append((C + qb * 512, 512, 0, NT))
                    for (q0, nq, k0, nk) in blocks:
                        ks = KS if nk % KS == 0 else 2
                        assert nk % ks == 0
                        nst = nk // ks
                        for j in range(nst):
                            steps.append(dict(hi=hi, gi=gi, dk=dk, row0=row0, q0=q0, nq=nq, kt=k0 + ks * j, ks=ks,
                                              first=(j == 0), last=(j == nst - 1)))
                blk_id = -1
                for st in steps:
                    if st["first"]:
                        blk_id += 1
                    st["blk"] = blk_id

                def emit_qk(idx):
                    st = steps[idx]
                    s_ = S[idx % 2]
                    kt_, qt_ = kT[st["gi"] % 2], qT[st["hi"] % 2]
                    dk, nq, q0 = st["dk"], st["nq"], st["q0"]
                    for a in range(st["ks"]):
                        kti = st["kt"] + a
                        kb.op(pe, lambda a=a, kti=kti: nc.tensor.matmul(
                            s_[:, a * 512:a * 512 + nq], lhsT=kt_[:, kti * 128:(kti + 1) * 128],
                            rhs=qt_[:, q0:q0 + nq], start=True, stop=True),
                            r=[kt_.d, qt_.d], w=[s_.d], sig=(a == st["ks"] - 1))

                cur_g, cur_h = -1, -1
                load_group(0)
                load_q(0)
                ob_i = 0
                for idx, st in enumerate(steps):
                    if st["hi"] != cur_h:
                        cur_h = st["hi"]
                        if cur_h + 1 < len(heads):
                            nh = heads[cur_h + 1]
                            if nh[0] != st["gi"]:
                                load_group(nh[0])
                            load_q(cur_h + 1)
                    if idx == 0:
                        emit_qk(0)
                    if idx + 1 < len(steps):
                        emit_qk(idx + 1)
                    s_, p_ = S[idx % 2], P[idx % 3]
                    nq = st["nq"]
                    scale = float(st["dk"]) ** -0.5
                    va_ = vA[st["gi"] % 2]
                    o_ = O[st["blk"] % 2]
                    ks = st["ks"]
                    if nq == 512:
                        kb.op(act, lambda: nc.scalar.activation(out=p_[:, 0:ks * 512], in_=s_[:, 0:ks * 512],
                                                                func=AF.Exp, scale=scale),
                              r=[s_.d], w=[p_.d])
                    else:
                        kb.op(act, lambda: nc.scalar.activation(
                            out=p_[:, 0:ks * 512].rearrange("p (a n) -> p a n", a=ks)[:, :, 0:nq],
                            in_=s_[:, 0:ks * 512].rearrange("p (a n) -> p a n", a=ks)[:, :, 0:nq],
                            func=AF.Exp, scale=scale),
                            r=[s_.d], w=[p_.d])
                    for a in range(ks):
                        kti = st["kt"] + a
                        kb.op(pe, lambda a=a, kti=kti: nc.tensor.matmul(
                            o_[:, 0:nq], lhsT=va_[:, kti, :], rhs=p_[:, a * 512:a * 512 + nq],
                            start=(st["first"] and a == 0), stop=(st["last"] and a == ks - 1)),
                            r=[va_.d, p_.d], w=[o_.d], sig=(a == ks - 1))
                    if st["last"]:
                        rc = rec[st["blk"] % 2]
                        ob_ = ob[ob_i % 3]
                        ob_i += 1
                        kb.op(dve, lambda: nc.vector.reciprocal(out=rc[64:128, 0:nq], in_=o_[64:128, 0:nq]),
                              r=[o_.d], w=[rc.d])
                        kb.op(dve, lambda: nc.vector.tensor_tensor(out=ob_[:, 0:nq], in0=o_[0:64, 0:nq],
                                                                   in1=rc[64:128, 0:nq], op=ALU.mult),
                              r=[o_.d, rc.d], w=[ob_.d])
                        kb.dma(sp, mixT[st["row0"]:st["row0"] + 64, st["q0"]:st["q0"] + nq], ob_[:, 0:nq],
                               r=[ob_.d], dw=[d_mix])
                kb.end_phase()

            tile0 = NCT if last else 0
            with ExitStack() as es:
                NB = 2
                wout = T(kb, es, "wout", [128, KD, D], BF16)
                wv = w_out[l].rearrange("(k p) n -> p k n", p=128)
                for k in range(KD):
                    kb.dma(pool, wout[:, k, :], wv[:, k, :], w=[wout.d])
                g2t = T(kb, es, "g2t", [128, D], F32)
                kb.dma(sp, g2t[:], bc_row(g_norm2[l]), w=[g2t.d])
                GT1 = T(kb, es, "GT1", [128, D], F32)
                A2 = T(kb, es, "A2", [128, D], F32)
                B2 = T(kb, es, "B2", [128, D], F32)
                mT = [T(kb, es, f"mT{i}", [128, KD, NB * 128], BF16) for i in range(2)]
                xblk = [T(kb, es, f"xc{i}", [128, NB, D], F32) for i in range(2)]
                tmp = [T(kb, es, f"tmpc{i}", [128, D], F32) for i in range(2)]
                junk = T(kb, es, "junkC", [128, D], BF16)
                ssq2 = T(kb, es, "ssq2", [128, NB], F32)
                rstd2 = T(kb, es, "rstd2", [128, NB], F32)
                hb2 = [T(kb, es, f"hb2{i}", [128, D], BF16) for i in range(2)]
                h2blk = [T(kb, es, f"h2blk{i}", [128, KD, NB * 128], BF16) for i in range(2)]
                pyo = [T(kb, es, f"pyo{i}", [128, 2, 512], F32, psum=True) for i in range(2)]
                pT = [T(kb, es, f"pTc{i}", [128, 1024], BF16, psum=True) for i in range(2)]
                blks = list(range(tile0 // NB, NT // NB))

                def load_c1(b):
                    t0 = b * NB
                    tsl = slice(t0 * 128, (t0 + NB) * 128)
                    kb.dma(sp, mT[b % 2][:], mixT[:, tsl].rearrange("(k p) t -> p k t", p=128), w=[mT[b % 2].d],
                           dr=[d_mix])
                    kb.dma(sp, xblk[b % 2][:], x_src(l, t0, NB), w=[xblk[b % 2].d], dr=[d_xs])

                cur_stream = None
                load_c1(blks[0])
                ti = 0
                for bi, b in enumerate(blks):
                    t0 = b * NB
                    j = 1 if t0 < NCT else 0
                    if cur_stream != j:
                        load_mod_tile(GT1, l, j, 2)
                        make_scale_tile(A2, g2t, l, j, 4)
                        load_mod_tile(B2, l, j, 3)
                        cur_stream = j
                    if bi + 1 < len(blks):
                        load_c1(blks[bi + 1])
                    m_, x_, h2_ = mT[b % 2], xblk[b % 2], h2blk[b % 2]
                    for i in range(NB):
                        py_ = pyo[ti % 2]
                        tm = tmp[ti % 2]
                        for hf in range(2):
                            for k in range(KD):
                                kb.op(pe, lambda i=i, hf=hf, k=k: nc.tensor.matmul(
                                    py_[:, hf, :], lhsT=m_[:, k, i * 128:(i + 1) * 128],
                                    rhs=wout[:, k, hf * 512:(hf + 1) * 512], start=(k == 0), stop=(k == KD - 1)),
                                    r=[m_.d, wout.d], w=[py_.d], sig=(k == KD - 1))
                        kb.op(dve, lambda: nc.vector.tensor_tensor(out=tm[:], in0=py_[:].rearrange("p a n -> p (a n)"),
                                                                   in1=GT1[:], op=ALU.mult),
                              r=[py_.d, GT1.d], w=[tm.d])
                        kb.op(pool, lambda i=i: nc.gpsimd.tensor_tensor(out=x_[:, i, :], in0=x_[:, i, :], in1=tm[:],
                                                                        op=ALU.add), r=[x_.d, tm.d], w=[x_.d])
                        kb.op(act, lambda i=i: nc.scalar.activation(out=junk[:], in_=x_[:, i, :], func=AF.Square,
                                                                    accum_out=ssq2[:, i:i + 1]),
                              r=[x_.d], w=[junk.d, ssq2.d])
                        ti += 1
                    rstd_chain(ssq2, rstd2, NB, 1.0 / D)
                    tsl = slice(t0 * 128, (t0 + NB) * 128)
                    kb.dma(sp, xs[tsl, :].rearrange("(i p) c -> p i c", p=128), x_[:], r=[x_.d], dw=[d_xs])
                    for i in range(NB):
                        tm = tmp[ti % 2]
                        hb_ = hb2[ti % 2]
                        p_ = pT[ti % 2]
                        ti += 1
                        kb.op(dve, lambda i=i: nc.vector.scalar_tensor_tensor(
                            out=tm[:], in0=x_[:, i, :], scalar=rstd2[:, i:i + 1], in1=A2[:],
                            op0=ALU.mult, op1=ALU.mult), r=[x_.d, rstd2.d, A2.d], w=[tm.d])
                        kb.op(pool, lambda: nc.gpsimd.tensor_tensor(out=hb_[:], in0=tm[:], in1=B2[:], op=ALU.add),
                              r=[tm.d, B2.d], w=[hb_.d])
                        for k in range(KD):
                            kb.op(pe, lambda k=k: nc.tensor.transpose(p_[:, k * 128:(k + 1) * 128],
                                                                      hb_[:, k * 128:(k + 1) * 128], ident[:]),
                                  r=[hb_.d, ident.d], w=[p_.d], sig=(k == KD - 1))
                        kb.op(act, lambda i=i: nc.scalar.copy(out=h2_[:, :, i * 128:(i + 1) * 128],
                                                              in_=p_[:, :].rearrange("p (k t) -> p k t", t=128)),
                              r=[p_.d], w=[h2_.d])
                    kb.dma(sp, h2T[:, tsl].rearrange("(k p) t -> p k t", p=128), h2_[:], r=[h2_.d], dw=[d_h2])
                kb.end_phase()

            with ExitStack() as es:
                w1 = T(kb, es, "w1", [128, KD, DFF], BF16)
                w2 = T(kb, es, "w2", [128, 32, D], BF16)
                wv1 = w_mlp1[l].rearrange("(k p) n -> p k n", p=128)
                wv2 = w_mlp2[l].rearrange("(k p) n -> p k n", p=128)
                for k in range(KD):
                    for hf in range(2):
                        kb.dma(pool, w1[:, k, hf * 2048:(hf + 1) * 2048], wv1[:, k, hf * 2048:(hf + 1) * 2048],
                               w=[w1.d])
                for k in range(0, 32, 2):
                    kb.dma(pool, w2[:, k:k + 2, :], wv2[:, k:k + 2, :], w=[w2.d])
                GT2 = T(kb, es, "GT2", [128, D], F32)
                NBC = 4
                hblk = [T(kb, es, f"hblk{i}", [128, KD, NBC * 128], BF16) for i in range(2)]
                aT = T(kb, es, "aT", [128, 32, NBC * 128], BF16)
                rl = [T(kb, es, f"rl{i}", [128, NBC * 128], BF16) for i in range(2)]
                x1t = [T(kb, es, f"x1t{i}", [128, D], F32) for i in range(3)]
                tmp = [T(kb, es, f"tmpd{i}", [128, D], F32) for i in range(2)]
                pm1 = [T(kb, es, f"pm1{i}", [128, 512], F32, psum=True) for i in range(2)]
                py2 = [T(kb, es, f"py2{i}", [128, 2, 512], F32, psum=True) for i in range(2)]
                blocks = []
                if not last:
                    blocks.append((0, NCT))
                tcur = NCT
                while tcur < NT:
                    n = min(NBC, NT - tcur)
                    blocks.append((tcur, n))
                    tcur += n

                def load_c2(bi):
                    t0, n = blocks[bi]
                    kb.dma(sp, hblk[bi % 2][:, :, 0:n * 128],
                           h2T[:, t0 * 128:(t0 + n) * 128].rearrange("(k p) t -> p k t", p=128),
                           w=[hblk[bi % 2].d], dr=[d_h2])

                load_c2(0)
                cur_stream = None
                ci = 0
                xi = 0
                for bi, (t0, n) in enumerate(blocks):
                    j = 1 if t0 < NCT else 0
                    if cur_stream != j:
                        load_mod_tile(GT2, l, j, 5)
                        cur_stream = j
                    if bi + 1 < len(blocks):
                        load_c2(bi + 1)
                    h_ = hblk[bi % 2]
                    N = n * 128
                    for jf in range(32):
                        pm_ = pm1[ci % 2]
                        r_ = rl[ci % 2]
                        ci += 1
                        for k in range(KD):
                            kb.op(pe, lambda jf=jf, k=k: nc.tensor.matmul(
                                pm_[:, 0:N], lhsT=w1[:, k, jf * 128:(jf + 1) * 128], rhs=h_[:, k, 0:N],
                                start=(k == 0), stop=(k == KD - 1)), r=[w1.d, h_.d], w=[pm_.d], sig=(k == KD - 1))
                        kb.op(act, lambda: nc.scalar.activation(out=r_[:, 0:N], in_=pm_[:, 0:N], func=AF.Relu),
                              r=[pm_.d], w=[r_.d])
                        kb.op(dve, lambda jf=jf: nc.vector.tensor_tensor(out=aT[:, jf, 0:N], in0=r_[:, 0:N],
                                                                         in1=r_[:, 0:N], op=ALU.mult),
                              r=[r_.d], w=[aT.d])
                    for i in range(n):
                        tt = t0 + i
                        xt_ = x1t[xi % 3]
                        tm = tmp[xi % 2]
                        py_ = py2[xi % 2]
                        xi += 1
                        kb.dma(sp, xt_[:], xs[tt * 128:(tt + 1) * 128, :], w=[xt_.d], dr=[d_xs])
                        for hf in range(2):
                            for jf in range(32):
                                kb.op(pe, lambda i=i, hf=hf, jf=jf: nc.tensor.matmul(
                                    py_[:, hf, :], lhsT=aT[:, jf, i * 128:(i + 1) * 128],
                                    rhs=w2[:, jf, hf * 512:(hf + 1) * 512], start=(jf == 0), stop=(jf == 31)),
                                    r=[aT.d, w2.d], w=[py_.d], sig=(jf == 31))
                        kb.op(dve, lambda: nc.vector.tensor_tensor(out=tm[:], in0=py_[:].rearrange("p a n -> p (a n)"),
                                                                   in1=GT2[:], op=ALU.mult),
                              r=[py_.d, GT2.d], w=[tm.d])
                        kb.op(pool, lambda: nc.gpsimd.tensor_tensor(out=xt_[:], in0=xt_[:], in1=tm[:], op=ALU.add),
                              r=[xt_.d, tm.d], w=[xt_.d])
                        if last:
                            kb.dma(sp, out[(tt - NCT) * 128:(tt - NCT + 1) * 128, :], xt_[:], r=[xt_.d], dw=[d_out])
                        else:
                            kb.dma(sp, xs[tt * 128:(tt + 1) * 128, :], xt_[:], r=[xt_.d], dw=[d_xs])
                kb.end_phase()
        for s, c in d_out.pending.items():
            kb._wait(sp, s, c)
    return nc


_CACHE = {}


def _prep_consts(cfg):
    cg, sg, cm, sm = _rope_tables(cfg["L"], cfg["C"])
    return dict(ident=np.eye(128, dtype=np.float32), cosg=cg, sing=sg, cosm=cm, sinm=sm, bands=_pool_bands())


def kernel(x, c, ctx, c_ctx, w_mod, b_mod, g_norm1, g_norm2, w_in, g_q_gqa, g_k_gqa, g_cq, g_ckv,
           w_uq, w_ukv, g_q_mla, g_k_mla, w_pool, ls_pool, w_out, w_mlp1, w_mlp2):
    cfg = dict(CFG)
    x = np.asarray(x, np.float32)
    B = x.shape[0]
    cfg["L"], cfg["C"], cfg["DEPTH"] = x.shape[1], np.asarray(ctx).shape[1], np.asarray(w_mod).shape[0]
    key = (cfg["L"], cfg["C"], cfg["DEPTH"], cfg["DEBUG"])
    if key not in _CACHE:
        _CACHE[key] = (build(cfg), _prep_consts(cfg))
    nc, consts = _CACHE[key]
    f = lambda a: np.ascontiguousarray(np.asarray(a, np.float32))
    shared = dict(w_mod=f(w_mod), b_mod=f(b_mod), g_norm1=f(g_norm1), g_norm2=f(g_norm2), w_in=f(w_in),
                  g_q_gqa=f(g_q_gqa), g_k_gqa=f(g_k_gqa), g_cq=f(g_cq), g_ckv=f(g_ckv), w_uq=f(w_uq),
                  w_ukv=f(w_ukv), g_q_mla=f(g_q_mla), g_k_mla=f(g_k_mla), w_pool=f(w_pool), ls_pool=f(ls_pool),
                  w_out=f(w_out), w_mlp1=f(w_mlp1), w_mlp2=f(w_mlp2), **consts)
    c = f(c)
    c_ctx = f(c_ctx)
    ctx = f(ctx)
    in_maps = []
    for b in range(B):
        cc = np.stack([c[b], c_ctx], axis=-1).reshape(KD, 128, 2).transpose(1, 0, 2)
        in_maps.append(dict(x=np.ascontiguousarray(x[b]), ctx=np.ascontiguousarray(ctx[b]),
                            cc=np.ascontiguousarray(cc), **shared))
    res = run_bass_kernel_spmd(nc, in_maps, core_ids=list(range(B)))
    kernel.last_results = res.results
    return np.stack([np.asarray(r["out"], np.float32) for r in res.results], axis=0)
```

```python
import numpy as np
from contextlib import ExitStack
import concourse.bass as bass
import concourse.mybir as mybir
from concourse.bass_utils import run_bass_kernel_spmd

F32 = mybir.dt.float32
BF16 = mybir.dt.bfloat16
AF = mybir.ActivationFunctionType
ALU = mybir.AluOpType
AX = mybir.AxisListType

CFG = dict(L=8192, C=256, DEPTH=4, NCORES=8, DEBUG=False)
import os
OPT_SAME = int(os.environ.get("K_SAME", "0"))
OPT_ILV_A = int(os.environ.get("K_ILV_A", "1"))
OPT_ILV_A2 = int(os.environ.get("K_ILV_A2", "1"))
OPT_ILV_C1 = int(os.environ.get("K_ILV_C1", "1"))
D = 1024
KD = 8
DFF = 4096
IN_W = 1312
EPS = 1e-6
POOL_W = (2, 4, 8, 16)


class SemC:
    def __init__(self, h):
        self.h = h
        self.n = 0


class Dep:
    __slots__ = ("name", "writers", "readers", "dsem")

    def __init__(self, name=""):
        self.name = name
        self.writers = []
        self.readers = []
        self.dsem = None


class DDep:
    def __init__(self, name=""):
        self.name = name
        self.pending = {}


class Eng:
    def __init__(self, name, h):
        self.name = name
        self.h = h
        self.sem = None
        self.waited = {}


class KB:
    def __init__(self, nc, es):
        self.nc = nc
        self.es = es
        self.pe = Eng("pe", nc.tensor)
        self.act = Eng("act", nc.scalar)
        self.dve = Eng("dve", nc.vector)
        self.pool = Eng("pool", nc.gpsimd)
        self.sp = Eng("sp", nc.sync)
        self.engines = [self.pe, self.act, self.dve, self.pool, self.sp]
        self.all_sems = []
        self.dsem_free = []
        self.dsem_all = []
        self.phase_deps = []
        for i in range(44):
            s = SemC(es.enter_context(nc.semaphore(f"d{i}")))
            self.dsem_free.append(s)
            self.dsem_all.append(s)
        self.epoch_id = 0
        self.new_epoch()

    def new_epoch(self):
        for E in self.engines:
            if E is self.sp:
                continue
            E.sem = SemC(self.es.enter_context(self.nc.semaphore(f"e{self.epoch_id}_{E.name}")))
            self.all_sems.append(E.sem)
        self.epoch_id += 1

    def dep(self, name=""):
        d = Dep(name)
        self.phase_deps.append(d)
        return d

    def _wait(self, E, sem, cnt):
        if E.waited.get(sem, 0) >= cnt:
            return
        E.h.wait_ge(sem.h, cnt)
        E.waited[sem] = cnt

    def op(self, E, fn, r=(), w=(), sig=True):
        need = {}

        def add(s, c):
            if E.waited.get(s, 0) >= c:
                return
            if need.get(s, 0) < c:
                need[s] = c

        for d in r:
            for (we, s, c) in d.writers:
                add(s, c)
        for d in w:
            for (we, s, c) in d.writers:
                if we is not E or (OPT_SAME and E is not self.pe):
                    add(s, c)
            for (re, s, c) in d.readers:
                if re is not E or (OPT_SAME and E is not self.pe):
                    add(s, c)
        items = list(need.items())
        for (s, c) in items[:-1]:
            E.h.wait_ge(s.h, c)
            E.waited[s] = c
        ins = fn()
        if items:
            s, c = items[-1]
            ins._wait_ge(s.h, c)
            E.waited[s] = c
        if sig:
            E.sem.n += 1
            ins.then_inc(E.sem.h, 1)
            rec = (E, E.sem, E.sem.n)
        else:
            rec = (E, E.sem, E.sem.n + 1)
        for d in r:
            d.readers = [x for x in d.readers if x[0] is not E] + [rec]
        for d in w:
            d.writers = [rec]
            d.readers = []
        return ins

    def dma(self, Q, out, in_, r=(), w=(), dr=(), dw=()):
        d0 = (list(w) + list(r))[0]
        if d0.dsem is None:
            d0.dsem = self.dsem_free.pop(0)
        ds = d0.dsem
        for d in r:
            for (we, s, c) in d.writers:
                self._wait(Q, s, c)
        for d in w:
            for (we, s, c) in d.writers:
                if s is not ds:
                    self._wait(Q, s, c)
            for (re, s, c) in d.readers:
                self._wait(Q, s, c)
        for dd in dr:
            for s, c in dd.pending.items():
                self._wait(Q, s, c)
        ins = Q.h.dma_start(out=out, in_=in_)
        ds.n += 16
        ins.then_inc(ds.h, 16)
        rec = (None, ds, ds.n)
        for d in r:
            d.readers = [x for x in d.readers if x[1] is not ds] + [rec]
        for d in w:
            d.writers = [rec]
            d.readers = []
        for dd in dw:
            dd.pending[ds] = ds.n
        return ins

    def barrier(self):
        for E in self.engines:
            for F in self.engines:
                if F is E or F.sem is None:
                    continue
                if F.sem.n > 0:
                    self._wait(E, F.sem, F.sem.n)
            for s in self.dsem_all:
                if s.n > 0:
                    self._wait(E, s, s.n)

    def end_phase(self):
        self.barrier()
        for d in self.phase_deps:
            if d.dsem is not None:
                self.dsem_free.append(d.dsem)
                d.dsem = None
        self.phase_deps = []


class T:
    _uid = [0]

    def __init__(self, kb, es, name, shape, dt, psum=False):
        T._uid[0] += 1
        name = f"{name}_{T._uid[0]}"
        if psum:
            self.t = es.enter_context(kb.nc.psum_tensor(name, shape, dt))
        else:
            self.t = es.enter_context(kb.nc.sbuf_tensor(name, shape, dt))
        self.d = kb.dep(name)

    def __getitem__(self, k):
        return self.t[k]


def _rope_tables(L, C):
    rows = L // 64
    row = np.repeat(np.arange(rows, dtype=np.float32), 64)
    col = np.tile(np.arange(64, dtype=np.float32), rows)

    def tab(rot):
        n = rot // 4
        inv = (10000.0 ** (-np.arange(n, dtype=np.float32) / n)).astype(np.float32)
        ang = np.concatenate([row[:, None] * inv, col[:, None] * inv], axis=-1).astype(np.float32)
        cos = np.concatenate([np.ones((C, rot // 2), np.float32), np.cos(ang).astype(np.float32)], 0)
        sin = np.concatenate([np.zeros((C, rot // 2), np.float32), np.sin(ang).astype(np.float32)], 0)
        return cos, sin

    cg, sg = tab(64)
    cm, sm = tab(32)
    T_ = L + C
    nt = T_ // 128

    def lay(a):
        return np.ascontiguousarray(a.reshape(nt, 128, -1).transpose(1, 0, 2))

    return lay(cg), lay(sg), lay(cm), lay(sm)


def _pool_bands():
    B = np.zeros((128, 5, 4, 128), np.float32)
    for g, w in enumerate(POOL_W):
        h = w // 2
        for t in range(128):
            for tp in range(t - h, t + h):
                if 0 <= tp < 128:
                    B[tp, 1, g, t] += 1.0 / w
                elif tp < 0:
                    B[tp + 128, 0, g, t] += 1.0 / w
                else:
                    B[tp - 128, 2, g, t] += 1.0 / w
            B[t, 1, g, t] -= 1.0
            lo = max(t - h, 0)
            cnt = (t + h) - lo
            for tp in range(lo, min(t + h, 128)):
                B[tp, 3, g, t] += 1.0 / cnt
            B[t, 3, g, t] -= 1.0
            hi = min(t + h, 128)
            cnt = hi - (t - h)
            for tp in range(max(t - h, 0), hi):
                B[tp, 4, g, t] += 1.0 / cnt
            B[t, 4, g, t] -= 1.0
    return B


def build(cfg):
    L, C, DEPTH, DEBUG = cfg["L"], cfg["C"], cfg["DEPTH"], cfg["DEBUG"]
    TT = L + C
    NT = TT // 128
    NCT = C // 128
    nc = bass.Bass("TRN2", target_bir_lowering=False)

    def din(name, shape, dt=F32):
        return nc.dram_tensor(name, shape, dt, kind="ExternalInput").ap()

    def dscr(name, shape, dt):
        kind = "ExternalOutput" if DEBUG else "Internal"
        return nc.dram_tensor(name, shape, dt, kind=kind).ap()

    x_in = din("x", [L, D])
    ctx_in = din("ctx", [C, D])
    cc_in = din("cc", [128, KD, 2])
    w_mod = din("w_mod", [DEPTH, D, 6 * D])
    b_mod = din("b_mod", [DEPTH, 6 * D])
    g_norm1 = din("g_norm1", [DEPTH, D])
    g_norm2 = din("g_norm2", [DEPTH, D])
    w_in = din("w_in", [DEPTH, D, IN_W])
    g_q_gqa = din("g_q_gqa", [DEPTH, 64])
    g_k_gqa = din("g_k_gqa", [DEPTH, 64])
    g_cq = din("g_cq", [DEPTH, 256])
    g_ckv = din("g_ckv", [DEPTH, 128])
    w_uq = din("w_uq", [DEPTH, 256, 576])
    w_ukv = din("w_ukv", [DEPTH, 128, 768])
    g_q_mla = din("g_q_mla", [DEPTH, 96])
    g_k_mla = din("g_k_mla", [DEPTH, 96])
    w_pool = din("w_pool", [DEPTH, 4, 64, 64])
    ls_pool = din("ls_pool", [DEPTH, 256])
    w_out = din("w_out", [DEPTH, D, D])
    w_mlp1 = din("w_mlp1", [DEPTH, D, DFF])
    w_mlp2 = din("w_mlp2", [DEPTH, DFF, D])
    ident_in = din("ident", [128, 128])
    cosg_in = din("cosg", [128, NT, 32])
    sing_in = din("sing", [128, NT, 32])
    cosm_in = din("cosm", [128, NT, 16])
    sinm_in = din("sinm", [128, NT, 16])
    bands_in = din("bands", [128, 5, 4, 128])
    out = nc.dram_tensor("out", [L, D], F32, kind="ExternalOutput").ap()

    xs = dscr("xs", [TT, D], F32)
    modv = dscr("modv", [DEPTH, 2, 6 * D], F32)
    QTg = dscr("QTg", [6, 64, TT], BF16)
    KTg = dscr("KTg", [2, 64, TT], BF16)
    QTm = dscr("QTm", [6, 96, TT], BF16)
    KTm = dscr("KTm", [6, 96, TT], BF16)
    Vs = dscr("Vs", [TT, 512], BF16)
    pin = dscr("pin", [TT, 256], F32)
    mixT = dscr("mixT", [D, TT], BF16)
    h2T = dscr("h2T", [D, TT], BF16)

    d_xs, d_modv, d_q, d_v, d_pin, d_mix, d_h2, d_out = [DDep(n) for n in
                                                       ("xs", "modv", "q", "v", "pin", "mix", "h2", "out")]

    ges = ExitStack()
    with ges:
        kb = KB(nc, ges)
        pe, act, dve, pool, sp = kb.pe, kb.act, kb.dve, kb.pool, kb.sp

        ident = T(kb, ges, "ident", [128, 128], BF16)
        kb.dma(pool, ident[:], ident_in[:, :], w=[ident.d])
        epsc = T(kb, ges, "epsc", [128, 1], F32)
        kb.op(dve, lambda: nc.vector.memset(epsc[:], EPS), w=[epsc.d])

        def rstd_chain(ssq, rstd, n, inv_dim):
            kb.op(act, lambda: nc.scalar.activation(out=rstd[:, 0:n], in_=ssq[:, 0:n], func=AF.Ln,
                                                    bias=epsc[:, 0:1], scale=inv_dim),
                  r=[ssq.d, epsc.d], w=[rstd.d])
            kb.op(act, lambda: nc.scalar.activation(out=rstd[:, 0:n], in_=rstd[:, 0:n], func=AF.Exp, scale=-0.5),
                  r=[rstd.d], w=[rstd.d])

        def x_src(l, t0, n):
            if l == 0:
                if t0 < NCT:
                    a = ctx_in[t0 * 128:(t0 + n) * 128, :]
                else:
                    a = x_in[(t0 - NCT) * 128:(t0 - NCT + n) * 128, :]
            else:
                a = xs[t0 * 128:(t0 + n) * 128, :]
            return a.rearrange("(i p) c -> p i c", p=128)

        def bc_row(ap1d):
            return ap1d.partition_broadcast(128)

        with ExitStack() as es:
            cs_raw = T(kb, es, "cs_raw", [128, KD, 2], F32)
            cs = T(kb, es, "cs", [128, KD, 2], F32)
            kb.dma(sp, cs_raw[:], cc_in[:, :, :], w=[cs_raw.d])
            kb.op(act, lambda: nc.scalar.activation(out=cs[:], in_=cs_raw[:], func=AF.Silu), r=[cs_raw.d], w=[cs.d])
            wm = [T(kb, es, f"wm{i}", [128, KD, 512], F32) for i in range(3)]
            pm = [T(kb, es, f"pm{i}", [128, 512], F32, psum=True) for i in range(2)]
            mo = [T(kb, es, f"mo{i}", [2, 512], F32) for i in range(2)]
            bm = [T(kb, es, f"bm{i}", [2, 6 * D], F32) for i in range(2)]
            it = 0
            for l in range(DEPTH):
                bml = bm[l % 2]
                for j in range(2):
                    kb.dma(sp, bml[j:j + 1, :], b_mod[l:l + 1, :], w=[bml.d])
                wv = w_mod[l].rearrange("(k p) n -> p k n", p=128)
                for n in range(12):
                    wt = wm[it % 3]
                    kb.dma(sp, wt[:], wv[:, :, n * 512:(n + 1) * 512], w=[wt.d])
                    pt = pm[it % 2]
                    for k in range(KD):
                        kb.op(pe, lambda k=k: nc.tensor.matmul(pt[0:2, :], lhsT=cs[:, k, :], rhs=wt[:, k, :],
                                                               start=(k == 0), stop=(k == KD - 1)),
                              r=[cs.d, wt.d], w=[pt.d], sig=(k == KD - 1))
                    mt = mo[it % 2]
                    kb.op(dve, lambda: nc.vector.tensor_tensor(out=mt[:], in0=pt[0:2, :],
                                                               in1=bml[:, n * 512:(n + 1) * 512], op=ALU.add),
                          r=[pt.d, bml.d], w=[mt.d])
                    kb.dma(sp, modv[l, :, n * 512:(n + 1) * 512], mt[:], r=[mt.d], dw=[d_modv])
                    it += 1
            kb.end_phase()

        def load_mod_tile(tile, l, j, idx, tmp=None):
            kb.dma(sp, tile[:], bc_row(modv[l, j, idx * D:(idx + 1) * D]), w=[tile.d], dr=[d_modv])

        def make_scale_tile(tile, gt, l, j, idx):
            load_mod_tile(tile, l, j, idx)
            kb.op(dve, lambda: nc.vector.scalar_tensor_tensor(out=tile[:], in0=tile[:], scalar=1.0, in1=gt[:],
                                                             op0=ALU.add, op1=ALU.mult),
                  r=[tile.d, gt.d], w=[tile.d])

        for l in range(DEPTH):
            last = (l == DEPTH - 1)
            if l > 0:
                kb.new_epoch()
            with ExitStack() as es:
                NB = 2
                win = T(kb, es, "win", [128, KD, IN_W], BF16)
                wv = w_in[l].rearrange("(k p) n -> p k n", p=128)
                for k in range(KD):
                    kb.dma(pool, win[:, k, :], wv[:, k, :], w=[win.d])
                wuq_f = T(kb, es, "wuq_f", [128, 2, 576], F32)
                wuq = T(kb, es, "wuq", [128, 2, 576], BF16)
                wukv_f = T(kb, es, "wukv_f", [128, 768], F32)
                wukv = T(kb, es, "wukv", [128, 768], BF16)
                gcq = T(kb, es, "gcq", [128, 2], F32)
                gckv = T(kb, es, "gckv", [128, 1], F32)
                kb.dma(sp, wuq_f[:], w_uq[l].rearrange("(k p) n -> p k n", p=128), w=[wuq_f.d])
                kb.dma(sp, wukv_f[:], w_ukv[l], w=[wukv_f.d])
                for k in range(2):
                    kb.dma(sp, gcq[:, k:k + 1], g_cq[l, k * 128:(k + 1) * 128].rearrange("(p o) -> p o", o=1),
                           w=[gcq.d])
                kb.dma(sp, gckv[:, 0:1], g_ckv[l].rearrange("(p o) -> p o", o=1), w=[gckv.d])
                for k in range(2):
                    kb.op(dve, lambda k=k: nc.vector.tensor_scalar(out=wuq[:, k, :], in0=wuq_f[:, k, :],
                                                                   scalar1=gcq[:, k:k + 1], scalar2=None,
                                                                   op0=ALU.mult),
                          r=[wuq_f.d, gcq.d], w=[wuq.d])
                kb.op(dve, lambda: nc.vector.tensor_scalar(out=wukv[:], in0=wukv_f[:], scalar1=gckv[:, 0:1],
                                                           scalar2=None, op0=ALU.mult),
                      r=[wukv_f.d, gckv.d], w=[wukv.d])
                GQK = T(kb, es, "GQK", [128, 8, 64], F32)
                for h in range(8):
                    src = g_q_gqa[l] if h < 6 else g_k_gqa[l]
                    kb.dma(sp, GQK[:, h, :], bc_row(src), w=[GQK.d])
                GQM = T(kb, es, "GQM", [128, 6, 96], F32)
                GKM = T(kb, es, "GKM", [128, 6, 96], F32)
                for h in range(6):
                    kb.dma(sp, GQM[:, h, :], bc_row(g_q_mla[l]), w=[GQM.d])
                    kb.dma(sp, GKM[:, h, :], bc_row(g_k_mla[l]), w=[GKM.d])
                cosg = T(kb, es, "cosg", [128, NT, 32], F32)
                sing = T(kb, es, "sing", [128, NT, 32], F32)
                cosm = T(kb, es, "cosm", [128, NT, 16], F32)
                sinm = T(kb, es, "sinm", [128, NT, 16], F32)
                kb.dma(sp, cosg[:], cosg_in[:, :, :], w=[cosg.d])
                kb.dma(sp, sing[:], sing_in[:, :, :], w=[sing.d])
                kb.dma(sp, cosm[:], cosm_in[:, :, :], w=[cosm.d])
                kb.dma(sp, sinm[:], sinm_in[:, :, :], w=[sinm.d])
                g1t = T(kb, es, "g1t", [128, D], F32)
                kb.dma(sp, g1t[:], bc_row(g_norm1[l]), w=[g1t.d])
                A1 = T(kb, es, "A1", [128, D], F32)
                B1 = T(kb, es, "B1", [128, D], F32)

                xblk = [T(kb, es, f"xblk{i}", [128, NB, D], F32) for i in range(2)]
                junk = [T(kb, es, f"junkA{i}", [128, D], BF16) for i in range(2)]
                ssq1 = T(kb, es, "ssq1", [128, NB], F32)
                rstd1 = T(kb, es, "rstd1", [128, NB], F32)
                t1 = T(kb, es, "t1", [128, NB, D], F32)
                hb = T(kb, es, "hb", [128, NB, D], BF16)
                hT = T(kb, es, "hT", [128, NB, KD, 128], BF16)
                u_sbs = [T(kb, es, f"u_sb{i}", [128, NB, IN_W], F32) for i in range(2)]
                sq_g = T(kb, es, "sq_g", [128, NB, 512], F32)
                sq = T(kb, es, "sq", [128, NB, 576], F32)
                ssq8 = T(kb, es, "ssq8", [128, NB * 8], F32)
                rstd8 = T(kb, es, "rstd8", [128, NB * 8], F32)
                tq = T(kb, es, "tq", [128, NB, 8, 64], F32)
                tq2 = T(kb, es, "tq2", [128, NB, 8, 64], F32)
                tq3 = T(kb, es, "tq3", [128, NB, 8, 64], F32)
                qk_bf = T(kb, es, "qk_bf", [128, NB, 8, 64], BF16)
                ssqc = T(kb, es, "ssqc", [128, NB * 3], F32)
                ssqcq = T(kb, es, "ssqcq", [128, NB], F32)
                ssqckv = T(kb, es, "ssqckv", [128, NB], F32)
                rstdcq = T(kb, es, "rstdcq", [128, NB], F32)
                rstdckv = T(kb, es, "rstdckv", [128, NB], F32)
                cb = T(kb, es, "cb", [128, NB, 384], BF16)
                cT = T(kb, es, "cT", [128, NB, 3, 128], BF16)
                qm_sb = T(kb, es, "qm_sb", [128, NB, 6, 96], F32)
                kv_sb = T(kb, es, "kv_sb", [128, NB, 6, 128], F32)
                kcat = T(kb, es, "kcat", [128, NB, 6, 96], F32)
                ssq6 = T(kb, es, "ssq6", [128, NB * 6], F32)
                rstd6 = T(kb, es, "rstd6", [128, NB * 6], F32)
                qg = T(kb, es, "qg", [128, NB, 6, 96], F32)
                rt = [T(kb, es, f"rt{i}", [128, NB, 6, 16], F32) for i in range(4)]
                qm_bf = T(kb, es, "qm_bf", [128, NB, 6, 96], BF16)
                km_bf = T(kb, es, "km_bf", [128, NB, 6, 96], BF16)
                vg_blk = T(kb, es, "vg_blk", [128, NB, 128], BF16)
                vm_blk = T(kb, es, "vm_blk", [128, NB, 384], BF16)
                qkT_blk = T(kb, es, "qkT_blk", [64, 8, NB * 128], BF16)
                qmT_blk = T(kb, es, "qmT_blk", [96, 6, NB * 128], BF16)
                kmT_blk = T(kb, es, "kmT_blk", [96, 6, NB * 128], BF16)
                pTa = T(kb, es, "pTa", [128, 1024], BF16, psum=True)
                pu = T(kb, es, "pu", [128, 3, 512], F32, psum=True)
                pTg = T(kb, es, "pTg", [128, 1024], BF16, psum=True)
                pkv = T(kb, es, "pkv", [128, 2, 512], F32, psum=True)
                pTm = T(kb, es, "pTm", [128, 1024], BF16, psum=True)

                nblk = NT // NB
                cur_stream = [None]
                jk = [0]

                def load_x(b):
                    xb_ = xblk[b % 2]
                    kb.dma(sp, xb_[:], x_src(l, b * NB, NB), w=[xb_.d], dr=[d_xs])

                def sg1(b):
                    t0 = b * NB
                    j = 1 if t0 < NCT else 0
                    if cur_stream[0] != j:
                        make_scale_tile(A1, g1t, l, j, 1)
                        load_mod_tile(B1, l, j, 0)
                        cur_stream[0] = j
                    if b + 1 < nblk:
                        load_x(b + 1)
                    xb_ = xblk[b % 2]
                    u_sb = u_sbs[b % 2]
                    for i in range(NB):
                        jk[0] += 1
                        jt = junk[jk[0] % 2]
                        kb.op(act, lambda i=i: nc.scalar.activation(out=jt[:], in_=xb_[:, i, :], func=AF.Square,
                                                                    accum_out=ssq1[:, i:i + 1]),
                              r=[xb_.d], w=[jt.d, ssq1.d])
                        yield
                    rstd_chain(ssq1, rstd1, NB, 1.0 / D)
                    yield
                    for i in range(NB):
                        kb.op(dve, lambda i=i: nc.vector.scalar_tensor_tensor(
                            out=t1[:, i, :], in0=xb_[:, i, :], scalar=rstd1[:, i:i + 1], in1=A1[:],
                            op0=ALU.mult, op1=ALU.mult), r=[xb_.d, rstd1.d, A1.d], w=[t1.d])
                        yield
                    for i in range(NB):
                        kb.op(pool, lambda i=i: nc.gpsimd.tensor_tensor(out=hb[:, i, :], in0=t1[:, i, :], in1=B1[:],
                                                                        op=ALU.add), r=[t1.d, B1.d], w=[hb.d])
                        yield
                    for i in range(NB):
                        for k in range(KD):
                            kb.op(pe, lambda i=i, k=k: nc.tensor.transpose(pTa[:, k * 128:(k + 1) * 128],
                                                                           hb[:, i, k * 128:(k + 1) * 128], ident[:]),
                                  r=[hb.d, ident.d], w=[pTa.d], sig=(k == KD - 1))
                        yield
                        kb.op(act, lambda i=i: nc.scalar.copy(out=hT[:, i, :, :].rearrange("p k t -> p (k t)"),
                                                              in_=pTa[:, :]), r=[pTa.d], w=[hT.d])
                        yield
                    for i in range(NB):
                        for c, (c0, cw) in enumerate(((0, 512), (512, 512), (1024, 288))):
                            for k in range(KD):
                                kb.op(pe, lambda i=i, c=c, c0=c0, cw=cw, k=k: nc.tensor.matmul(
                                    pu[:, c, 0:cw], lhsT=hT[:, i, k, :], rhs=win[:, k, c0:c0 + cw],
                                    start=(k == 0), stop=(k == KD - 1)),
                                    r=[hT.d, win.d], w=[pu.d], sig=(k == KD - 1))
                            yield
                        kb.op(act, lambda i=i: nc.scalar.copy(out=u_sb[:, i, 0:1024].rearrange("p (c n) -> p c n", c=2),
                                                              in_=pu[:, 0:2, :]), r=[pu.d], w=[u_sb.d])
                        yield
                        kb.op(dve, lambda i=i: nc.vector.tensor_copy(out=u_sb[:, i, 1024:IN_W], in_=pu[:, 2, 0:288]),
                              r=[pu.d], w=[u_sb.d])
                        yield

                def sg2a(b):
                    t0 = b * NB
                    u_sb = u_sbs[b % 2]
                    tsl = slice(t0 * 128, (t0 + NB) * 128)
                    uqk = u_sb[:, :, 0:512]
                    kb.op(dve, lambda: nc.vector.tensor_tensor(out=sq_g[:], in0=uqk, in1=uqk, op=ALU.mult),
                          r=[u_sb.d], w=[sq_g.d])
                    yield
                    kb.op(dve, lambda: nc.vector.tensor_reduce(
                        out=ssq8[:, :].rearrange("p (n h) -> p n h", h=8),
                        in_=sq_g[:].rearrange("p n (h d) -> p n h d", d=64),
                        axis=AX.X, op=ALU.add), r=[sq_g.d], w=[ssq8.d])
                    yield
                    rstd_chain(ssq8, rstd8, NB * 8, 1.0 / 64)
                    yield
                    kb.op(dve, lambda: nc.vector.tensor_tensor(
                        out=tq[:],
                        in0=uqk.rearrange("p n (h d) -> p n h d", d=64),
                        in1=rstd8[:, :].rearrange("p (n h) -> p n h", h=8).unsqueeze(3).to_broadcast([128, NB, 8, 64]),
                        op=ALU.mult),
                        r=[u_sb.d, rstd8.d], w=[tq.d])
                    yield
                    for i in range(NB):
                        kb.op(pool, lambda i=i: nc.gpsimd.tensor_tensor(out=tq[:, i, :, :], in0=tq[:, i, :, :],
                                                                        in1=GQK[:], op=ALU.mult),
                              r=[tq.d, GQK.d], w=[tq.d])
                        yield
                    for i in range(NB):
                        cg = cosg[:, t0 + i, :].unsqueeze(1).to_broadcast([128, 8, 32])
                        sg = sing[:, t0 + i, :].unsqueeze(1).to_broadcast([128, 8, 32])
                        lo = tq[:, i, :, 0:32]
                        hi = tq[:, i, :, 32:64]
                        kb.op(dve, lambda: nc.vector.tensor_tensor(out=tq2[:, i, :, 0:32], in0=lo, in1=cg, op=ALU.mult),
                              r=[tq.d, cosg.d], w=[tq2.d])
                        kb.op(pool, lambda: nc.gpsimd.tensor_tensor(out=tq3[:, i, :, 0:32], in0=hi, in1=sg, op=ALU.mult),
                              r=[tq.d, sing.d], w=[tq3.d])
                        yield
                        kb.op(dve, lambda: nc.vector.tensor_tensor(out=tq2[:, i, :, 32:64], in0=hi, in1=cg, op=ALU.mult),
                              r=[tq.d, cosg.d], w=[tq2.d])
                        kb.op(pool, lambda: nc.gpsimd.tensor_tensor(out=tq3[:, i, :, 32:64], in0=lo, in1=sg, op=ALU.mult),
                              r=[tq.d, sing.d], w=[tq3.d])
                        yield
                        kb.op(dve, lambda: nc.vector.tensor_tensor(out=qk_bf[:, i, :, 0:32], in0=tq2[:, i, :, 0:32],
                                                                   in1=tq3[:, i, :, 0:32], op=ALU.subtract),
                              r=[tq2.d, tq3.d], w=[qk_bf.d])
                        yield
                        kb.op(dve, lambda: nc.vector.tensor_tensor(out=qk_bf[:, i, :, 32:64], in0=tq2[:, i, :, 32:64],
                                                                   in1=tq3[:, i, :, 32:64], op=ALU.add),
                              r=[tq2.d, tq3.d], w=[qk_bf.d])
                        yield
                    kb.op(act, lambda: nc.scalar.copy(out=vg_blk[:], in_=u_sb[:, :, 512:640]),
                          r=[u_sb.d], w=[vg_blk.d])
                    yield
                    kb.dma(sp, Vs[tsl, 0:128].rearrange("(i p) c -> p i c", p=128), vg_blk[:], r=[vg_blk.d], dw=[d_v])
                    kb.dma(sp, pin[tsl, :].rearrange("(i p) c -> p i c", p=128), u_sb[:, :, 1056:IN_W],
                           r=[u_sb.d], dw=[d_pin])
                    for i in range(NB):
                        for h in range(8):
                            kb.op(pe, lambda i=i, h=h: nc.tensor.transpose(pTg[0:64, h * 128:(h + 1) * 128],
                                                                           qk_bf[:, i, h, :], ident[:]),
                                  r=[qk_bf.d, ident.d], w=[pTg.d], sig=(h == 7))
                        yield
                        kb.op(act, lambda i=i: nc.scalar.copy(out=qkT_blk[:, :, i * 128:(i + 1) * 128],
                                                              in_=pTg[0:64, :].rearrange("p (h t) -> p h t", t=128)),
                              r=[pTg.d], w=[qkT_blk.d])
                        yield
                    kb.dma(sp, QTg[:, :, tsl].rearrange("h d t -> d h t"), qkT_blk[:, 0:6, :], r=[qkT_blk.d], dw=[d_q])
                    kb.dma(sp, KTg[:, :, tsl].rearrange("h d t -> d h t"), qkT_blk[:, 6:8, :], r=[qkT_blk.d], dw=[d_q])
                    yield

                def mla_norm_rope(src, G, dst, t0):
                    s3 = src[:].rearrange("p n h d -> p (n h) d")
                    kb.op(dve, lambda: nc.vector.tensor_tensor(
                        out=sq[:].rearrange("p n (h d) -> p (n h) d", d=96), in0=s3, in1=s3, op=ALU.mult),
                        r=[src.d], w=[sq.d])
                    yield
                    kb.op(dve, lambda: nc.vector.tensor_reduce(
                        out=ssq6[:, :], in_=sq[:].rearrange("p n (h d) -> p (n h) d", d=96),
                        axis=AX.X, op=ALU.add), r=[sq.d], w=[ssq6.d])
                    yield
                    rstd_chain(ssq6, rstd6, NB * 6, 1.0 / 96)
                    yield
                    kb.op(dve, lambda: nc.vector.tensor_tensor(
                        out=qg[:].rearrange("p n h d -> p (n h) d"), in0=s3,
                        in1=rstd6[:, :].unsqueeze(2).to_broadcast([128, NB * 6, 96]), op=ALU.mult),
                        r=[src.d, rstd6.d], w=[qg.d])
                    yield
                    for i in range(NB):
                        kb.op(pool, lambda i=i: nc.gpsimd.tensor_tensor(out=qg[:, i, :, :], in0=qg[:, i, :, :],
                                                                        in1=G[:], op=ALU.mult),
                              r=[qg.d, G.d], w=[qg.d])
                        yield
                    kb.op(act, lambda: nc.scalar.copy(out=dst[:, :, :, 0:64], in_=qg[:, :, :, 0:64]),
                          r=[qg.d], w=[dst.d])
                    yield
                    for i in range(NB):
                        cm_ = cosm[:, t0 + i, :].unsqueeze(1).to_broadcast([128, 6, 16])
                        sm_ = sinm[:, t0 + i, :].unsqueeze(1).to_broadcast([128, 6, 16])
                        lo = qg[:, i, :, 64:80]
                        hi = qg[:, i, :, 80:96]
                        kb.op(dve, lambda: nc.vector.tensor_tensor(out=rt[0][:, i, :, :], in0=lo, in1=cm_, op=ALU.mult),
                              r=[qg.d, cosm.d], w=[rt[0].d])
                        kb.op(pool, lambda: nc.gpsimd.tensor_tensor(out=rt[2][:, i, :, :], in0=lo, in1=sm_, op=ALU.mult),
                              r=[qg.d, sinm.d], w=[rt[2].d])
                        yield
                        kb.op(dve, lambda: nc.vector.tensor_tensor(out=rt[1][:, i, :, :], in0=hi, in1=sm_, op=ALU.mult),
                              r=[qg.d, sinm.d], w=[rt[1].d])
                        kb.op(pool, lambda: nc.gpsimd.tensor_tensor(out=rt[3][:, i, :, :], in0=hi, in1=cm_, op=ALU.mult),
                              r=[qg.d, cosm.d], w=[rt[3].d])
                        yield
                    kb.op(dve, lambda: nc.vector.tensor_tensor(out=dst[:, :, :, 64:80], in0=rt[0][:], in1=rt[1][:],
                                                               op=ALU.subtract), r=[rt[0].d, rt[1].d], w=[dst.d])
                    yield
                    kb.op(dve, lambda: nc.vector.tensor_tensor(out=dst[:, :, :, 80:96], in0=rt[2][:], in1=rt[3][:],
                                                               op=ALU.add), r=[rt[2].d, rt[3].d], w=[dst.d])
                    yield

                def sg2b(b):
                    t0 = b * NB
                    u_sb = u_sbs[b % 2]
                    tsl = slice(t0 * 128, (t0 + NB) * 128)
                    ucc = u_sb[:, :, 640:1024]
                    kb.op(dve, lambda: nc.vector.tensor_tensor(out=sq[:, :, 0:384], in0=ucc, in1=ucc, op=ALU.mult),
                          r=[u_sb.d], w=[sq.d])
                    yield
                    kb.op(dve, lambda: nc.vector.tensor_reduce(
                        out=ssqc[:, :].rearrange("p (n c) -> p n c", c=3),
                        in_=sq[:, :, 0:384].rearrange("p n (c d) -> p n c d", d=128),
                        axis=AX.X, op=ALU.add), r=[sq.d], w=[ssqc.d])
                    yield
                    sv = ssqc[:, :].rearrange("p (n c) -> p n c", c=3)
                    kb.op(dve, lambda: nc.vector.tensor_tensor(out=ssqcq[:, :], in0=sv[:, :, 0], in1=sv[:, :, 1],
                                                               op=ALU.add), r=[ssqc.d], w=[ssqcq.d])
                    kb.op(dve, lambda: nc.vector.tensor_copy(out=ssqckv[:, :], in_=sv[:, :, 2]),
                          r=[ssqc.d], w=[ssqckv.d])
                    yield
                    rstd_chain(ssqcq, rstdcq, NB, 1.0 / 256)
                    rstd_chain(ssqckv, rstdckv, NB, 1.0 / 128)
                    yield
                    kb.op(pool, lambda: nc.gpsimd.tensor_copy(out=cb[:], in_=ucc), r=[u_sb.d], w=[cb.d])
                    yield
                    for i in range(NB):
                        for k in range(3):
                            kb.op(pe, lambda i=i, k=k: nc.tensor.transpose(pTm[:, k * 128:(k + 1) * 128],
                                                                           cb[:, i, k * 128:(k + 1) * 128], ident[:]),
                                  r=[cb.d, ident.d], w=[pTm.d], sig=(k == 2))
                        yield
                        kb.op(act, lambda i=i: nc.scalar.copy(out=cT[:, i, :, :].rearrange("p k t -> p (k t)"),
                                                              in_=pTm[:, 0:384]), r=[pTm.d], w=[cT.d])
                        yield
                    for i in range(NB):
                        for (c, c0, cw) in ((0, 0, 512), (1, 512, 64)):
                            for k in range(2):
                                kb.op(pe, lambda i=i, c=c, c0=c0, cw=cw, k=k: nc.tensor.matmul(
                                    pkv[:, c, 0:cw], lhsT=cT[:, i, k, :], rhs=wuq[:, k, c0:c0 + cw],
                                    start=(k == 0), stop=(k == 1)), r=[cT.d, wuq.d], w=[pkv.d], sig=(k == 1))
                        yield
                        qmv = qm_sb[:, i, :, :].rearrange("p h d -> p (h d)")
                        kb.op(dve, lambda i=i: nc.vector.tensor_scalar(out=qmv[:, 0:512], in0=pkv[:, 0, :],
                                                                       scalar1=rstdcq[:, i:i + 1], scalar2=None,
                                                                       op0=ALU.mult),
                              r=[pkv.d, rstdcq.d], w=[qm_sb.d])
                        kb.op(dve, lambda i=i: nc.vector.tensor_scalar(out=qmv[:, 512:576], in0=pkv[:, 1, 0:64],
                                                                       scalar1=rstdcq[:, i:i + 1], scalar2=None,
                                                                       op0=ALU.mult),
                              r=[pkv.d, rstdcq.d], w=[qm_sb.d])
                        yield
                        for (c, c0, cw) in ((0, 0, 512), (1, 512, 256)):
                            kb.op(pe, lambda i=i, c=c, c0=c0, cw=cw: nc.tensor.matmul(
                                pkv[:, c, 0:cw], lhsT=cT[:, i, 2, :], rhs=wukv[:, c0:c0 + cw],
                                start=True, stop=True), r=[cT.d, wukv.d], w=[pkv.d])
                        yield
                        kvv = kv_sb[:, i, :, :].rearrange("p h d -> p (h d)")
                        kb.op(act, lambda i=i: nc.scalar.activation(out=kvv[:, 0:512], in_=pkv[:, 0, :],
                                                                    func=AF.Identity, scale=rstdckv[:, i:i + 1]),
                              r=[pkv.d, rstdckv.d], w=[kv_sb.d])
                        kb.op(act, lambda i=i: nc.scalar.activation(out=kvv[:, 512:768], in_=pkv[:, 1, 0:256],
                                                                    func=AF.Identity, scale=rstdckv[:, i:i + 1]),
                              r=[pkv.d, rstdckv.d], w=[kv_sb.d])
                        yield
                    kb.op(pool, lambda: nc.gpsimd.tensor_copy(out=kcat[:, :, :, 0:64], in_=kv_sb[:, :, :, 0:64]),
                          r=[kv_sb.d], w=[kcat.d])
                    yield
                    kb.op(pool, lambda: nc.gpsimd.tensor_copy(
                        out=kcat[:, :, :, 64:96],
                        in_=u_sb[:, :, 1024:1056].unsqueeze(2).to_broadcast([128, NB, 6, 32])),
                        r=[u_sb.d], w=[kcat.d])
                    yield
                    kb.op(act, lambda: nc.scalar.copy(out=vm_blk[:].rearrange("p n (h d) -> p n h d", d=64),
                                                      in_=kv_sb[:, :, :, 64:128]), r=[kv_sb.d], w=[vm_blk.d])
                    yield
                    kb.dma(sp, Vs[tsl, 128:512].rearrange("(i p) c -> p i c", p=128), vm_blk[:], r=[vm_blk.d], dw=[d_v])
                    yield from mla_norm_rope(qm_sb, GQM, qm_bf, t0)
                    yield from mla_norm_rope(kcat, GKM, km_bf, t0)
                    for (srcb, dstb, dram) in ((qm_bf, qmT_blk, QTm), (km_bf, kmT_blk, KTm)):
                        for i in range(NB):
                            for h in range(6):
                                kb.op(pe, lambda i=i, h=h: nc.tensor.transpose(pTm[0:96, h * 128:(h + 1) * 128],
                                                                               srcb[:, i, h, :], ident[:]),
                                      r=[srcb.d, ident.d], w=[pTm.d], sig=(h == 5))
                            yield
                            kb.op(dve, lambda i=i: nc.vector.tensor_copy(
                                out=dstb[:, :, i * 128:(i + 1) * 128],
                                in_=pTm[0:96, 0:768].rearrange("p (h t) -> p h t", t=128)), r=[pTm.d], w=[dstb.d])
                            yield
                        kb.dma(sp, dram[:, :, tsl].rearrange("h d t -> d h t"), dstb[:], r=[dstb.d], dw=[d_q])
                        yield

                def run_interleaved(gens, weights):
                    gens = [g for g in gens if g is not None]
                    if not OPT_ILV_A:
                        for g in gens:
                            for _ in g:
                                pass
                        return
                    alive = list(range(len(gens)))
                    while alive:
                        for gi in list(alive):
                            for _ in range(weights[gi]):
                                try:
                                    next(gens[gi])
                                except StopIteration:
                                    alive.remove(gi)
                                    break

                load_x(0)
                for b in range(nblk + 1):
                    g1 = sg1(b) if b < nblk else iter(())
                    g2 = sg2a(b - 1) if b >= 1 else iter(())
                    g3 = sg2b(b - 1) if b >= 1 else iter(())
                    run_interleaved([g1, g2, g3], [1, 1, 3])
                kb.end_phase()

            with ExitStack() as es:
                bands = T(kb, es, "bands", [128, 5, 4, 128], F32)
                kb.dma(sp, bands[:], bands_in[:, :, :, :], w=[bands.d])
                wpool = T(kb, es, "wpool", [64, 4, 64], BF16)
                kb.dma(pool, wpool[:], w_pool[l].rearrange("g c d -> c g d"), w=[wpool.d])
                lsp = T(kb, es, "lsp", [64, 4], F32)
                for g in range(4):
                    kb.dma(sp, lsp[:, g:g + 1], ls_pool[l, g * 64:(g + 1) * 64].rearrange("(p o) -> p o", o=1),
                           w=[lsp.d])
                pw = [T(kb, es, f"pw{i}", [128, 3, 256], F32) for i in range(3)]
                py = [T(kb, es, f"py{i}", [64, 512], F32, psum=True) for i in range(2)]
                po = [T(kb, es, f"po{i}", [64, 512], F32, psum=True) for i in range(2)]
                yT = [T(kb, es, f"yT{i}", [64, 4, 128], BF16) for i in range(2)]
                oT = [T(kb, es, f"oT{i}", [64, 4, 128], BF16) for i in range(2)]
                seqs = [(NCT, NT)] if last else [(0, NCT), (NCT, NT)]

                def pool_tile(tt, s0, s1, it):
                    first, lastt = (tt == s0), (tt == s1 - 1)
                    j0 = 1 if first else 0
                    j1 = 2 if lastt else 3
                    pw_ = pw[it % 3]
                    kb.dma(sp, pw_[:, j0:j1, :],
                           pin[(tt - 1 + j0) * 128:(tt - 1 + j1) * 128, :].rearrange("(j p) c -> p j c", p=128),
                           w=[pw_.d], dr=[d_pin])
                    py_, po_, yT_, oT_ = py[it % 2], po[it % 2], yT[it % 2], oT[it % 2]
                    for g in range(4):
                        js = list(range(j0, j1))
                        for jj in js:
                            var = jj if jj != 1 else (3 if first else (4 if lastt else 1))
                            kb.op(pe, lambda g=g, jj=jj, var=var: nc.tensor.matmul(
                                py_[:, g * 128:(g + 1) * 128], lhsT=pw_[:, jj, g * 64:(g + 1) * 64],
                                rhs=bands[:, var, g, :], start=(jj == js[0]), stop=(jj == js[-1])),
                                r=[pw_.d, bands.d], w=[py_.d], sig=(jj == js[-1]))
                        yield
                    kb.op(act, lambda: nc.scalar.copy(out=yT_[:].rearrange("p g t -> p (g t)"), in_=py_[:, :]),
                          r=[py_.d], w=[yT_.d])
                    yield
                    for g in range(4):
                        kb.op(pe, lambda g=g: nc.tensor.matmul(po_[:, g * 128:(g + 1) * 128], lhsT=wpool[:, g, :],
                                                               rhs=yT_[:, g, :], start=True, stop=True),
                              r=[wpool.d, yT_.d], w=[po_.d], sig=(g == 3))
                    yield
                    kb.op(dve, lambda: nc.vector.tensor_tensor(
                        out=oT_[:], in0=po_[:, :].rearrange("p (g t) -> p g t", t=128),
                        in1=lsp[:, :].unsqueeze(2).to_broadcast([64, 4, 128]), op=ALU.mult),
                        r=[po_.d, lsp.d], w=[oT_.d])
                    yield
                    kb.dma(sp, mixT[768:1024, tt * 128:(tt + 1) * 128].rearrange("(g d) t -> d g t", d=64), oT_[:],
                           r=[oT_.d], dw=[d_mix])
                    yield

                tiles = [(tt, s0, s1) for (s0, s1) in seqs for tt in range(s0, s1)]
                gens = [pool_tile(tt, s0, s1, it) for it, (tt, s0, s1) in enumerate(tiles)]
                active = []
                gi = 0
                while gi < len(gens) or active:
                    if gi < len(gens) and len(active) < (2 if OPT_ILV_A2 else 1):
                        active.append(gens[gi])
                        gi += 1
                    for g_ in list(active):
                        try:
                            next(g_)
                        except StopIteration:
                            active.remove(g_)
                kb.end_phase()

            with ExitStack() as es:
                KS = 3
                S = [T(kb, es, f"S{i}", [128, KS * 512], F32, psum=True) for i in range(2)]
                O = [T(kb, es, f"O{i}", [128, 512], F32, psum=True) for i in range(2)]
                kT = [T(kb, es, f"kT{i}", [128, TT], BF16) for i in range(2)]
                vA = [T(kb, es, f"vA{i}", [128, NT, 128], BF16) for i in range(2)]
                qT = [T(kb, es, f"qT{i}", [128, TT], BF16) for i in range(2)]
                P = [T(kb, es, f"P{i}", [128, KS * 512], BF16) for i in range(3)]
                for z_ in kT + qT:
                    kb.op(pool, lambda z_=z_: nc.gpsimd.memset(z_[64:128, :], 0.0), w=[z_.d])
                rec = [T(kb, es, f"rec{i}", [128, 512], F32) for i in range(2)]
                ob = [T(kb, es, f"ob{i}", [64, 512], BF16) for i in range(3)]
                for v_ in vA:
                    kb.op(pool, lambda v_=v_: nc.gpsimd.memset(v_[:, :, 64:128], 1.0), w=[v_.d])
                groups = []
                for g in range(2):
                    groups.append((64, KTg[g], g * 64, [(QTg[3 * g + i], (3 * g + i) * 64) for i in range(3)]))
                for h in range(6):
                    groups.append((96, KTm[h], 128 + h * 64, [(QTm[h], 384 + h * 64)]))
                heads = []
                for gi, (dk, ksrc, vcol, qs) in enumerate(groups):
                    for (qsrc, row0) in qs:
                        heads.append((gi, dk, ksrc, vcol, qsrc, row0))

                loaded_g = {}

                def load_group(gi):
                    dk, ksrc, vcol, qs = groups[gi]
                    kt_, va_ = kT[gi % 2], vA[gi % 2]
                    kb.dma(sp, kt_[0:dk, :], ksrc[:, :], w=[kt_.d], dr=[d_q])
                    kb.dma(sp, va_[:, :, 0:64], Vs[:, vcol:vcol + 64].rearrange("(i p) c -> p i c", p=128),
                           w=[va_.d], dr=[d_v])

                def load_q(hi):
                    gi, dk, ksrc, vcol, qsrc, row0 = heads[hi]
                    kb.dma(sp, qT[hi % 2][0:dk, :], qsrc[:, :], w=[qT[hi % 2].d], dr=[d_q])

                steps = []
                for hi, (gi, dk, ksrc, vcol, qsrc, row0) in enumerate(heads):
                    blocks = []
                    if not last:
                        blocks.append((0, C, 0, NCT))
                    for qb in range(L // 512):
                        blocks.append((C + qb * 512, 512, 0, NT))
                    for (q0, nq, k0, nk) in blocks:
                        ks = KS if nk % KS == 0 else 2
                        assert nk % ks == 0
                        nst = nk // ks
                        for j in range(nst):
                            steps.append(dict(hi=hi, gi=gi, dk=dk, row0=row0, q0=q0, nq=nq, kt=k0 + ks * j, ks=ks,
                                              first=(j == 0), last=(j == nst - 1)))
                blk_id = -1
                for st in steps:
                    if st["first"]:
                        blk_id += 1
                    st["blk"] = blk_id

                def emit_qk(idx):
                    st = steps[idx]
                    s_ = S[idx % 2]
                    kt_, qt_ = kT[st["gi"] % 2], qT[st["hi"] % 2]
                    dk, nq, q0 = st["dk"], st["nq"], st["q0"]
                    for a in range(st["ks"]):
                        kti = st["kt"] + a
                        kb.op(pe, lambda a=a, kti=kti: nc.tensor.matmul(
                            s_[:, a * 512:a * 512 + nq], lhsT=kt_[:, kti * 128:(kti + 1) * 128],
                            rhs=qt_[:, q0:q0 + nq], start=True, stop=True),
                            r=[kt_.d, qt_.d], w=[s_.d], sig=(a == st["ks"] - 1))

                cur_g, cur_h = -1, -1
                load_group(0)
                load_q(0)
                ob_i = 0
                for idx, st in enumerate(steps):
                    if st["hi"] != cur_h:
                        cur_h = st["hi"]
                        if cur_h + 1 < len(heads):
                            nh = heads[cur_h + 1]
                            if nh[0] != st["gi"]:
                                load_group(nh[0])
                            load_q(cur_h + 1)
                    if idx == 0:
                        emit_qk(0)
                    if idx + 1 < len(steps):
                        emit_qk(idx + 1)
                    s_, p_ = S[idx % 2], P[idx % 3]
                    nq = st["nq"]
                    scale = float(st["dk"]) ** -0.5
                    va_ = vA[st["gi"] % 2]
                    o_ = O[st["blk"] % 2]
                    ks = st["ks"]
                    if nq == 512:
                        kb.op(act, lambda: nc.scalar.activation(out=p_[:, 0:ks * 512], in_=s_[:, 0:ks * 512],
                                                                func=AF.Exp, scale=scale),
                              r=[s_.d], w=[p_.d])
                    else:
                        kb.op(act, lambda: nc.scalar.activation(
                            out=p_[:, 0:ks * 512].rearrange("p (a n) -> p a n", a=ks)[:, :, 0:nq],
                            in_=s_[:, 0:ks * 512].rearrange("p (a n) -> p a n", a=ks)[:, :, 0:nq],
                            func=AF.Exp, scale=scale),
                            r=[s_.d], w=[p_.d])
                    for a in range(ks):
                        kti = st["kt"] + a
                        kb.op(pe, lambda a=a, kti=kti: nc.tensor.matmul(
                            o_[:, 0:nq], lhsT=va_[:, kti, :], rhs=p_[:, a * 512:a * 512 + nq],
                            start=(st["first"] and a == 0), stop=(st["last"] and a == ks - 1)),
                            r=[va_.d, p_.d], w=[o_.d], sig=(a == ks - 1))
                    if st["last"]:
                        rc = rec[st["blk"] % 2]
                        ob_ = ob[ob_i % 3]
                        ob_i += 1
                        kb.op(dve, lambda: nc.vector.reciprocal(out=rc[64:128, 0:nq], in_=o_[64:128, 0:nq]),
                              r=[o_.d], w=[rc.d])
                        kb.op(dve, lambda: nc.vector.tensor_tensor(out=ob_[:, 0:nq], in0=o_[0:64, 0:nq],
                                                                   in1=rc[64:128, 0:nq], op=ALU.mult),
                              r=[o_.d, rc.d], w=[ob_.d])
                        kb.dma(sp, mixT[st["row0"]:st["row0"] + 64, st["q0"]:st["q0"] + nq], ob_[:, 0:nq],
                               r=[ob_.d], dw=[d_mix])
                kb.end_phase()

            tile0 = NCT if last else 0
            with ExitStack() as es:
                NB = 2
                wout = T(kb, es, "wout", [128, KD, D], BF16)
                wv = w_out[l].rearrange("(k p) n -> p k n", p=128)
                for k in range(KD):
                    kb.dma(pool, wout[:, k, :], wv[:, k, :], w=[wout.d])
                g2t = T(kb, es, "g2t", [128, D], F32)
                kb.dma(sp, g2t[:], bc_row(g_norm2[l]), w=[g2t.d])
                GT1 = T(kb, es, "GT1", [128, D], F32)
                A2 = T(kb, es, "A2", [128, D], F32)
                B2 = T(kb, es, "B2", [128, D], F32)
                mT = [T(kb, es, f"mT{i}", [128, KD, NB * 128], BF16) for i in range(3)]
                xblk = [T(kb, es, f"xc{i}", [128, NB, D], F32) for i in range(3)]
                tmp = [T(kb, es, f"tmpc{i}", [128, D], F32) for i in range(2)]
                junk = [T(kb, es, f"junkC{i}", [128, D], BF16) for i in range(2)]
                ssq2s = [T(kb, es, f"ssq2{i}", [128, NB], F32) for i in range(2)]
                rstd2s = [T(kb, es, f"rstd2{i}", [128, NB], F32) for i in range(2)]
                hb2 = [T(kb, es, f"hb2{i}", [128, D], BF16) for i in range(2)]
                h2blk = [T(kb, es, f"h2blk{i}", [128, KD, NB * 128], BF16) for i in range(2)]
                pyo = [T(kb, es, f"pyo{i}", [128, 2, 512], F32, psum=True) for i in range(2)]
                pT = [T(kb, es, f"pTc{i}", [128, 1024], BF16, psum=True) for i in range(2)]
                blks = list(range(tile0 // NB, NT // NB))

                def load_c1(b):
                    t0 = b * NB
                    tsl = slice(t0 * 128, (t0 + NB) * 128)
                    kb.dma(sp, mT[b % 3][:], mixT[:, tsl].rearrange("(k p) t -> p k t", p=128), w=[mT[b % 3].d],
                           dr=[d_mix])
                    kb.dma(sp, xblk[b % 3][:], x_src(l, t0, NB), w=[xblk[b % 3].d], dr=[d_xs])

                cur_stream = [None]
                load_c1(blks[0])
                tic = [0]

                def c1_block(bi, b):
                    t0 = b * NB
                    j = 1 if t0 < NCT else 0
                    if cur_stream[0] != j:
                        load_mod_tile(GT1, l, j, 2)
                        make_scale_tile(A2, g2t, l, j, 4)
                        load_mod_tile(B2, l, j, 3)
                        cur_stream[0] = j
                    if bi + 1 < len(blks):
                        load_c1(blks[bi + 1])
                    m_, x_, h2_ = mT[b % 3], xblk[b % 3], h2blk[b % 2]
                    ssq2, rstd2 = ssq2s[b % 2], rstd2s[b % 2]
                    for i in range(NB):
                        ti = tic[0]
                        tic[0] += 1
                        py_ = pyo[ti % 2]
                        tm = tmp[ti % 2]
                        for hf in range(2):
                            for k in range(KD):
                                kb.op(pe, lambda i=i, hf=hf, k=k: nc.tensor.matmul(
                                    py_[:, hf, :], lhsT=m_[:, k, i * 128:(i + 1) * 128],
                                    rhs=wout[:, k, hf * 512:(hf + 1) * 512], start=(k == 0), stop=(k == KD - 1)),
                                    r=[m_.d, wout.d], w=[py_.d], sig=(k == KD - 1))
                            yield
                        kb.op(dve, lambda: nc.vector.tensor_tensor(out=tm[:], in0=py_[:].rearrange("p a n -> p (a n)"),
                                                                   in1=GT1[:], op=ALU.mult),
                              r=[py_.d, GT1.d], w=[tm.d])
                        yield
                        kb.op(pool, lambda i=i: nc.gpsimd.tensor_tensor(out=x_[:, i, :], in0=x_[:, i, :], in1=tm[:],
                                                                        op=ALU.add), r=[x_.d, tm.d], w=[x_.d])
                        yield
                        jt = junk[ti % 2]
                        kb.op(act, lambda i=i: nc.scalar.activation(out=jt[:], in_=x_[:, i, :], func=AF.Square,
                                                                    accum_out=ssq2[:, i:i + 1]),
                              r=[x_.d], w=[jt.d, ssq2.d])
                        yield
                    rstd_chain(ssq2, rstd2, NB, 1.0 / D)
                    yield
                    tsl = slice(t0 * 128, (t0 + NB) * 128)
                    kb.dma(sp, xs[tsl, :].rearrange("(i p) c -> p i c", p=128), x_[:], r=[x_.d], dw=[d_xs])
                    for i in range(NB):
                        ti = tic[0]
                        tic[0] += 1
                        tm = tmp[ti % 2]
                        hb_ = hb2[ti % 2]
                        p_ = pT[ti % 2]
                        kb.op(dve, lambda i=i: nc.vector.scalar_tensor_tensor(
                            out=tm[:], in0=x_[:, i, :], scalar=rstd2[:, i:i + 1], in1=A2[:],
                            op0=ALU.mult, op1=ALU.mult), r=[x_.d, rstd2.d, A2.d], w=[tm.d])
                        yield
                        kb.op(pool, lambda: nc.gpsimd.tensor_tensor(out=hb_[:], in0=tm[:], in1=B2[:], op=ALU.add),
                              r=[tm.d, B2.d], w=[hb_.d])
                        yield
                        for k in range(KD):
                            kb.op(pe, lambda k=k: nc.tensor.transpose(p_[:, k * 128:(k + 1) * 128],
                                                                      hb_[:, k * 128:(k + 1) * 128], ident[:]),
                                  r=[hb_.d, ident.d], w=[p_.d], sig=(k == KD - 1))
                        yield
                        kb.op(act, lambda i=i: nc.scalar.copy(out=h2_[:, :, i * 128:(i + 1) * 128],
                                                              in_=p_[:, :].rearrange("p (k t) -> p k t", t=128)),
                              r=[p_.d], w=[h2_.d])
                        yield
                    kb.dma(sp, h2T[:, tsl].rearrange("(k p) t -> p k t", p=128), h2_[:], r=[h2_.d], dw=[d_h2])
                    yield

                bi = 0
                if blks and blks[0] * NB < NCT:
                    for _ in c1_block(0, blks[0]):
                        pass
                    bi = 1
                active = []
                while bi < len(blks) or active:
                    if bi < len(blks) and len(active) < (2 if OPT_ILV_C1 else 1):
                        active.append(c1_block(bi, blks[bi]))
                        bi += 1
                    for g_ in list(active):
                        try:
                            next(g_)
                        except StopIteration:
                            active.remove(g_)
                kb.end_phase()

            with ExitStack() as es:
                w1 = T(kb, es, "w1", [128, KD, DFF], BF16)
                w2 = T(kb, es, "w2", [128, 32, D], BF16)
                wv1 = w_mlp1[l].rearrange("(k p) n -> p k n", p=128)
                wv2 = w_mlp2[l].rearrange("(k p) n -> p k n", p=128)
                for k in range(KD):
                    for hf in range(2):
                        kb.dma(pool, w1[:, k, hf * 2048:(hf + 1) * 2048], wv1[:, k, hf * 2048:(hf + 1) * 2048],
                               w=[w1.d])
                for k in range(0, 32, 2):
                    kb.dma(pool, w2[:, k:k + 2, :], wv2[:, k:k + 2, :], w=[w2.d])
                GT2 = T(kb, es, "GT2", [128, D], F32)
                NBC = 4
                hblk = [T(kb, es, f"hblk{i}", [128, KD, NBC * 128], BF16) for i in range(2)]
                aT = T(kb, es, "aT", [128, 32, NBC * 128], BF16)
                rl = [T(kb, es, f"rl{i}", [128, NBC * 128], BF16) for i in range(2)]
                x1t = [T(kb, es, f"x1t{i}", [128, D], F32) for i in range(3)]
                tmp = [T(kb, es, f"tmpd{i}", [128, D], F32) for i in range(2)]
                pm1 = [T(kb, es, f"pm1{i}", [128, 512], F32, psum=True) for i in range(2)]
                py2 = [T(kb, es, f"py2{i}", [128, 2, 512], F32, psum=True) for i in range(2)]
                blocks = []
                if not last:
                    blocks.append((0, NCT))
                tcur = NCT
                while tcur < NT:
                    n = min(NBC, NT - tcur)
                    blocks.append((tcur, n))
                    tcur += n

                def load_c2(bi):
                    t0, n = blocks[bi]
                    kb.dma(sp, hblk[bi % 2][:, :, 0:n * 128],
                           h2T[:, t0 * 128:(t0 + n) * 128].rearrange("(k p) t -> p k t", p=128),
                           w=[hblk[bi % 2].d], dr=[d_h2])

                load_c2(0)
                cur_stream = None
                ci = 0
                xi = 0
                for bi, (t0, n) in enumerate(blocks):
                    j = 1 if t0 < NCT else 0
                    if cur_stream != j:
                        load_mod_tile(GT2, l, j, 5)
                        cur_stream = j
                    if bi + 1 < len(blocks):
                        load_c2(bi + 1)
                    h_ = hblk[bi % 2]
                    N = n * 128
                    for jf in range(32):
                        pm_ = pm1[ci % 2]
                        r_ = rl[ci % 2]
                        ci += 1
                        for k in range(KD):
                            kb.op(pe, lambda jf=jf, k=k: nc.tensor.matmul(
                                pm_[:, 0:N], lhsT=w1[:, k, jf * 128:(jf + 1) * 128], rhs=h_[:, k, 0:N],
                                start=(k == 0), stop=(k == KD - 1)), r=[w1.d, h_.d], w=[pm_.d], sig=(k == KD - 1))
                        kb.op(act, lambda: nc.scalar.activation(out=r_[:, 0:N], in_=pm_[:, 0:N], func=AF.Relu),
                              r=[pm_.d], w=[r_.d])
                        kb.op(dve, lambda jf=jf: nc.vector.tensor_tensor(out=aT[:, jf, 0:N], in0=r_[:, 0:N],
                                                                         in1=r_[:, 0:N], op=ALU.mult),
                              r=[r_.d], w=[aT.d])
                    for i in range(n):
                        tt = t0 + i
                        xt_ = x1t[xi % 3]
                        tm = tmp[xi % 2]
                        py_ = py2[xi % 2]
                        xi += 1
                        kb.dma(sp, xt_[:], xs[tt * 128:(tt + 1) * 128, :], w=[xt_.d], dr=[d_xs])
                        for hf in range(2):
                            for jf in range(32):
                                kb.op(pe, lambda i=i, hf=hf, jf=jf: nc.tensor.matmul(
                                    py_[:, hf, :], lhsT=aT[:, jf, i * 128:(i + 1) * 128],
                                    rhs=w2[:, jf, hf * 512:(hf + 1) * 512], start=(jf == 0), stop=(jf == 31)),
                                    r=[aT.d, w2.d], w=[py_.d], sig=(jf == 31))
                        kb.op(dve, lambda: nc.vector.tensor_tensor(out=tm[:], in0=py_[:].rearrange("p a n -> p (a n)"),
                                                                   in1=GT2[:], op=ALU.mult),
                              r=[py_.d, GT2.d], w=[tm.d])
                        kb.op(pool, lambda: nc.gpsimd.tensor_tensor(out=xt_[:], in0=xt_[:], in1=tm[:], op=ALU.add),
                              r=[xt_.d, tm.d], w=[xt_.d])
                        if last:
                            kb.dma(sp, out[(tt - NCT) * 128:(tt - NCT + 1) * 128, :], xt_[:], r=[xt_.d], dw=[d_out])
                        else:
                            kb.dma(sp, xs[tt * 128:(tt + 1) * 128, :], xt_[:], r=[xt_.d], dw=[d_xs])
                kb.end_phase()
        for s, c in d_out.pending.items():
            kb._wait(sp, s, c)
    return nc


_CACHE = {}


def _prep_consts(cfg):
    cg, sg, cm, sm = _rope_tables(cfg["L"], cfg["C"])
    return dict(ident=np.eye(128, dtype=np.float32), cosg=cg, sing=sg, cosm=cm, sinm=sm, bands=_pool_bands())


def kernel(x, c, ctx, c_ctx, w_mod, b_mod, g_norm1, g_norm2, w_in, g_q_gqa, g_k_gqa, g_cq, g_ckv,
           w_uq, w_ukv, g_q_mla, g_k_mla, w_pool, ls_pool, w_out, w_mlp1, w_mlp2):
    cfg = dict(CFG)
    x = np.asarray(x, np.float32)
    B = x.shape[0]
    cfg["L"], cfg["C"], cfg["DEPTH"] = x.shape[1], np.asarray(ctx).shape[1], np.asarray(w_mod).shape[0]
    key = (cfg["L"], cfg["C"], cfg["DEPTH"], cfg["DEBUG"])
    if key not in _CACHE:
        _CACHE[key] = (build(cfg), _prep_consts(cfg))
    nc, consts = _CACHE[key]
    f = lambda a: np.ascontiguousarray(np.asarray(a, np.float32))
    shared = dict(w_mod=f(w_mod), b_mod=f(b_mod), g_norm1=f(g_norm1), g_norm2=f(g_norm2), w_in=f(w_in),
                  g_q_gqa=f(g_q_gqa), g_k_gqa=f(g_k_gqa), g_cq=f(g_cq), g_ckv=f(g_ckv), w_uq=f(w_uq),
                  w_ukv=f(w_ukv), g_q_mla=f(g_q_mla), g_k_mla=f(g_k_mla), w_pool=f(w_pool), ls_pool=f(ls_pool),
                  w_out=f(w_out), w_mlp1=f(w_mlp1), w_mlp2=f(w_mlp2), **consts)
    c = f(c)
    c_ctx = f(c_ctx)
    ctx = f(ctx)
    in_maps = []
    for b in range(B):
        cc = np.stack([c[b], c_ctx], axis=-1).reshape(KD, 128, 2).transpose(1, 0, 2)
        in_maps.append(dict(x=np.ascontiguousarray(x[b]), ctx=np.ascontiguousarray(ctx[b]),
                            cc=np.ascontiguousarray(cc), **shared))
    res = run_bass_kernel_spmd(nc, in_maps, core_ids=list(range(B)))
    kernel.last_results = res.results
    return np.stack([np.asarray(r["out"], np.float32) for r in res.results], axis=0)
```

```python
import numpy as np
from contextlib import ExitStack
import concourse.bass as bass
import concourse.mybir as mybir
from concourse.bass_utils import run_bass_kernel_spmd

F32 = mybir.dt.float32
BF16 = mybir.dt.bfloat16
AF = mybir.ActivationFunctionType
ALU = mybir.AluOpType
AX = mybir.AxisListType

CFG = dict(L=8192, C=256, DEPTH=4, NCORES=8, DEBUG=False)
import os
OPT_SAME = int(os.environ.get("K_SAME", "0"))
OPT_ILV_A = int(os.environ.get("K_ILV_A", "1"))
OPT_ILV_A2 = int(os.environ.get("K_ILV_A2", "1"))
OPT_ILV_C1 = int(os.environ.get("K_ILV_C1", "1"))
D = 1024
KD = 8
DFF = 4096
IN_W = 1312
EPS = 1e-6
POOL_W = (2, 4, 8, 16)


class SemC:
    def __init__(self, h):
        self.h = h
        self.n = 0


class Dep:
    __slots__ = ("name", "writers", "readers", "dsem")

    def __init__(self, name=""):
        self.name = name
        self.writers = []
        self.readers = []
        self.dsem = None


class DDep:
    def __init__(self, name=""):
        self.name = name
        self.pending = {}


class Eng:
    def __init__(self, name, h):
        self.name = name
        self.h = h
        self.sem = None
        self.waited = {}


class KB:
    def __init__(self, nc, es):
        self.nc = nc
        self.es = es
        self.pe = Eng("pe", nc.tensor)
        self.act = Eng("act", nc.scalar)
        self.dve = Eng("dve", nc.vector)
        self.pool = Eng("pool", nc.gpsimd)
        self.sp = Eng("sp", nc.sync)
        self.engines = [self.pe, self.act, self.dve, self.pool, self.sp]
        self.all_sems = []
        self.dsem_free = []
        self.dsem_all = []
        self.phase_deps = []
        for i in range(44):
            s = SemC(es.enter_context(nc.semaphore(f"d{i}")))
            self.dsem_free.append(s)
            self.dsem_all.append(s)
        self.epoch_id = 0
        self.new_epoch()

    def new_epoch(self):
        for E in self.engines:
            if E is self.sp:
                continue
            E.sem = SemC(self.es.enter_context(self.nc.semaphore(f"e{self.epoch_id}_{E.name}")))
            self.all_sems.append(E.sem)
        self.epoch_id += 1

    def dep(self, name=""):
        d = Dep(name)
        self.phase_deps.append(d)
        return d

    def _wait(self, E, sem, cnt):
        if E.waited.get(sem, 0) >= cnt:
            return
        E.h.wait_ge(sem.h, cnt)
        E.waited[sem] = cnt

    def op(self, E, fn, r=(), w=(), sig=True):
        need = {}

        def add(s, c):
            if E.waited.get(s, 0) >= c:
                return
            if need.get(s, 0) < c:
                need[s] = c

        for d in r:
            for (we, s, c) in d.writers:
                add(s, c)
        for d in w:
            for (we, s, c) in d.writers:
                if we is not E or (OPT_SAME and E is not self.pe):
                    add(s, c)
            for (re, s, c) in d.readers:
                if re is not E or (OPT_SAME and E is not self.pe):
                    add(s, c)
        items = list(need.items())
        for (s, c) in items[:-1]:
            E.h.wait_ge(s.h, c)
            E.waited[s] = c
        ins = fn()
        if items:
            s, c = items[-1]
            ins._wait_ge(s.h, c)
            E.waited[s] = c
        if sig:
            E.sem.n += 1
            ins.then_inc(E.sem.h, 1)
            rec = (E, E.sem, E.sem.n)
        else:
            rec = (E, E.sem, E.sem.n + 1)
        for d in r:
            d.readers = [x for x in d.readers if x[0] is not E] + [rec]
        for d in w:
            d.writers = [rec]
            d.readers = []
        return ins

    def dma(self, Q, out, in_, r=(), w=(), dr=(), dw=()):
        d0 = (list(w) + list(r))[0]
        if d0.dsem is None:
            d0.dsem = self.dsem_free.pop(0)
        ds = d0.dsem
        for d in r:
            for (we, s, c) in d.writers:
                self._wait(Q, s, c)
        for d in w:
            for (we, s, c) in d.writers:
                if s is not ds:
                    self._wait(Q, s, c)
            for (re, s, c) in d.readers:
                self._wait(Q, s, c)
        for dd in dr:
            for s, c in dd.pending.items():
                self._wait(Q, s, c)
        ins = Q.h.dma_start(out=out, in_=in_)
        ds.n += 16
        ins.then_inc(ds.h, 16)
        rec = (None, ds, ds.n)
        for d in r:
            d.readers = [x for x in d.readers if x[1] is not ds] + [rec]
        for d in w:
            d.writers = [rec]
            d.readers = []
        for dd in dw:
            dd.pending[ds] = ds.n
        return ins

    def barrier(self):
        for E in self.engines:
            for F in self.engines:
                if F is E or F.sem is None:
                    continue
                if F.sem.n > 0:
                    self._wait(E, F.sem, F.sem.n)
            for s in self.dsem_all:
                if s.n > 0:
                    self._wait(E, s, s.n)

    def end_phase(self):
        self.barrier()
        for d in self.phase_deps:
            if d.dsem is not None:
                self.dsem_free.append(d.dsem)
                d.dsem = None
        self.phase_deps = []


class T:
    _uid = [0]

    def __init__(self, kb, es, name, shape, dt, psum=False):
        T._uid[0] += 1
        name = f"{name}_{T._uid[0]}"
        if psum:
            self.t = es.enter_context(kb.nc.psum_tensor(name, shape, dt))
        else:
            self.t = es.enter_context(kb.nc.sbuf_tensor(name, shape, dt))
        self.d = kb.dep(name)

    def __getitem__(self, k):
        return self.t[k]


def _rope_tables(L, C):
    rows = L // 64
    row = np.repeat(np.arange(rows, dtype=np.float32), 64)
    col = np.tile(np.arange(64, dtype=np.float32), rows)

    def tab(rot):
        n = rot // 4
        inv = (10000.0 ** (-np.arange(n, dtype=np.float32) / n)).astype(np.float32)
        ang = np.concatenate([row[:, None] * inv, col[:, None] * inv], axis=-1).astype(np.float32)
        cos = np.concatenate([np.ones((C, rot // 2), np.float32), np.cos(ang).astype(np.float32)], 0)
        sin = np.concatenate([np.zeros((C, rot // 2), np.float32), np.sin(ang).astype(np.float32)], 0)
        return cos, sin

    cg, sg = tab(64)
    cm, sm = tab(32)
    T_ = L + C
    nt = T_ // 128

    def lay(a):
        return np.ascontiguousarray(a.reshape(nt, 128, -1).transpose(1, 0, 2))

    return lay(cg), lay(sg), lay(cm), lay(sm)


def _pool_bands():
    B = np.zeros((128, 5, 4, 128), np.float32)
    for g, w in enumerate(POOL_W):
        h = w // 2
        for t in range(128):
            for tp in range(t - h, t + h):
                if 0 <= tp < 128:
                    B[tp, 1, g, t] += 1.0 / w
                elif tp < 0:
                    B[tp + 128, 0, g, t] += 1.0 / w
                else:
                    B[tp - 128, 2, g, t] += 1.0 / w
            B[t, 1, g, t] -= 1.0
            lo = max(t - h, 0)
            cnt = (t + h) - lo
            for tp in range(lo, min(t + h, 128)):
                B[tp, 3, g, t] += 1.0 / cnt
            B[t, 3, g, t] -= 1.0
            hi = min(t + h, 128)
            cnt = hi - (t - h)
            for tp in range(max(t - h, 0), hi):
                B[tp, 4, g, t] += 1.0 / cnt
            B[t, 4, g, t] -= 1.0
    return B


def build(cfg):
    L, C, DEPTH, DEBUG = cfg["L"], cfg["C"], cfg["DEPTH"], cfg["DEBUG"]
    TT = L + C
    NT = TT // 128
    NCT = C // 128
    nc = bass.Bass("TRN2", target_bir_lowering=False)

    def din(name, shape, dt=F32):
        return nc.dram_tensor(name, shape, dt, kind="ExternalInput").ap()

    def dscr(name, shape, dt):
        kind = "ExternalOutput" if DEBUG else "Internal"
        return nc.dram_tensor(name, shape, dt, kind=kind).ap()

    x_in = din("x", [L, D])
    ctx_in = din("ctx", [C, D])
    cc_in = din("cc", [128, KD, 2])
    w_mod = din("w_mod", [DEPTH, D, 6 * D])
    b_mod = din("b_mod", [DEPTH, 6 * D])
    g_norm1 = din("g_norm1", [DEPTH, D])
    g_norm2 = din("g_norm2", [DEPTH, D])
    w_in = din("w_in", [DEPTH, D, IN_W])
    g_q_gqa = din("g_q_gqa", [DEPTH, 64])
    g_k_gqa = din("g_k_gqa", [DEPTH, 64])
    g_cq = din("g_cq", [DEPTH, 256])
    g_ckv = din("g_ckv", [DEPTH, 128])
    w_uq = din("w_uq", [DEPTH, 256, 576])
    w_ukv = din("w_ukv", [DEPTH, 128, 768])
    g_q_mla = din("g_q_mla", [DEPTH, 96])
    g_k_mla = din("g_k_mla", [DEPTH, 96])
    w_pool = din("w_pool", [DEPTH, 4, 64, 64])
    ls_pool = din("ls_pool", [DEPTH, 256])
    w_out = din("w_out", [DEPTH, D, D])
    w_mlp1 = din("w_mlp1", [DEPTH, D, DFF])
    w_mlp2 = din("w_mlp2", [DEPTH, DFF, D])
    ident_in = din("ident", [128, 128])
    cosg_in = din("cosg", [128, NT, 32])
    sing_in = din("sing", [128, NT, 32])
    cosm_in = din("cosm", [128, NT, 16])
    sinm_in = din("sinm", [128, NT, 16])
    bands_in = din("bands", [128, 5, 4, 128])
    out = nc.dram_tensor("out", [L, D], F32, kind="ExternalOutput").ap()

    xs = dscr("xs", [TT, D], F32)
    modv = dscr("modv", [DEPTH, 2, 6 * D], F32)
    QTg = dscr("QTg", [6, 64, TT], BF16)
    KTg = dscr("KTg", [2, 64, TT], BF16)
    QTm = dscr("QTm", [6, 96, TT], BF16)
    KTm = dscr("KTm", [6, 96, TT], BF16)
    Vs = dscr("Vs", [TT, 512], BF16)
    pin = dscr("pin", [TT, 256], F32)
    mixT = dscr("mixT", [D, TT], BF16)
    h2T = dscr("h2T", [D, TT], BF16)

    d_xs, d_modv, d_q, d_v, d_pin, d_mix, d_h2, d_out = [DDep(n) for n in
                                                       ("xs", "modv", "q", "v", "pin", "mix", "h2", "out")]

    ges = ExitStack()
    with ges:
        kb = KB(nc, ges)
        pe, act, dve, pool, sp = kb.pe, kb.act, kb.dve, kb.pool, kb.sp

        ident = T(kb, ges, "ident", [128, 128], BF16)
        kb.dma(pool, ident[:], ident_in[:, :], w=[ident.d])
        epsc = T(kb, ges, "epsc", [128, 1], F32)
        kb.op(dve, lambda: nc.vector.memset(epsc[:], EPS), w=[epsc.d])

        def rstd_chain(ssq, rstd, n, inv_dim):
            kb.op(act, lambda: nc.scalar.activation(out=rstd[:, 0:n], in_=ssq[:, 0:n], func=AF.Ln,
                                                    bias=epsc[:, 0:1], scale=inv_dim),
                  r=[ssq.d, epsc.d], w=[rstd.d])
            kb.op(act, lambda: nc.scalar.activation(out=rstd[:, 0:n], in_=rstd[:, 0:n], func=AF.Exp, scale=-0.5),
                  r=[rstd.d], w=[rstd.d])

        def x_src(l, t0, n):
            if l == 0:
                if t0 < NCT:
                    a = ctx_in[t0 * 128:(t0 + n) * 128, :]
                else:
                    a = x_in[(t0 - NCT) * 128:(t0 - NCT + n) * 128, :]
            else:
                a = xs[t0 * 128:(t0 + n) * 128, :]
            return a.rearrange("(i p) c -> p i c", p=128)

        def bc_row(ap1d):
            return ap1d.partition_broadcast(128)

        with ExitStack() as es:
            cs_raw = T(kb, es, "cs_raw", [128, KD, 2], F32)
            cs = T(kb, es, "cs", [128, KD, 2], F32)
            kb.dma(sp, cs_raw[:], cc_in[:, :, :], w=[cs_raw.d])
            kb.op(act, lambda: nc.scalar.activation(out=cs[:], in_=cs_raw[:], func=AF.Silu), r=[cs_raw.d], w=[cs.d])
            wm = [T(kb, es, f"wm{i}", [128, KD, 512], F32) for i in range(3)]
            pm = [T(kb, es, f"pm{i}", [128, 512], F32, psum=True) for i in range(2)]
            mo = [T(kb, es, f"mo{i}", [2, 512], F32) for i in range(2)]
            bm = [T(kb, es, f"bm{i}", [2, 6 * D], F32) for i in range(2)]
            it = 0
            for l in range(DEPTH):
                bml = bm[l % 2]
                for j in range(2):
                    kb.dma(sp, bml[j:j + 1, :], b_mod[l:l + 1, :], w=[bml.d])
                wv = w_mod[l].rearrange("(k p) n -> p k n", p=128)
                for n in range(12):
                    wt = wm[it % 3]
                    kb.dma(sp, wt[:], wv[:, :, n * 512:(n + 1) * 512], w=[wt.d])
                    pt = pm[it % 2]
                    for k in range(KD):
                        kb.op(pe, lambda k=k: nc.tensor.matmul(pt[0:2, :], lhsT=cs[:, k, :], rhs=wt[:, k, :],
                                                               start=(k == 0), stop=(k == KD - 1)),
                              r=[cs.d, wt.d], w=[pt.d], sig=(k == KD - 1))
                    mt = mo[it % 2]
                    kb.op(dve, lambda: nc.vector.tensor_tensor(out=mt[:], in0=pt[0:2, :],
                                                               in1=bml[:, n * 512:(n + 1) * 512], op=ALU.add),
                          r=[pt.d, bml.d], w=[mt.d])
                    kb.dma(sp, modv[l, :, n * 512:(n + 1) * 512], mt[:], r=[mt.d], dw=[d_modv])
                    it += 1
            kb.end_phase()

        def load_mod_tile(tile, l, j, idx, tmp=None):
            kb.dma(sp, tile[:], bc_row(modv[l, j, idx * D:(idx + 1) * D]), w=[tile.d], dr=[d_modv])

        def make_scale_tile(tile, gt, l, j, idx):
            load_mod_tile(tile, l, j, idx)
            kb.op(dve, lambda: nc.vector.scalar_tensor_tensor(out=tile[:], in0=tile[:], scalar=1.0, in1=gt[:],
                                                             op0=ALU.add, op1=ALU.mult),
                  r=[tile.d, gt.d], w=[tile.d])

        for l in range(DEPTH):
            last = (l == DEPTH - 1)
            if l > 0:
                kb.new_epoch()
            with ExitStack() as es:
                NB = 2
                win = T(kb, es, "win", [128, KD, IN_W], BF16)
                wv = w_in[l].rearrange("(k p) n -> p k n", p=128)
                for k in range(KD):
                    kb.dma(pool, win[:, k, :], wv[:, k, :], w=[win.d])
                wuq_f = T(kb, es, "wuq_f", [128, 2, 576], F32)
                wuq = T(kb, es, "wuq", [128, 2, 576], BF16)
                wukv_f = T(kb, es, "wukv_f", [128, 768], F32)
                wukv = T(kb, es, "wukv", [128, 768], BF16)
                gcq = T(kb, es, "gcq", [128, 2], F32)
                gckv = T(kb, es, "gckv", [128, 1], F32)
                kb.dma(sp, wuq_f[:], w_uq[l].rearrange("(k p) n -> p k n", p=128), w=[wuq_f.d])
                kb.dma(sp, wukv_f[:], w_ukv[l], w=[wukv_f.d])
                for k in range(2):
                    kb.dma(sp, gcq[:, k:k + 1], g_cq[l, k * 128:(k + 1) * 128].rearrange("(p o) -> p o", o=1),
                           w=[gcq.d])
                kb.dma(sp, gckv[:, 0:1], g_ckv[l].rearrange("(p o) -> p o", o=1), w=[gckv.d])
                for k in range(2):
                    kb.op(dve, lambda k=k: nc.vector.tensor_scalar(out=wuq[:, k, :], in0=wuq_f[:, k, :],
                                                                   scalar1=gcq[:, k:k + 1], scalar2=None,
                                                                   op0=ALU.mult),
                          r=[wuq_f.d, gcq.d], w=[wuq.d])
                kb.op(dve, lambda: nc.vector.tensor_scalar(out=wukv[:], in0=wukv_f[:], scalar1=gckv[:, 0:1],
                                                           scalar2=None, op0=ALU.mult),
                      r=[wukv_f.d, gckv.d], w=[wukv.d])
                GQK = T(kb, es, "GQK", [128, 8, 64], F32)
                for h in range(8):
                    src = g_q_gqa[l] if h < 6 else g_k_gqa[l]
                    kb.dma(sp, GQK[:, h, :], bc_row(src), w=[GQK.d])
                GQM = T(kb, es, "GQM", [128, 6, 96], F32)
                GKM = T(kb, es, "GKM", [128, 6, 96], F32)
                for h in range(6):
                    kb.dma(sp, GQM[:, h, :], bc_row(g_q_mla[l]), w=[GQM.d])
                    kb.dma(sp, GKM[:, h, :], bc_row(g_k_mla[l]), w=[GKM.d])
                cosg = T(kb, es, "cosg", [128, NT, 32], F32)
                sing = T(kb, es, "sing", [128, NT, 32], F32)
                cosm = T(kb, es, "cosm", [128, NT, 16], F32)
                sinm = T(kb, es, "sinm", [128, NT, 16], F32)
                kb.dma(sp, cosg[:], cosg_in[:, :, :], w=[cosg.d])
                kb.dma(sp, sing[:], sing_in[:, :, :], w=[sing.d])
                kb.dma(sp, cosm[:], cosm_in[:, :, :], w=[cosm.d])
                kb.dma(sp, sinm[:], sinm_in[:, :, :], w=[sinm.d])
                g1t = T(kb, es, "g1t", [128, D], F32)
                kb.dma(sp, g1t[:], bc_row(g_norm1[l]), w=[g1t.d])
                A1 = T(kb, es, "A1", [128, D], F32)
                B1 = T(kb, es, "B1", [128, D], F32)

                xblk = [T(kb, es, f"xblk{i}", [128, NB, D], F32) for i in range(2)]
                junk = [T(kb, es, f"junkA{i}", [128, D], BF16) for i in range(2)]
                ssq1 = T(kb, es, "ssq1", [128, NB], F32)
                rstd1 = T(kb, es, "rstd1", [128, NB], F32)
                t1 = T(kb, es, "t1", [128, NB, D], F32)
                hb = T(kb, es, "hb", [128, NB, D], BF16)
                hT = T(kb, es, "hT", [128, NB, KD, 128], BF16)
                u_sbs = [T(kb, es, f"u_sb{i}", [128, NB, IN_W], F32) for i in range(2)]
                sq_g = T(kb, es, "sq_g", [128, NB, 512], F32)
                sq = T(kb, es, "sq", [128, NB, 576], F32)
                ssq8 = T(kb, es, "ssq8", [128, NB * 8], F32)
                rstd8 = T(kb, es, "rstd8", [128, NB * 8], F32)
                tq = T(kb, es, "tq", [128, NB, 8, 64], F32)
                tq2 = T(kb, es, "tq2", [128, NB, 8, 64], F32)
                tq3 = T(kb, es, "tq3", [128, NB, 8, 64], F32)
                qk_bf = T(kb, es, "qk_bf", [128, NB, 8, 64], BF16)
                ssqc = T(kb, es, "ssqc", [128, NB * 3], F32)
                ssqcq = T(kb, es, "ssqcq", [128, NB], F32)
                ssqckv = T(kb, es, "ssqckv", [128, NB], F32)
                rstdcq = T(kb, es, "rstdcq", [128, NB], F32)
                rstdckv = T(kb, es, "rstdckv", [128, NB], F32)
                cb = T(kb, es, "cb", [128, NB, 384], BF16)
                cT = T(kb, es, "cT", [128, NB, 3, 128], BF16)
                qm_sb = T(kb, es, "qm_sb", [128, NB, 6, 96], F32)
                kv_sb = T(kb, es, "kv_sb", [128, NB, 6, 128], F32)
                kcat = T(kb, es, "kcat", [128, NB, 6, 96], F32)
                ssq6 = T(kb, es, "ssq6", [128, NB * 6], F32)
                rstd6 = T(kb, es, "rstd6", [128, NB * 6], F32)
                qg = T(kb, es, "qg", [128, NB, 6, 96], F32)
                rt = [T(kb, es, f"rt{i}", [128, NB, 6, 16], F32) for i in range(4)]
                qm_bf = T(kb, es, "qm_bf", [128, NB, 6, 96], BF16)
                km_bf = T(kb, es, "km_bf", [128, NB, 6, 96], BF16)
                vg_blk = T(kb, es, "vg_blk", [128, NB, 128], BF16)
                vm_blk = T(kb, es, "vm_blk", [128, NB, 384], BF16)
                qkT_blk = T(kb, es, "qkT_blk", [64, 8, NB * 128], BF16)
                qmT_blk = T(kb, es, "qmT_blk", [96, 6, NB * 128], BF16)
                kmT_blk = T(kb, es, "kmT_blk", [96, 6, NB * 128], BF16)
                pTa = T(kb, es, "pTa", [128, 1024], BF16, psum=True)
                pu = T(kb, es, "pu", [128, 3, 512], F32, psum=True)
                pTg = T(kb, es, "pTg", [128, 1024], BF16, psum=True)
                pkv = T(kb, es, "pkv", [128, 2, 512], F32, psum=True)
                pTm = T(kb, es, "pTm", [128, 1024], BF16, psum=True)

                nblk = NT // NB
                cur_stream = [None]
                jk = [0]

                def load_x(b):
                    xb_ = xblk[b % 2]
                    kb.dma(sp, xb_[:], x_src(l, b * NB, NB), w=[xb_.d], dr=[d_xs])

                def sg1(b):
                    t0 = b * NB
                    j = 1 if t0 < NCT else 0
                    if cur_stream[0] != j:
                        make_scale_tile(A1, g1t, l, j, 1)
                        load_mod_tile(B1, l, j, 0)
                        cur_stream[0] = j
                    if b + 1 < nblk:
                        load_x(b + 1)
                    xb_ = xblk[b % 2]
                    u_sb = u_sbs[b % 2]
                    for i in range(NB):
                        jk[0] += 1
                        jt = junk[jk[0] % 2]
                        kb.op(act, lambda i=i: nc.scalar.activation(out=jt[:], in_=xb_[:, i, :], func=AF.Square,
                                                                    accum_out=ssq1[:, i:i + 1]),
                              r=[xb_.d], w=[jt.d, ssq1.d])
                        yield
                    rstd_chain(ssq1, rstd1, NB, 1.0 / D)
                    yield
                    for i in range(NB):
                        kb.op(dve, lambda i=i: nc.vector.scalar_tensor_tensor(
                            out=t1[:, i, :], in0=xb_[:, i, :], scalar=rstd1[:, i:i + 1], in1=A1[:],
                            op0=ALU.mult, op1=ALU.mult), r=[xb_.d, rstd1.d, A1.d], w=[t1.d])
                        yield
                    for i in range(NB):
                        kb.op(pool, lambda i=i: nc.gpsimd.tensor_tensor(out=hb[:, i, :], in0=t1[:, i, :], in1=B1[:],
                                                                        op=ALU.add), r=[t1.d, B1.d], w=[hb.d])
                        yield
                    for i in range(NB):
                        for k in range(KD):
                            kb.op(pe, lambda i=i, k=k: nc.tensor.transpose(pTa[:, k * 128:(k + 1) * 128],
                                                                           hb[:, i, k * 128:(k + 1) * 128], ident[:]),
                                  r=[hb.d, ident.d], w=[pTa.d], sig=(k == KD - 1))
                        yield
                        kb.op(act, lambda i=i: nc.scalar.copy(out=hT[:, i, :, :].rearrange("p k t -> p (k t)"),
                                                              in_=pTa[:, :]), r=[pTa.d], w=[hT.d])
                        yield
                    for i in range(NB):
                        for c, (c0, cw) in enumerate(((0, 512), (512, 512), (1024, 288))):
                            for k in range(KD):
                                kb.op(pe, lambda i=i, c=c, c0=c0, cw=cw, k=k: nc.tensor.matmul(
                                    pu[:, c, 0:cw], lhsT=hT[:, i, k, :], rhs=win[:, k, c0:c0 + cw],
                                    start=(k == 0), stop=(k == KD - 1)),
                                    r=[hT.d, win.d], w=[pu.d], sig=(k == KD - 1))
                            yield
                        kb.op(act, lambda i=i: nc.scalar.copy(out=u_sb[:, i, 0:1024].rearrange("p (c n) -> p c n", c=2),
                                                              in_=pu[:, 0:2, :]), r=[pu.d], w=[u_sb.d])
                        yield
                        kb.op(act, lambda i=i: nc.scalar.copy(out=u_sb[:, i, 1024:IN_W], in_=pu[:, 2, 0:288]),
                              r=[pu.d], w=[u_sb.d])
                        yield

                def sg2a(b):
                    t0 = b * NB
                    u_sb = u_sbs[b % 2]
                    tsl = slice(t0 * 128, (t0 + NB) * 128)
                    uqk = u_sb[:, :, 0:512]
                    kb.op(dve, lambda: nc.vector.tensor_tensor(out=sq_g[:], in0=uqk, in1=uqk, op=ALU.mult),
                          r=[u_sb.d], w=[sq_g.d])
                    yield
                    kb.op(dve, lambda: nc.vector.tensor_reduce(
                        out=ssq8[:, :].rearrange("p (n h) -> p n h", h=8),
                        in_=sq_g[:].rearrange("p n (h d) -> p n h d", d=64),
                        axis=AX.X, op=ALU.add), r=[sq_g.d], w=[ssq8.d])
                    yield
                    rstd_chain(ssq8, rstd8, NB * 8, 1.0 / 64)
                    yield
                    kb.op(dve, lambda: nc.vector.tensor_tensor(
                        out=tq[:],
                        in0=uqk.rearrange("p n (h d) -> p n h d", d=64),
                        in1=rstd8[:, :].rearrange("p (n h) -> p n h", h=8).unsqueeze(3).to_broadcast([128, NB, 8, 64]),
                        op=ALU.mult),
                        r=[u_sb.d, rstd8.d], w=[tq.d])
                    yield
                    for i in range(NB):
                        kb.op(pool, lambda i=i: nc.gpsimd.tensor_tensor(out=tq[:, i, :, :], in0=tq[:, i, :, :],
                                                                        in1=GQK[:], op=ALU.mult),
                              r=[tq.d, GQK.d], w=[tq.d])
                        yield
                    for i in range(NB):
                        cg = cosg[:, t0 + i, :].unsqueeze(1).to_broadcast([128, 8, 32])
                        sg = sing[:, t0 + i, :].unsqueeze(1).to_broadcast([128, 8, 32])
                        lo = tq[:, i, :, 0:32]
                        hi = tq[:, i, :, 32:64]
                        kb.op(dve, lambda: nc.vector.tensor_tensor(out=tq2[:, i, :, 0:32], in0=lo, in1=cg, op=ALU.mult),
                              r=[tq.d, cosg.d], w=[tq2.d])
                        kb.op(pool, lambda: nc.gpsimd.tensor_tensor(out=tq3[:, i, :, 0:32], in0=hi, in1=sg, op=ALU.mult),
                              r=[tq.d, sing.d], w=[tq3.d])
                        yield
                        kb.op(dve, lambda: nc.vector.tensor_tensor(out=tq2[:, i, :, 32:64], in0=hi, in1=cg, op=ALU.mult),
                              r=[tq.d, cosg.d], w=[tq2.d])
                        kb.op(pool, lambda: nc.gpsimd.tensor_tensor(out=tq3[:, i, :, 32:64], in0=lo, in1=sg, op=ALU.mult),
                              r=[tq.d, sing.d], w=[tq3.d])
                        yield
                        kb.op(dve, lambda: nc.vector.tensor_tensor(out=qk_bf[:, i, :, 0:32], in0=tq2[:, i, :, 0:32],
                                                                   in1=tq3[:, i, :, 0:32], op=ALU.subtract),
                              r=[tq2.d, tq3.d], w=[qk_bf.d])
                        yield
                        kb.op(dve, lambda: nc.vector.tensor_tensor(out=qk_bf[:, i, :, 32:64], in0=tq2[:, i, :, 32:64],
                                                                   in1=tq3[:, i, :, 32:64], op=ALU.add),
                              r=[tq2.d, tq3.d], w=[qk_bf.d])
                        yield
                    kb.op(act, lambda: nc.scalar.copy(out=vg_blk[:], in_=u_sb[:, :, 512:640]),
                          r=[u_sb.d], w=[vg_blk.d])
                    yield
                    kb.dma(sp, Vs[tsl, 0:128].rearrange("(i p) c -> p i c", p=128), vg_blk[:], r=[vg_blk.d], dw=[d_v])
                    kb.dma(sp, pin[tsl, :].rearrange("(i p) c -> p i c", p=128), u_sb[:, :, 1056:IN_W],
                           r=[u_sb.d], dw=[d_pin])
                    for i in range(NB):
                        for h in range(8):
                            kb.op(pe, lambda i=i, h=h: nc.tensor.transpose(pTg[0:64, h * 128:(h + 1) * 128],
                                                                           qk_bf[:, i, h, :], ident[:]),
                                  r=[qk_bf.d, ident.d], w=[pTg.d], sig=(h == 7))
                        yield
                        kb.op(act, lambda i=i: nc.scalar.copy(out=qkT_blk[:, :, i * 128:(i + 1) * 128],
                                                              in_=pTg[0:64, :].rearrange("p (h t) -> p h t", t=128)),
                              r=[pTg.d], w=[qkT_blk.d])
                        yield
                    kb.dma(sp, QTg[:, :, tsl].rearrange("h d t -> d h t"), qkT_blk[:, 0:6, :], r=[qkT_blk.d], dw=[d_q])
                    kb.dma(sp, KTg[:, :, tsl].rearrange("h d t -> d h t"), qkT_blk[:, 6:8, :], r=[qkT_blk.d], dw=[d_q])
                    yield

                def mla_norm_rope(src, G, dst, t0):
                    s3 = src[:].rearrange("p n h d -> p (n h) d")
                    kb.op(dve, lambda: nc.vector.tensor_tensor(
                        out=sq[:].rearrange("p n (h d) -> p (n h) d", d=96), in0=s3, in1=s3, op=ALU.mult),
                        r=[src.d], w=[sq.d])
                    yield
                    kb.op(dve, lambda: nc.vector.tensor_reduce(
                        out=ssq6[:, :], in_=sq[:].rearrange("p n (h d) -> p (n h) d", d=96),
                        axis=AX.X, op=ALU.add), r=[sq.d], w=[ssq6.d])
                    yield
                    rstd_chain(ssq6, rstd6, NB * 6, 1.0 / 96)
                    yield
                    kb.op(dve, lambda: nc.vector.tensor_tensor(
                        out=qg[:].rearrange("p n h d -> p (n h) d"), in0=s3,
                        in1=rstd6[:, :].unsqueeze(2).to_broadcast([128, NB * 6, 96]), op=ALU.mult),
                        r=[src.d, rstd6.d], w=[qg.d])
                    yield
                    for i in range(NB):
                        kb.op(pool, lambda i=i: nc.gpsimd.tensor_tensor(out=qg[:, i, :, :], in0=qg[:, i, :, :],
                                                                        in1=G[:], op=ALU.mult),
                              r=[qg.d, G.d], w=[qg.d])
                        yield
                    kb.op(act, lambda: nc.scalar.copy(out=dst[:, :, :, 0:64], in_=qg[:, :, :, 0:64]),
                          r=[qg.d], w=[dst.d])
                    yield
                    for i in range(NB):
                        cm_ = cosm[:, t0 + i, :].unsqueeze(1).to_broadcast([128, 6, 16])
                        sm_ = sinm[:, t0 + i, :].unsqueeze(1).to_broadcast([128, 6, 16])
                        lo = qg[:, i, :, 64:80]
                        hi = qg[:, i, :, 80:96]
                        kb.op(dve, lambda: nc.vector.tensor_tensor(out=rt[0][:, i, :, :], in0=lo, in1=cm_, op=ALU.mult),
                              r=[qg.d, cosm.d], w=[rt[0].d])
                        kb.op(pool, lambda: nc.gpsimd.tensor_tensor(out=rt[2][:, i, :, :], in0=lo, in1=sm_, op=ALU.mult),
                              r=[qg.d, sinm.d], w=[rt[2].d])
                        yield
                        kb.op(dve, lambda: nc.vector.tensor_tensor(out=rt[1][:, i, :, :], in0=hi, in1=sm_, op=ALU.mult),
                              r=[qg.d, sinm.d], w=[rt[1].d])
                        kb.op(pool, lambda: nc.gpsimd.tensor_tensor(out=rt[3][:, i, :, :], in0=hi, in1=cm_, op=ALU.mult),
                              r=[qg.d, cosm.d], w=[rt[3].d])
                        yield
                    kb.op(dve, lambda: nc.vector.tensor_tensor(out=dst[:, :, :, 64:80], in0=rt[0][:], in1=rt[1][:],
                                                               op=ALU.subtract), r=[rt[0].d, rt[1].d], w=[dst.d])
                    yield
                    kb.op(dve, lambda: nc.vector.tensor_tensor(out=dst[:, :, :, 80:96], in0=rt[2][:], in1=rt[3][:],
                                                               op=ALU.add), r=[rt[2].d, rt[3].d], w=[dst.d])
                    yield

                def sg2b(b):
                    t0 = b * NB
                    u_sb = u_sbs[b % 2]
                    tsl = slice(t0 * 128, (t0 + NB) * 128)
                    ucc = u_sb[:, :, 640:1024]
                    kb.op(dve, lambda: nc.vector.tensor_tensor(out=sq[:, :, 0:384], in0=ucc, in1=ucc, op=ALU.mult),
                          r=[u_sb.d], w=[sq.d])
                    yield
                    kb.op(dve, lambda: nc.vector.tensor_reduce(
                        out=ssqc[:, :].rearrange("p (n c) -> p n c", c=3),
                        in_=sq[:, :, 0:384].rearrange("p n (c d) -> p n c d", d=128),
                        axis=AX.X, op=ALU.add), r=[sq.d], w=[ssqc.d])
                    yield
                    sv = ssqc[:, :].rearrange("p (n c) -> p n c", c=3)
                    kb.op(dve, lambda: nc.vector.tensor_tensor(out=ssqcq[:, :], in0=sv[:, :, 0], in1=sv[:, :, 1],
                                                               op=ALU.add), r=[ssqc.d], w=[ssqcq.d])
                    kb.op(dve, lambda: nc.vector.tensor_copy(out=ssqckv[:, :], in_=sv[:, :, 2]),
                          r=[ssqc.d], w=[ssqckv.d])
                    yield
                    rstd_chain(ssqcq, rstdcq, NB, 1.0 / 256)
                    rstd_chain(ssqckv, rstdckv, NB, 1.0 / 128)
                    yield
                    kb.op(pool, lambda: nc.gpsimd.tensor_copy(out=cb[:], in_=ucc), r=[u_sb.d], w=[cb.d])
                    yield
                    for i in range(NB):
                        for k in range(3):
                            kb.op(pe, lambda i=i, k=k: nc.tensor.transpose(pTm[:, k * 128:(k + 1) * 128],
                                                                           cb[:, i, k * 128:(k + 1) * 128], ident[:]),
                                  r=[cb.d, ident.d], w=[pTm.d], sig=(k == 2))
                        yield
                        kb.op(act, lambda i=i: nc.scalar.copy(out=cT[:, i, :, :].rearrange("p k t -> p (k t)"),
                                                              in_=pTm[:, 0:384]), r=[pTm.d], w=[cT.d])
                        yield
                    for i in range(NB):
                        for (c, c0, cw) in ((0, 0, 512), (1, 512, 64)):
                            for k in range(2):
                                kb.op(pe, lambda i=i, c=c, c0=c0, cw=cw, k=k: nc.tensor.matmul(
                                    pkv[:, c, 0:cw], lhsT=cT[:, i, k, :], rhs=wuq[:, k, c0:c0 + cw],
                                    start=(k == 0), stop=(k == 1)), r=[cT.d, wuq.d], w=[pkv.d], sig=(k == 1))
                        yield
                        qmv = qm_sb[:, i, :, :].rearrange("p h d -> p (h d)")
                        kb.op(act, lambda i=i: nc.scalar.activation(out=qmv[:, 0:512], in_=pkv[:, 0, :],
                                                                    func=AF.Identity, scale=rstdcq[:, i:i + 1]),
                              r=[pkv.d, rstdcq.d], w=[qm_sb.d])
                        kb.op(act, lambda i=i: nc.scalar.activation(out=qmv[:, 512:576], in_=pkv[:, 1, 0:64],
                                                                    func=AF.Identity, scale=rstdcq[:, i:i + 1]),
                              r=[pkv.d, rstdcq.d], w=[qm_sb.d])
                        yield
                        for (c, c0, cw) in ((0, 0, 512), (1, 512, 256)):
                            kb.op(pe, lambda i=i, c=c, c0=c0, cw=cw: nc.tensor.matmul(
                                pkv[:, c, 0:cw], lhsT=cT[:, i, 2, :], rhs=wukv[:, c0:c0 + cw],
                                start=True, stop=True), r=[cT.d, wukv.d], w=[pkv.d])
                        yield
                        kvv = kv_sb[:, i, :, :].rearrange("p h d -> p (h d)")
                        kb.op(act, lambda i=i: nc.scalar.activation(out=kvv[:, 0:512], in_=pkv[:, 0, :],
                                                                    func=AF.Identity, scale=rstdckv[:, i:i + 1]),
                              r=[pkv.d, rstdckv.d], w=[kv_sb.d])
                        kb.op(act, lambda i=i: nc.scalar.activation(out=kvv[:, 512:768], in_=pkv[:, 1, 0:256],
                                                                    func=AF.Identity, scale=rstdckv[:, i:i + 1]),
                              r=[pkv.d, rstdckv.d], w=[kv_sb.d])
                        yield
                    kb.op(pool, lambda: nc.gpsimd.tensor_copy(out=kcat[:, :, :, 0:64], in_=kv_sb[:, :, :, 0:64]),
                          r=[kv_sb.d], w=[kcat.d])
                    yield
                    kb.op(pool, lambda: nc.gpsimd.tensor_copy(
                        out=kcat[:, :, :, 64:96],
                        in_=u_sb[:, :, 1024:1056].unsqueeze(2).to_broadcast([128, NB, 6, 32])),
                        r=[u_sb.d], w=[kcat.d])
                    yield
                    kb.op(act, lambda: nc.scalar.copy(out=vm_blk[:].rearrange("p n (h d) -> p n h d", d=64),
                                                      in_=kv_sb[:, :, :, 64:128]), r=[kv_sb.d], w=[vm_blk.d])
                    yield
                    kb.dma(sp, Vs[tsl, 128:512].rearrange("(i p) c -> p i c", p=128), vm_blk[:], r=[vm_blk.d], dw=[d_v])
                    yield from mla_norm_rope(qm_sb, GQM, qm_bf, t0)
                    yield from mla_norm_rope(kcat, GKM, km_bf, t0)
                    for (srcb, dstb, dram) in ((qm_bf, qmT_blk, QTm), (km_bf, kmT_blk, KTm)):
                        for i in range(NB):
                            for h in range(6):
                                kb.op(pe, lambda i=i, h=h: nc.tensor.transpose(pTm[0:96, h * 128:(h + 1) * 128],
                                                                               srcb[:, i, h, :], ident[:]),
                                      r=[srcb.d, ident.d], w=[pTm.d], sig=(h == 5))
                            yield
                            kb.op(act, lambda i=i: nc.scalar.copy(
                                out=dstb[:, :, i * 128:(i + 1) * 128],
                                in_=pTm[0:96, 0:768].rearrange("p (h t) -> p h t", t=128)), r=[pTm.d], w=[dstb.d])
                            yield
                        kb.dma(sp, dram[:, :, tsl].rearrange("h d t -> d h t"), dstb[:], r=[dstb.d], dw=[d_q])
                        yield

                def run_interleaved(gens, weights):
                    gens = [g for g in gens if g is not None]
                    if not OPT_ILV_A:
                        for g in gens:
                            for _ in g:
                                pass
                        return
                    alive = list(range(len(gens)))
                    while alive:
                        for gi in list(alive):
                            for _ in range(weights[gi]):
                                try:
                                    next(gens[gi])
                                except StopIteration:
                                    alive.remove(gi)
                                    break

                load_x(0)
                for b in range(nblk + 1):
                    g1 = sg1(b) if b < nblk else iter(())
                    g2 = sg2a(b - 1) if b >= 1 else iter(())
                    g3 = sg2b(b - 1) if b >= 1 else iter(())
                    run_interleaved([g1, g2, g3], [1, 1, 3])
                kb.end_phase()

            with ExitStack() as es:
                bands = T(kb, es, "bands", [128, 5, 4, 128], F32)
                kb.dma(sp, bands[:], bands_in[:, :, :, :], w=[bands.d])
                wpool = T(kb, es, "wpool", [64, 4, 64], BF16)
                kb.dma(pool, wpool[:], w_pool[l].rearrange("g c d -> c g d"), w=[wpool.d])
                lsp = T(kb, es, "lsp", [64, 4], F32)
                for g in range(4):
                    kb.dma(sp, lsp[:, g:g + 1], ls_pool[l, g * 64:(g + 1) * 64].rearrange("(p o) -> p o", o=1),
                           w=[lsp.d])
                pw = [T(kb, es, f"pw{i}", [128, 3, 256], F32) for i in range(3)]
                py = [T(kb, es, f"py{i}", [64, 512], F32, psum=True) for i in range(2)]
                po = [T(kb, es, f"po{i}", [64, 512], F32, psum=True) for i in range(2)]
                yT = [T(kb, es, f"yT{i}", [64, 4, 128], BF16) for i in range(2)]
                oT = [T(kb, es, f"oT{i}", [64, 4, 128], BF16) for i in range(2)]
                seqs = [(NCT, NT)] if last else [(0, NCT), (NCT, NT)]

                def pool_tile(tt, s0, s1, it):
                    first, lastt = (tt == s0), (tt == s1 - 1)
                    j0 = 1 if first else 0
                    j1 = 2 if lastt else 3
                    pw_ = pw[it % 3]
                    kb.dma(sp, pw_[:, j0:j1, :],
                           pin[(tt - 1 + j0) * 128:(tt - 1 + j1) * 128, :].rearrange("(j p) c -> p j c", p=128),
                           w=[pw_.d], dr=[d_pin])
                    py_, po_, yT_, oT_ = py[it % 2], po[it % 2], yT[it % 2], oT[it % 2]
                    for g in range(4):
                        js = list(range(j0, j1))
                        for jj in js:
                            var = jj if jj != 1 else (3 if first else (4 if lastt else 1))
                            kb.op(pe, lambda g=g, jj=jj, var=var: nc.tensor.matmul(
                                py_[:, g * 128:(g + 1) * 128], lhsT=pw_[:, jj, g * 64:(g + 1) * 64],
                                rhs=bands[:, var, g, :], start=(jj == js[0]), stop=(jj == js[-1])),
                                r=[pw_.d, bands.d], w=[py_.d], sig=(jj == js[-1]))
                        yield
                    kb.op(act, lambda: nc.scalar.copy(out=yT_[:].rearrange("p g t -> p (g t)"), in_=py_[:, :]),
                          r=[py_.d], w=[yT_.d])
                    yield
                    for g in range(4):
                        kb.op(pe, lambda g=g: nc.tensor.matmul(po_[:, g * 128:(g + 1) * 128], lhsT=wpool[:, g, :],
                                                               rhs=yT_[:, g, :], start=True, stop=True),
                              r=[wpool.d, yT_.d], w=[po_.d], sig=(g == 3))
                    yield
                    kb.op(dve, lambda: nc.vector.tensor_tensor(
                        out=oT_[:], in0=po_[:, :].rearrange("p (g t) -> p g t", t=128),
                        in1=lsp[:, :].unsqueeze(2).to_broadcast([64, 4, 128]), op=ALU.mult),
                        r=[po_.d, lsp.d], w=[oT_.d])
                    yield
                    kb.dma(sp, mixT[768:1024, tt * 128:(tt + 1) * 128].rearrange("(g d) t -> d g t", d=64), oT_[:],
                           r=[oT_.d], dw=[d_mix])
                    yield

                tiles = [(tt, s0, s1) for (s0, s1) in seqs for tt in range(s0, s1)]
                gens = [pool_tile(tt, s0, s1, it) for it, (tt, s0, s1) in enumerate(tiles)]
                active = []
                gi = 0
                while gi < len(gens) or active:
                    if gi < len(gens) and len(active) < (2 if OPT_ILV_A2 else 1):
                        active.append(gens[gi])
                        gi += 1
                    for g_ in list(active):
                        try:
                            next(g_)
                        except StopIteration:
                            active.remove(g_)
                kb.end_phase()

            with ExitStack() as es:
                KS = 3
                S = [T(kb, es, f"S{i}", [128, KS * 512], F32, psum=True) for i in range(2)]
                O = [T(kb, es, f"O{i}", [128, 512], F32, psum=True) for i in range(2)]
                kT = [T(kb, es, f"kT{i}", [128, TT], BF16) for i in range(2)]
                vA = [T(kb, es, f"vA{i}", [128, NT, 128], BF16) for i in range(2)]
                qT = [T(kb, es, f"qT{i}", [128, TT], BF16) for i in range(2)]
                P = [T(kb, es, f"P{i}", [128, KS * 512], BF16) for i in range(3)]
                for z_ in kT + qT:
                    kb.op(pool, lambda z_=z_: nc.gpsimd.memset(z_[64:128, :], 0.0), w=[z_.d])
                rec = [T(kb, es, f"rec{i}", [128, 512], F32) for i in range(2)]
                ob = [T(kb, es, f"ob{i}", [64, 512], BF16) for i in range(3)]
                for v_ in vA:
                    kb.op(pool, lambda v_=v_: nc.gpsimd.memset(v_[:, :, 64:128], 1.0), w=[v_.d])
                groups = []
                for g in range(2):
                    groups.append((64, KTg[g], g * 64, [(QTg[3 * g + i], (3 * g + i) * 64) for i in range(3)]))
                for h in range(6):
                    groups.append((96, KTm[h], 128 + h * 64, [(QTm[h], 384 + h * 64)]))
                heads = []
                for gi, (dk, ksrc, vcol, qs) in enumerate(groups):
                    for (qsrc, row0) in qs:
                        heads.append((gi, dk, ksrc, vcol, qsrc, row0))

                loaded_g = {}

                def load_group(gi):
                    dk, ksrc, vcol, qs = groups[gi]
                    kt_, va_ = kT[gi % 2], vA[gi % 2]
                    kb.dma(sp, kt_[0:dk, :], ksrc[:, :], w=[kt_.d], dr=[d_q])
                    kb.dma(sp, va_[:, :, 0:64], Vs[:, vcol:vcol + 64].rearrange("(i p) c -> p i c", p=128),
                           w=[va_.d], dr=[d_v])

                def load_q(hi):
                    gi, dk, ksrc, vcol, qsrc, row0 = heads[hi]
                    kb.dma(sp, qT[hi % 2][0:dk, :], qsrc[:, :], w=[qT[hi % 2].d], dr=[d_q])

                steps = []
                for hi, (gi, dk, ksrc, vcol, qsrc, row0) in enumerate(heads):
                    blocks = []
                    if not last:
                        blocks.append((0, C, 0, NCT))
                    for qb in range(L // 512):
                        blocks.append((C + qb * 512, 512, 0, NT))
                    for (q0, nq, k0, nk) in blocks:
                        ks = KS if nk % KS == 0 else 2
                        assert nk % ks == 0
                        nst = nk // ks
                        for j in range(nst):
                            steps.append(dict(hi=hi, gi=gi, dk=dk, row0=row0, q0=q0, nq=nq, kt=k0 + ks * j, ks=ks,
                                              first=(j == 0), last=(j == nst - 1)))
                blk_id = -1
                for st in steps:
                    if st["first"]:
                        blk_id += 1
                    st["blk"] = blk_id

                def emit_qk(idx):
                    st = steps[idx]
                    s_ = S[idx % 2]
                    kt_, qt_ = kT[st["gi"] % 2], qT[st["hi"] % 2]
                    dk, nq, q0 = st["dk"], st["nq"], st["q0"]
                    for a in range(st["ks"]):
                        kti = st["kt"] + a
                        kb.op(pe, lambda a=a, kti=kti: nc.tensor.matmul(
                            s_[:, a * 512:a * 512 + nq], lhsT=kt_[:, kti * 128:(kti + 1) * 128],
                            rhs=qt_[:, q0:q0 + nq], start=True, stop=True),
                            r=[kt_.d, qt_.d], w=[s_.d], sig=(a == st["ks"] - 1))

                cur_g, cur_h = -1, -1
                load_group(0)
                load_q(0)
                ob_i = 0
                for idx, st in enumerate(steps):
                    if st["hi"] != cur_h:
                        cur_h = st["hi"]
                        if cur_h + 1 < len(heads):
                            nh = heads[cur_h + 1]
                            if nh[0] != st["gi"]:
                                load_group(nh[0])
                            load_q(cur_h + 1)
                    if idx == 0:
                        emit_qk(0)
                    if idx + 1 < len(steps):
                        emit_qk(idx + 1)
                    s_, p_ = S[idx % 2], P[idx % 3]
                    nq = st["nq"]
                    scale = float(st["dk"]) ** -0.5
                    va_ = vA[st["gi"] % 2]
                    o_ = O[st["blk"] % 2]
                    ks = st["ks"]
                    if nq == 512:
                        kb.op(act, lambda: nc.scalar.activation(out=p_[:, 0:ks * 512], in_=s_[:, 0:ks * 512],
                                                                func=AF.Exp, scale=scale),
                              r=[s_.d], w=[p_.d])
                    else:
                        kb.op(act, lambda: nc.scalar.activation(
                            out=p_[:, 0:ks * 512].rearrange("p (a n) -> p a n", a=ks)[:, :, 0:nq],
                            in_=s_[:, 0:ks * 512].rearrange("p (a n) -> p a n", a=ks)[:, :, 0:nq],
                            func=AF.Exp, scale=scale),
                            r=[s_.d], w=[p_.d])
                    for a in range(ks):
                        kti = st["kt"] + a
                        kb.op(pe, lambda a=a, kti=kti: nc.tensor.matmul(
                            o_[:, 0:nq], lhsT=va_[:, kti, :], rhs=p_[:, a * 512:a * 512 + nq],
                            start=(st["first"] and a == 0), stop=(st["last"] and a == ks - 1)),
                            r=[va_.d, p_.d], w=[o_.d], sig=(a == ks - 1))
                    if st["last"]:
                        rc = rec[st["blk"] % 2]
                        ob_ = ob[ob_i % 3]
                        ob_i += 1
                        kb.op(dve, lambda: nc.vector.reciprocal(out=rc[64:128, 0:nq], in_=o_[64:128, 0:nq]),
                              r=[o_.d], w=[rc.d])
                        kb.op(dve, lambda: nc.vector.tensor_tensor(out=ob_[:, 0:nq], in0=o_[0:64, 0:nq],
                                                                   in1=rc[64:128, 0:nq], op=ALU.mult),
                              r=[o_.d, rc.d], w=[ob_.d])
                        kb.dma(sp, mixT[st["row0"]:st["row0"] + 64, st["q0"]:st["q0"] + nq], ob_[:, 0:nq],
                               r=[ob_.d], dw=[d_mix])
                kb.end_phase()

            tile0 = NCT if last else 0
            with ExitStack() as es:
                NB = 2
                wout = T(kb, es, "wout", [128, KD, D], BF16)
                wv = w_out[l].rearrange("(k p) n -> p k n", p=128)
                for k in range(KD):
                    kb.dma(pool, wout[:, k, :], wv[:, k, :], w=[wout.d])
                g2t = T(kb, es, "g2t", [128, D], F32)
                kb.dma(sp, g2t[:], bc_row(g_norm2[l]), w=[g2t.d])
                GT1 = T(kb, es, "GT1", [128, D], F32)
                A2 = T(kb, es, "A2", [128, D], F32)
                B2 = T(kb, es, "B2", [128, D], F32)
                mT = [T(kb, es, f"mT{i}", [128, KD, NB * 128], BF16) for i in range(3)]
                xblk = [T(kb, es, f"xc{i}", [128, NB, D], F32) for i in range(3)]
                tmp = [T(kb, es, f"tmpc{i}", [128, D], F32) for i in range(2)]
                junk = [T(kb, es, f"junkC{i}", [128, D], BF16) for i in range(2)]
                ssq2s = [T(kb, es, f"ssq2{i}", [128, NB], F32) for i in range(2)]
                rstd2s = [T(kb, es, f"rstd2{i}", [128, NB], F32) for i in range(2)]
                hb2 = [T(kb, es, f"hb2{i}", [128, D], BF16) for i in range(2)]
                h2blk = [T(kb, es, f"h2blk{i}", [128, KD, NB * 128], BF16) for i in range(2)]
                pyo = [T(kb, es, f"pyo{i}", [128, 2, 512], F32, psum=True) for i in range(2)]
                pT = [T(kb, es, f"pTc{i}", [128, 1024], BF16, psum=True) for i in range(2)]
                blks = list(range(tile0 // NB, NT // NB))

                def load_c1(b):
                    t0 = b * NB
                    tsl = slice(t0 * 128, (t0 + NB) * 128)
                    kb.dma(sp, mT[b % 3][:], mixT[:, tsl].rearrange("(k p) t -> p k t", p=128), w=[mT[b % 3].d],
                           dr=[d_mix])
                    kb.dma(sp, xblk[b % 3][:], x_src(l, t0, NB), w=[xblk[b % 3].d], dr=[d_xs])

                cur_stream = [None]
                load_c1(blks[0])
                tic = [0]

                def c1_block(bi, b):
                    t0 = b * NB
                    j = 1 if t0 < NCT else 0
                    if cur_stream[0] != j:
                        load_mod_tile(GT1, l, j, 2)
                        make_scale_tile(A2, g2t, l, j, 4)
                        load_mod_tile(B2, l, j, 3)
                        cur_stream[0] = j
                    if bi + 1 < len(blks):
                        load_c1(blks[bi + 1])
                    m_, x_, h2_ = mT[b % 3], xblk[b % 3], h2blk[b % 2]
                    ssq2, rstd2 = ssq2s[b % 2], rstd2s[b % 2]
                    for i in range(NB):
                        ti = tic[0]
                        tic[0] += 1
                        py_ = pyo[ti % 2]
                        tm = tmp[ti % 2]
                        for hf in range(2):
                            for k in range(KD):
                                kb.op(pe, lambda i=i, hf=hf, k=k: nc.tensor.matmul(
                                    py_[:, hf, :], lhsT=m_[:, k, i * 128:(i + 1) * 128],
                                    rhs=wout[:, k, hf * 512:(hf + 1) * 512], start=(k == 0), stop=(k == KD - 1)),
                                    r=[m_.d, wout.d], w=[py_.d], sig=(k == KD - 1))
                            yield
                        kb.op(dve, lambda: nc.vector.tensor_tensor(out=tm[:], in0=py_[:].rearrange("p a n -> p (a n)"),
                                                                   in1=GT1[:], op=ALU.mult),
                              r=[py_.d, GT1.d], w=[tm.d])
                        yield
                        kb.op(pool, lambda i=i: nc.gpsimd.tensor_tensor(out=x_[:, i, :], in0=x_[:, i, :], in1=tm[:],
                                                                        op=ALU.add), r=[x_.d, tm.d], w=[x_.d])
                        yield
                        jt = junk[ti % 2]
                        kb.op(act, lambda i=i: nc.scalar.activation(out=jt[:], in_=x_[:, i, :], func=AF.Square,
                                                                    accum_out=ssq2[:, i:i + 1]),
                              r=[x_.d], w=[jt.d, ssq2.d])
                        yield
                    rstd_chain(ssq2, rstd2, NB, 1.0 / D)
                    yield
                    tsl = slice(t0 * 128, (t0 + NB) * 128)
                    kb.dma(sp, xs[tsl, :].rearrange("(i p) c -> p i c", p=128), x_[:], r=[x_.d], dw=[d_xs])
                    for i in range(NB):
                        ti = tic[0]
                        tic[0] += 1
                        tm = tmp[ti % 2]
                        hb_ = hb2[ti % 2]
                        p_ = pT[ti % 2]
                        kb.op(dve, lambda i=i: nc.vector.scalar_tensor_tensor(
                            out=tm[:], in0=x_[:, i, :], scalar=rstd2[:, i:i + 1], in1=A2[:],
                            op0=ALU.mult, op1=ALU.mult), r=[x_.d, rstd2.d, A2.d], w=[tm.d])
                        yield
                        kb.op(pool, lambda: nc.gpsimd.tensor_tensor(out=hb_[:], in0=tm[:], in1=B2[:], op=ALU.add),
                              r=[tm.d, B2.d], w=[hb_.d])
                        yield
                        for k in range(KD):
                            kb.op(pe, lambda k=k: nc.tensor.transpose(p_[:, k * 128:(k + 1) * 128],
                                                                      hb_[:, k * 128:(k + 1) * 128], ident[:]),
                                  r=[hb_.d, ident.d], w=[p_.d], sig=(k == KD - 1))
                        yield
                        kb.op(act, lambda i=i: nc.scalar.copy(out=h2_[:, :, i * 128:(i + 1) * 128],
                                                              in_=p_[:, :].rearrange("p (k t) -> p k t", t=128)),
                              r=[p_.d], w=[h2_.d])
                        yield
                    kb.dma(sp, h2T[:, tsl].rearrange("(k p) t -> p k t", p=128), h2_[:], r=[h2_.d], dw=[d_h2])
                    yield

                bi = 0
                if blks and blks[0] * NB < NCT:
                    for _ in c1_block(0, blks[0]):
                        pass
                    bi = 1
                active = []
                while bi < len(blks) or active:
                    if bi < len(blks) and len(active) < (2 if OPT_ILV_C1 else 1):
                        active.append(c1_block(bi, blks[bi]))
                        bi += 1
                    for g_ in list(active):
                        try:
                            next(g_)
                        except StopIteration:
                            active.remove(g_)
                kb.end_phase()

            with ExitStack() as es:
                wv1 = w_mlp1[l].rearrange("(k p) n -> p k n", p=128)
                wv2 = w_mlp2[l].rearrange("(k p) n -> p k n", p=128)
                w1c = [T(kb, es, f"w1c{i}", [128, KD, 512], BF16) for i in range(8)]
                w2c = [T(kb, es, f"w2c{i}", [128, 8, D], BF16) for i in range(4)]
                for c in range(8):
                    for kh in range(2):
                        kb.dma(pool, w1c[c][:, kh * 4:(kh + 1) * 4, :], wv1[:, kh * 4:(kh + 1) * 4, c * 512:(c + 1) * 512],
                               w=[w1c[c].d])
                for c in range(4):
                    for kh in range(4):
                        kb.dma(pool, w2c[c][:, kh * 2:(kh + 1) * 2, :], wv2[:, c * 8 + kh * 2:c * 8 + (kh + 1) * 2, :],
                               w=[w2c[c].d])
                GT2 = T(kb, es, "GT2", [128, D], F32)
                NBC = 4
                hblk = [T(kb, es, f"hblk{i}", [128, KD, NBC * 128], BF16) for i in range(2)]
                aT = T(kb, es, "aT", [128, 32, NBC * 128], BF16)
                rl = [T(kb, es, f"rl{i}", [128, NBC * 128], BF16) for i in range(2)]
                x1t = [T(kb, es, f"x1t{i}", [128, D], F32) for i in range(3)]
                tmp = [T(kb, es, f"tmpd{i}", [128, D], F32) for i in range(2)]
                pm1 = [T(kb, es, f"pm1{i}", [128, 512], F32, psum=True) for i in range(2)]
                py2 = [T(kb, es, f"py2{i}", [128, 2, 512], F32, psum=True) for i in range(2)]
                blocks = []
                if not last:
                    blocks.append((0, NCT))
                tcur = NCT
                while tcur < NT:
                    n = min(NBC, NT - tcur)
                    blocks.append((tcur, n))
                    tcur += n

                def load_c2(bi):
                    t0, n = blocks[bi]
                    kb.dma(sp, hblk[bi % 2][:, :, 0:n * 128],
                           h2T[:, t0 * 128:(t0 + n) * 128].rearrange("(k p) t -> p k t", p=128),
                           w=[hblk[bi % 2].d], dr=[d_h2])

                load_c2(0)
                cur_stream = None
                ci = 0
                xi = 0
                for bi, (t0, n) in enumerate(blocks):
                    j = 1 if t0 < NCT else 0
                    if cur_stream != j:
                        load_mod_tile(GT2, l, j, 5)
                        cur_stream = j
                    if bi + 1 < len(blocks):
                        load_c2(bi + 1)
                    h_ = hblk[bi % 2]
                    N = n * 128
                    for jf in range(32):
                        pm_ = pm1[ci % 2]
                        r_ = rl[ci % 2]
                        ci += 1
                        for k in range(KD):
                            kb.op(pe, lambda jf=jf, k=k: nc.tensor.matmul(
                                pm_[:, 0:N], lhsT=w1c[jf // 4][:, k, (jf % 4) * 128:(jf % 4 + 1) * 128], rhs=h_[:, k, 0:N],
                                start=(k == 0), stop=(k == KD - 1)), r=[w1c[jf // 4].d, h_.d], w=[pm_.d],
                                sig=(k == KD - 1))
                        kb.op(act, lambda: nc.scalar.activation(out=r_[:, 0:N], in_=pm_[:, 0:N], func=AF.Relu),
                              r=[pm_.d], w=[r_.d])
                        kb.op(dve, lambda jf=jf: nc.vector.tensor_tensor(out=aT[:, jf, 0:N], in0=r_[:, 0:N],
                                                                         in1=r_[:, 0:N], op=ALU.mult),
                              r=[r_.d], w=[aT.d])
                    for i in range(n):
                        tt = t0 + i
                        xt_ = x1t[xi % 3]
                        tm = tmp[xi % 2]
                        py_ = py2[xi % 2]
                        xi += 1
                        kb.dma(sp, xt_[:], xs[tt * 128:(tt + 1) * 128, :], w=[xt_.d], dr=[d_xs])
                        for hf in range(2):
                            for jf in range(32):
                                kb.op(pe, lambda i=i, hf=hf, jf=jf: nc.tensor.matmul(
                                    py_[:, hf, :], lhsT=aT[:, jf, i * 128:(i + 1) * 128],
                                    rhs=w2c[jf // 8][:, jf % 8, hf * 512:(hf + 1) * 512], start=(jf == 0), stop=(jf == 31)),
                                    r=[aT.d, w2c[jf // 8].d], w=[py_.d], sig=(jf == 31))
                        kb.op(dve, lambda: nc.vector.tensor_tensor(out=tm[:], in0=py_[:].rearrange("p a n -> p (a n)"),
                                                                   in1=GT2[:], op=ALU.mult),
                              r=[py_.d, GT2.d], w=[tm.d])
                        kb.op(pool, lambda: nc.gpsimd.tensor_tensor(out=xt_[:], in0=xt_[:], in1=tm[:], op=ALU.add),
                              r=[xt_.d, tm.d], w=[xt_.d])
                        if last:
                            kb.dma(sp, out[(tt - NCT) * 128:(tt - NCT + 1) * 128, :], xt_[:], r=[xt_.d], dw=[d_out])
                        else:
                            kb.dma(sp, xs[tt * 128:(tt + 1) * 128, :], xt_[:], r=[xt_.d], dw=[d_xs])
                kb.end_phase()
        for s, c in d_out.pending.items():
            kb._wait(sp, s, c)
    return nc


_CACHE = {}


def _prep_consts(cfg):
    cg, sg, cm, sm = _rope_tables(cfg["L"], cfg["C"])
    return dict(ident=np.eye(128, dtype=np.float32), cosg=cg, sing=sg, cosm=cm, sinm=sm, bands=_pool_bands())


def kernel(x, c, ctx, c_ctx, w_mod, b_mod, g_norm1, g_norm2, w_in, g_q_gqa, g_k_gqa, g_cq, g_ckv,
           w_uq, w_ukv, g_q_mla, g_k_mla, w_pool, ls_pool, w_out, w_mlp1, w_mlp2):
    cfg = dict(CFG)
    x = np.asarray(x, np.float32)
    B = x.shape[0]
    cfg["L"], cfg["C"], cfg["DEPTH"] = x.shape[1], np.asarray(ctx).shape[1], np.asarray(w_mod).shape[0]
    key = (cfg["L"], cfg["C"], cfg["DEPTH"], cfg["DEBUG"])
    if key not in _CACHE:
        _CACHE[key] = (build(cfg), _prep_consts(cfg))
    nc, consts = _CACHE[key]
    f = lambda a: np.ascontiguousarray(np.asarray(a, np.float32))
    shared = dict(w_mod=f(w_mod), b_mod=f(b_mod), g_norm1=f(g_norm1), g_norm2=f(g_norm2), w_in=f(w_in),
                  g_q_gqa=f(g_q_gqa), g_k_gqa=f(g_k_gqa), g_cq=f(g_cq), g_ckv=f(g_ckv), w_uq=f(w_uq),
                  w_ukv=f(w_ukv), g_q_mla=f(g_q_mla), g_k_mla=f(g_k_mla), w_pool=f(w_pool), ls_pool=f(ls_pool),
                  w_out=f(w_out), w_mlp1=f(w_mlp1), w_mlp2=f(w_mlp2), **consts)
    c = f(c)
    c_ctx = f(c_ctx)
    ctx = f(ctx)
    in_maps = []
    for b in range(B):
        cc = np.stack([c[b], c_ctx], axis=-1).reshape(KD, 128, 2).transpose(1, 0, 2)
        in_maps.append(dict(x=np.ascontiguousarray(x[b]), ctx=np.ascontiguousarray(ctx[b]),
                            cc=np.ascontiguousarray(cc), **shared))
    res = run_bass_kernel_spmd(nc, in_maps, core_ids=list(range(B)))
    kernel.last_results = res.results
    return np.stack([np.asarray(r["out"], np.float32) for r in res.results], axis=0)
```

```python
import numpy as np
from contextlib import ExitStack
import concourse.bass as bass
import concourse.mybir as mybir
from concourse.bass_utils import run_bass_kernel_spmd

F32 = mybir.dt.float32
BF16 = mybir.dt.bfloat16
AF = mybir.ActivationFunctionType
ALU = mybir.AluOpType
AX = mybir.AxisListType

CFG = dict(L=8192, C=256, DEPTH=4, NCORES=8, DEBUG=False)
import os
OPT_SAME = int(os.environ.get("K_SAME", "0"))
OPT_ILV_A = int(os.environ.get("K_ILV_A", "1"))
OPT_ILV_A2 = int(os.environ.get("K_ILV_A2", "1"))
OPT_ILV_C1 = int(os.environ.get("K_ILV_C1", "1"))
D = 1024
KD = 8
DFF = 4096
IN_W = 1312
EPS = 1e-6
POOL_W = (2, 4, 8, 16)


class SemC:
    def __init__(self, h):
        self.h = h
        self.n = 0


class Dep:
    __slots__ = ("name", "writers", "readers", "dsem")

    def __init__(self, name=""):
        self.name = name
        self.writers = []
        self.readers = []
        self.dsem = None


class DDep:
    def __init__(self, name=""):
        self.name = name
        self.pending = {}


class Eng:
    def __init__(self, name, h):
        self.name = name
        self.h = h
        self.sem = None
        self.waited = {}


class KB:
    def __init__(self, nc, es):
        self.nc = nc
        self.es = es
        self.pe = Eng("pe", nc.tensor)
        self.act = Eng("act", nc.scalar)
        self.dve = Eng("dve", nc.vector)
        self.pool = Eng("pool", nc.gpsimd)
        self.sp = Eng("sp", nc.sync)
        self.engines = [self.pe, self.act, self.dve, self.pool, self.sp]
        self.all_sems = []
        self.dsem_free = []
        self.dsem_all = []
        self.phase_deps = []
        for i in range(44):
            s = SemC(es.enter_context(nc.semaphore(f"d{i}")))
            self.dsem_free.append(s)
            self.dsem_all.append(s)
        self.epoch_id = 0
        self.new_epoch()

    def new_epoch(self):
        for E in self.engines:
            if E is self.sp:
                continue
            E.sem = SemC(self.es.enter_context(self.nc.semaphore(f"e{self.epoch_id}_{E.name}")))
            self.all_sems.append(E.sem)
        self.epoch_id += 1

    def dep(self, name=""):
        d = Dep(name)
        self.phase_deps.append(d)
        return d

    def _wait(self, E, sem, cnt):
        if E.waited.get(sem, 0) >= cnt:
            return
        E.h.wait_ge(sem.h, cnt)
        E.waited[sem] = cnt

    def op(self, E, fn, r=(), w=(), sig=True):
        need = {}

        def add(s, c):
            if E.waited.get(s, 0) >= c:
                return
            if need.get(s, 0) < c:
                need[s] = c

        for d in r:
            for (we, s, c) in d.writers:
                add(s, c)
        for d in w:
            for (we, s, c) in d.writers:
                if we is not E or (OPT_SAME and E is not self.pe):
                    add(s, c)
            for (re, s, c) in d.readers:
                if re is not E or (OPT_SAME and E is not self.pe):
                    add(s, c)
        items = list(need.items())
        for (s, c) in items[:-1]:
            E.h.wait_ge(s.h, c)
            E.waited[s] = c
        ins = fn()
        if items:
            s, c = items[-1]
            ins._wait_ge(s.h, c)
            E.waited[s] = c
        if sig:
            E.sem.n += 1
            ins.then_inc(E.sem.h, 1)
            rec = (E, E.sem, E.sem.n)
        else:
            rec = (E, E.sem, E.sem.n + 1)
        for d in r:
            d.readers = [x for x in d.readers if x[0] is not E] + [rec]
        for d in w:
            d.writers = [rec]
            d.readers = []
        return ins

    def dma(self, Q, out, in_, r=(), w=(), dr=(), dw=()):
        d0 = (list(w) + list(r))[0]
        if d0.dsem is None:
            d0.dsem = self.dsem_free.pop(0)
        ds = d0.dsem
        for d in r:
            for (we, s, c) in d.writers:
                self._wait(Q, s, c)
        for d in w:
            for (we, s, c) in d.writers:
                if s is not ds:
                    self._wait(Q, s, c)
            for (re, s, c) in d.readers:
                self._wait(Q, s, c)
        for dd in dr:
            for s, c in dd.pending.items():
                self._wait(Q, s, c)
        ins = Q.h.dma_start(out=out, in_=in_)
        ds.n += 16
        ins.then_inc(ds.h, 16)
        rec = (None, ds, ds.n)
        for d in r:
            d.readers = [x for x in d.readers if x[1] is not ds] + [rec]
        for d in w:
            d.writers = [rec]
            d.readers = []
        for dd in dw:
            dd.pending[ds] = ds.n
        return ins

    def barrier(self):
        for E in self.engines:
            for F in self.engines:
                if F is E or F.sem is None:
                    continue
                if F.sem.n > 0:
                    self._wait(E, F.sem, F.sem.n)
            for s in self.dsem_all:
                if s.n > 0:
                    self._wait(E, s, s.n)

    def end_phase(self):
        self.barrier()
        for d in self.phase_deps:
            if d.dsem is not None:
                self.dsem_free.append(d.dsem)
                d.dsem = None
        self.phase_deps = []


class T:
    _uid = [0]

    def __init__(self, kb, es, name, shape, dt, psum=False):
        T._uid[0] += 1
        name = f"{name}_{T._uid[0]}"
        if psum:
            self.t = es.enter_context(kb.nc.psum_tensor(name, shape, dt))
        else:
            self.t = es.enter_context(kb.nc.sbuf_tensor(name, shape, dt))
        self.d = kb.dep(name)

    def __getitem__(self, k):
        return self.t[k]


def _rope_tables(L, C):
    rows = L // 64
    row = np.repeat(np.arange(rows, dtype=np.float32), 64)
    col = np.tile(np.arange(64, dtype=np.float32), rows)

    def tab(rot):
        n = rot // 4
        inv = (10000.0 ** (-np.arange(n, dtype=np.float32) / n)).astype(np.float32)
        ang = np.concatenate([row[:, None] * inv, col[:, None] * inv], axis=-1).astype(np.float32)
        cos = np.concatenate([np.ones((C, rot // 2), np.float32), np.cos(ang).astype(np.float32)], 0)
        sin = np.concatenate([np.zeros((C, rot // 2), np.float32), np.sin(ang).astype(np.float32)], 0)
        return cos, sin

    cg, sg = tab(64)
    cm, sm = tab(32)
    T_ = L + C
    nt = T_ // 128

    def lay(a):
        return np.ascontiguousarray(a.reshape(nt, 128, -1).transpose(1, 0, 2))

    return lay(cg), lay(sg), lay(cm), lay(sm)


def _pool_bands():
    B = np.zeros((128, 5, 4, 128), np.float32)
    for g, w in enumerate(POOL_W):
        h = w // 2
        for t in range(128):
            for tp in range(t - h, t + h):
                if 0 <= tp < 128:
                    B[tp, 1, g, t] += 1.0 / w
                elif tp < 0:
                    B[tp + 128, 0, g, t] += 1.0 / w
                else:
                    B[tp - 128, 2, g, t] += 1.0 / w
            B[t, 1, g, t] -= 1.0
            lo = max(t - h, 0)
            cnt = (t + h) - lo
            for tp in range(lo, min(t + h, 128)):
                B[tp, 3, g, t] += 1.0 / cnt
            B[t, 3, g, t] -= 1.0
            hi = min(t + h, 128)
            cnt = hi - (t - h)
            for tp in range(max(t - h, 0), hi):
                B[tp, 4, g, t] += 1.0 / cnt
            B[t, 4, g, t] -= 1.0
    return B


def build(cfg):
    L, C, DEPTH, DEBUG = cfg["L"], cfg["C"], cfg["DEPTH"], cfg["DEBUG"]
    TT = L + C
    NT = TT // 128
    NCT = C // 128
    nc = bass.Bass("TRN2", target_bir_lowering=False)

    def din(name, shape, dt=F32):
        return nc.dram_tensor(name, shape, dt, kind="ExternalInput").ap()

    def dscr(name, shape, dt):
        kind = "ExternalOutput" if DEBUG else "Internal"
        return nc.dram_tensor(name, shape, dt, kind=kind).ap()

    x_in = din("x", [L, D])
    ctx_in = din("ctx", [C, D])
    cc_in = din("cc", [128, KD, 2])
    w_mod = din("w_mod", [DEPTH, D, 6 * D])
    b_mod = din("b_mod", [DEPTH, 6 * D])
    g_norm1 = din("g_norm1", [DEPTH, D])
    g_norm2 = din("g_norm2", [DEPTH, D])
    w_in = din("w_in", [DEPTH, D, IN_W])
    g_q_gqa = din("g_q_gqa", [DEPTH, 64])
    g_k_gqa = din("g_k_gqa", [DEPTH, 64])
    g_cq = din("g_cq", [DEPTH, 256])
    g_ckv = din("g_ckv", [DEPTH, 128])
    w_uq = din("w_uq", [DEPTH, 256, 576])
    w_ukv = din("w_ukv", [DEPTH, 128, 768])
    g_q_mla = din("g_q_mla", [DEPTH, 96])
    g_k_mla = din("g_k_mla", [DEPTH, 96])
    w_pool = din("w_pool", [DEPTH, 4, 64, 64])
    ls_pool = din("ls_pool", [DEPTH, 256])
    w_out = din("w_out", [DEPTH, D, D])
    w_mlp1 = din("w_mlp1", [DEPTH, D, DFF])
    w_mlp2 = din("w_mlp2", [DEPTH, DFF, D])
    ident_in = din("ident", [128, 128])
    cosg_in = din("cosg", [128, NT, 32])
    sing_in = din("sing", [128, NT, 32])
    cosm_in = din("cosm", [128, NT, 16])
    sinm_in = din("sinm", [128, NT, 16])
    bands_in = din("bands", [128, 5, 4, 128])
    out = nc.dram_tensor("out", [L, D], F32, kind="ExternalOutput").ap()

    xs = dscr("xs", [TT, D], F32)
    modv = dscr("modv", [DEPTH, 2, 6 * D], F32)
    QTg = dscr("QTg", [6, 64, TT], BF16)
    KTg = dscr("KTg", [2, 64, TT], BF16)
    QTm = dscr("QTm", [6, 96, TT], BF16)
    KTm = dscr("KTm", [6, 96, TT], BF16)
    Vs = dscr("Vs", [TT, 512], BF16)
    pin = dscr("pin", [TT, 256], F32)
    mixT = dscr("mixT", [D, TT], BF16)
    h2T = dscr("h2T", [D, TT], BF16)

    d_xs, d_modv, d_q, d_v, d_pin, d_mix, d_h2, d_out = [DDep(n) for n in
                                                       ("xs", "modv", "q", "v", "pin", "mix", "h2", "out")]

    ges = ExitStack()
    with ges:
        kb = KB(nc, ges)
        pe, act, dve, pool, sp = kb.pe, kb.act, kb.dve, kb.pool, kb.sp

        ident = T(kb, ges, "ident", [128, 128], BF16)
        kb.dma(pool, ident[:], ident_in[:, :], w=[ident.d])
        epsc = T(kb, ges, "epsc", [128, 1], F32)
        kb.op(dve, lambda: nc.vector.memset(epsc[:], EPS), w=[epsc.d])

        def rstd_chain(ssq, rstd, n, inv_dim):
            kb.op(act, lambda: nc.scalar.activation(out=rstd[:, 0:n], in_=ssq[:, 0:n], func=AF.Ln,
                                                    bias=epsc[:, 0:1], scale=inv_dim),
                  r=[ssq.d, epsc.d], w=[rstd.d])
            kb.op(act, lambda: nc.scalar.activation(out=rstd[:, 0:n], in_=rstd[:, 0:n], func=AF.Exp, scale=-0.5),
                  r=[rstd.d], w=[rstd.d])

        def x_src(l, t0, n):
            if l == 0:
                if t0 < NCT:
                    a = ctx_in[t0 * 128:(t0 + n) * 128, :]
                else:
                    a = x_in[(t0 - NCT) * 128:(t0 - NCT + n) * 128, :]
            else:
                a = xs[t0 * 128:(t0 + n) * 128, :]
            return a.rearrange("(i p) c -> p i c", p=128)

        def bc_row(ap1d):
            return ap1d.partition_broadcast(128)

        with ExitStack() as es:
            cs_raw = T(kb, es, "cs_raw", [128, KD, 2], F32)
            cs = T(kb, es, "cs", [128, KD, 2], F32)
            kb.dma(sp, cs_raw[:], cc_in[:, :, :], w=[cs_raw.d])
            kb.op(act, lambda: nc.scalar.activation(out=cs[:], in_=cs_raw[:], func=AF.Silu), r=[cs_raw.d], w=[cs.d])
            wm = [T(kb, es, f"wm{i}", [128, KD, 512], F32) for i in range(3)]
            pm = [T(kb, es, f"pm{i}", [128, 512], F32, psum=True) for i in range(2)]
            mo = [T(kb, es, f"mo{i}", [2, 512], F32) for i in range(2)]
            bm = [T(kb, es, f"bm{i}", [2, 6 * D], F32) for i in range(2)]
            it = 0
            for l in range(DEPTH):
                bml = bm[l % 2]
                for j in range(2):
                    kb.dma(sp, bml[j:j + 1, :], b_mod[l:l + 1, :], w=[bml.d])
                wv = w_mod[l].rearrange("(k p) n -> p k n", p=128)
                for n in range(12):
                    wt = wm[it % 3]
                    kb.dma(sp, wt[:], wv[:, :, n * 512:(n + 1) * 512], w=[wt.d])
                    pt = pm[it % 2]
                    for k in range(KD):
                        kb.op(pe, lambda k=k: nc.tensor.matmul(pt[0:2, :], lhsT=cs[:, k, :], rhs=wt[:, k, :],
                                                               start=(k == 0), stop=(k == KD - 1)),
                              r=[cs.d, wt.d], w=[pt.d], sig=(k == KD - 1))
                    mt = mo[it % 2]
                    kb.op(dve, lambda: nc.vector.tensor_tensor(out=mt[:], in0=pt[0:2, :],
                                                               in1=bml[:, n * 512:(n + 1) * 512], op=ALU.add),
                          r=[pt.d, bml.d], w=[mt.d])
                    kb.dma(sp, modv[l, :, n * 512:(n + 1) * 512], mt[:], r=[mt.d], dw=[d_modv])
                    it += 1
            kb.end_phase()

        def load_mod_tile(tile, l, j, idx, tmp=None):
            kb.dma(sp, tile[:], bc_row(modv[l, j, idx * D:(idx + 1) * D]), w=[tile.d], dr=[d_modv])

        def make_scale_tile(tile, gt, l, j, idx):
            load_mod_tile(tile, l, j, idx)
            kb.op(dve, lambda: nc.vector.scalar_tensor_tensor(out=tile[:], in0=tile[:], scalar=1.0, in1=gt[:],
                                                             op0=ALU.add, op1=ALU.mult),
                  r=[tile.d, gt.d], w=[tile.d])

        for l in range(DEPTH):
            last = (l == DEPTH - 1)
            if l > 0:
                kb.new_epoch()
            with ExitStack() as es:
                NB = 2
                win = T(kb, es, "win", [128, KD, IN_W], BF16)
                wv = w_in[l].rearrange("(k p) n -> p k n", p=128)
                for k in range(KD):
                    kb.dma(pool, win[:, k, :], wv[:, k, :], w=[win.d])
                wuq_f = T(kb, es, "wuq_f", [128, 2, 576], F32)
                wuq = T(kb, es, "wuq", [128, 2, 576], BF16)
                wukv_f = T(kb, es, "wukv_f", [128, 768], F32)
                wukv = T(kb, es, "wukv", [128, 768], BF16)
                gcq = T(kb, es, "gcq", [128, 2], F32)
                gckv = T(kb, es, "gckv", [128, 1], F32)
                kb.dma(sp, wuq_f[:], w_uq[l].rearrange("(k p) n -> p k n", p=128), w=[wuq_f.d])
                kb.dma(sp, wukv_f[:], w_ukv[l], w=[wukv_f.d])
                for k in range(2):
                    kb.dma(sp, gcq[:, k:k + 1], g_cq[l, k * 128:(k + 1) * 128].rearrange("(p o) -> p o", o=1),
                           w=[gcq.d])
                kb.dma(sp, gckv[:, 0:1], g_ckv[l].rearrange("(p o) -> p o", o=1), w=[gckv.d])
                for k in range(2):
                    kb.op(dve, lambda k=k: nc.vector.tensor_scalar(out=wuq[:, k, :], in0=wuq_f[:, k, :],
                                                                   scalar1=gcq[:, k:k + 1], scalar2=None,
                                                                   op0=ALU.mult),
                          r=[wuq_f.d, gcq.d], w=[wuq.d])
                kb.op(dve, lambda: nc.vector.tensor_scalar(out=wukv[:], in0=wukv_f[:], scalar1=gckv[:, 0:1],
                                                           scalar2=None, op0=ALU.mult),
                      r=[wukv_f.d, gckv.d], w=[wukv.d])
                GQK = T(kb, es, "GQK", [128, 8, 64], F32)
                for h in range(8):
                    src = g_q_gqa[l] if h < 6 else g_k_gqa[l]
                    kb.dma(sp, GQK[:, h, :], bc_row(src), w=[GQK.d])
                GQM = T(kb, es, "GQM", [128, 6, 96], F32)
                GKM = T(kb, es, "GKM", [128, 6, 96], F32)
                for h in range(6):
                    kb.dma(sp, GQM[:, h, :], bc_row(g_q_mla[l]), w=[GQM.d])
                    kb.dma(sp, GKM[:, h, :], bc_row(g_k_mla[l]), w=[GKM.d])
                cosg = T(kb, es, "cosg", [128, NT, 32], F32)
                sing = T(kb, es, "sing", [128, NT, 32], F32)
                cosm = T(kb, es, "cosm", [128, NT, 16], F32)
                sinm = T(kb, es, "sinm", [128, NT, 16], F32)
                kb.dma(sp, cosg[:], cosg_in[:, :, :], w=[cosg.d])
                kb.dma(sp, sing[:], sing_in[:, :, :], w=[sing.d])
                kb.dma(sp, cosm[:], cosm_in[:, :, :], w=[cosm.d])
                kb.dma(sp, sinm[:], sinm_in[:, :, :], w=[sinm.d])
                g1t = T(kb, es, "g1t", [128, D], F32)
                kb.dma(sp, g1t[:], bc_row(g_norm1[l]), w=[g1t.d])
                A1 = T(kb, es, "A1", [128, D], F32)
                B1 = T(kb, es, "B1", [128, D], F32)

                xblk = [T(kb, es, f"xblk{i}", [128, NB, D], F32) for i in range(2)]
                junk = [T(kb, es, f"junkA{i}", [128, D], BF16) for i in range(2)]
                ssq1 = T(kb, es, "ssq1", [128, NB], F32)
                rstd1 = T(kb, es, "rstd1", [128, NB], F32)
                t1 = T(kb, es, "t1", [128, NB, D], F32)
                hb = T(kb, es, "hb", [128, NB, D], BF16)
                hT = T(kb, es, "hT", [128, NB, KD, 128], BF16)
                u_sbs = [T(kb, es, f"u_sb{i}", [128, NB, IN_W], F32) for i in range(2)]
                sq_g = T(kb, es, "sq_g", [128, NB, 512], F32)
                sq = T(kb, es, "sq", [128, NB, 576], F32)
                ssq8 = T(kb, es, "ssq8", [128, NB * 8], F32)
                rstd8 = T(kb, es, "rstd8", [128, NB * 8], F32)
                tq = T(kb, es, "tq", [128, NB, 8, 64], F32)
                tq2 = T(kb, es, "tq2", [128, NB, 8, 64], F32)
                tq3 = T(kb, es, "tq3", [128, NB, 8, 64], F32)
                qk_bf = T(kb, es, "qk_bf", [128, NB, 8, 64], BF16)
                ssqc = T(kb, es, "ssqc", [128, NB * 3], F32)
                ssqcq = T(kb, es, "ssqcq", [128, NB], F32)
                ssqckv = T(kb, es, "ssqckv", [128, NB], F32)
                rstdcq = T(kb, es, "rstdcq", [128, NB], F32)
                rstdckv = T(kb, es, "rstdckv", [128, NB], F32)
                cb = T(kb, es, "cb", [128, NB, 384], BF16)
                cT = T(kb, es, "cT", [128, NB, 3, 128], BF16)
                qm_sb = T(kb, es, "qm_sb", [128, NB, 6, 96], F32)
                kv_sb = T(kb, es, "kv_sb", [128, NB, 6, 128], F32)
                kcat = T(kb, es, "kcat", [128, NB, 6, 96], F32)
                ssq6 = T(kb, es, "ssq6", [128, NB * 6], F32)
                rstd6 = T(kb, es, "rstd6", [128, NB * 6], F32)
                qg = T(kb, es, "qg", [128, NB, 6, 96], F32)
                rt = [T(kb, es, f"rt{i}", [128, NB, 6, 16], F32) for i in range(4)]
                qm_bf = T(kb, es, "qm_bf", [128, NB, 6, 96], BF16)
                km_bf = T(kb, es, "km_bf", [128, NB, 6, 96], BF16)
                vg_blk = T(kb, es, "vg_blk", [128, NB, 128], BF16)
                vm_blk = T(kb, es, "vm_blk", [128, NB, 384], BF16)
                qkT_blk = T(kb, es, "qkT_blk", [64, 8, NB * 128], BF16)
                qmT_blk = T(kb, es, "qmT_blk", [96, 6, NB * 128], BF16)
                kmT_blk = T(kb, es, "kmT_blk", [96, 6, NB * 128], BF16)
                pTa = T(kb, es, "pTa", [128, 1024], BF16, psum=True)
                pu = T(kb, es, "pu", [128, 3, 512], F32, psum=True)
                pTg = T(kb, es, "pTg", [128, 1024], BF16, psum=True)
                pkv = T(kb, es, "pkv", [128, 2, 512], F32, psum=True)
                pTm = T(kb, es, "pTm", [128, 1024], BF16, psum=True)

                nblk = NT // NB
                cur_stream = [None]
                jk = [0]

                def load_x(b):
                    xb_ = xblk[b % 2]
                    kb.dma(sp, xb_[:], x_src(l, b * NB, NB), w=[xb_.d], dr=[d_xs])

                def sg1(b):
                    t0 = b * NB
                    j = 1 if t0 < NCT else 0
                    if cur_stream[0] != j:
                        make_scale_tile(A1, g1t, l, j, 1)
                        load_mod_tile(B1, l, j, 0)
                        cur_stream[0] = j
                    if b + 1 < nblk:
                        load_x(b + 1)
                    xb_ = xblk[b % 2]
                    u_sb = u_sbs[b % 2]
                    for i in range(NB):
                        jk[0] += 1
                        jt = junk[jk[0] % 2]
                        kb.op(act, lambda i=i: nc.scalar.activation(out=jt[:], in_=xb_[:, i, :], func=AF.Square,
                                                                    accum_out=ssq1[:, i:i + 1]),
                              r=[xb_.d], w=[jt.d, ssq1.d])
                        yield
                    rstd_chain(ssq1, rstd1, NB, 1.0 / D)
                    yield
                    for i in range(NB):
                        kb.op(dve, lambda i=i: nc.vector.scalar_tensor_tensor(
                            out=t1[:, i, :], in0=xb_[:, i, :], scalar=rstd1[:, i:i + 1], in1=A1[:],
                            op0=ALU.mult, op1=ALU.mult), r=[xb_.d, rstd1.d, A1.d], w=[t1.d])
                        yield
                    for i in range(NB):
                        kb.op(pool, lambda i=i: nc.gpsimd.tensor_tensor(out=hb[:, i, :], in0=t1[:, i, :], in1=B1[:],
                                                                        op=ALU.add), r=[t1.d, B1.d], w=[hb.d])
                        yield
                    for i in range(NB):
                        for k in range(KD):
                            kb.op(pe, lambda i=i, k=k: nc.tensor.transpose(pTa[:, k * 128:(k + 1) * 128],
                                                                           hb[:, i, k * 128:(k + 1) * 128], ident[:]),
                                  r=[hb.d, ident.d], w=[pTa.d], sig=(k == KD - 1))
                        yield
                        kb.op(act, lambda i=i: nc.scalar.copy(out=hT[:, i, :, :].rearrange("p k t -> p (k t)"),
                                                              in_=pTa[:, :]), r=[pTa.d], w=[hT.d])
                        yield
                    for i in range(NB):
                        for c, (c0, cw) in enumerate(((0, 512), (512, 512), (1024, 288))):
                            for k in range(KD):
                                kb.op(pe, lambda i=i, c=c, c0=c0, cw=cw, k=k: nc.tensor.matmul(
                                    pu[:, c, 0:cw], lhsT=hT[:, i, k, :], rhs=win[:, k, c0:c0 + cw],
                                    start=(k == 0), stop=(k == KD - 1)),
                                    r=[hT.d, win.d], w=[pu.d], sig=(k == KD - 1))
                            yield
                        kb.op(act, lambda i=i: nc.scalar.copy(out=u_sb[:, i, 0:1024].rearrange("p (c n) -> p c n", c=2),
                                                              in_=pu[:, 0:2, :]), r=[pu.d], w=[u_sb.d])
                        yield
                        kb.op(act, lambda i=i: nc.scalar.copy(out=u_sb[:, i, 1024:IN_W], in_=pu[:, 2, 0:288]),
                              r=[pu.d], w=[u_sb.d])
                        yield

                def sg2a(b):
                    t0 = b * NB
                    u_sb = u_sbs[b % 2]
                    tsl = slice(t0 * 128, (t0 + NB) * 128)
                    uqk = u_sb[:, :, 0:512]
                    kb.op(dve, lambda: nc.vector.tensor_tensor(out=sq_g[:], in0=uqk, in1=uqk, op=ALU.mult),
                          r=[u_sb.d], w=[sq_g.d])
                    yield
                    kb.op(dve, lambda: nc.vector.tensor_reduce(
                        out=ssq8[:, :].rearrange("p (n h) -> p n h", h=8),
                        in_=sq_g[:].rearrange("p n (h d) -> p n h d", d=64),
                        axis=AX.X, op=ALU.add), r=[sq_g.d], w=[ssq8.d])
                    yield
                    rstd_chain(ssq8, rstd8, NB * 8, 1.0 / 64)
                    yield
                    kb.op(dve, lambda: nc.vector.tensor_tensor(
                        out=tq[:],
                        in0=uqk.rearrange("p n (h d) -> p n h d", d=64),
                        in1=rstd8[:, :].rearrange("p (n h) -> p n h", h=8).unsqueeze(3).to_broadcast([128, NB, 8, 64]),
                        op=ALU.mult),
                        r=[u_sb.d, rstd8.d], w=[tq.d])
                    yield
                    for i in range(NB):
                        kb.op(pool, lambda i=i: nc.gpsimd.tensor_tensor(out=tq[:, i, :, :], in0=tq[:, i, :, :],
                                                                        in1=GQK[:], op=ALU.mult),
                              r=[tq.d, GQK.d], w=[tq.d])
                        yield
                    for i in range(NB):
                        cg = cosg[:, t0 + i, :].unsqueeze(1).to_broadcast([128, 8, 32])
                        sg = sing[:, t0 + i, :].unsqueeze(1).to_broadcast([128, 8, 32])
                        lo = tq[:, i, :, 0:32]
                        hi = tq[:, i, :, 32:64]
                        kb.op(dve, lambda: nc.vector.tensor_tensor(out=tq2[:, i, :, 0:32], in0=lo, in1=cg, op=ALU.mult),
                              r=[tq.d, cosg.d], w=[tq2.d])
                        kb.op(pool, lambda: nc.gpsimd.tensor_tensor(out=tq3[:, i, :, 0:32], in0=hi, in1=sg, op=ALU.mult),
                              r=[tq.d, sing.d], w=[tq3.d])
                        yield
                        kb.op(dve, lambda: nc.vector.tensor_tensor(out=tq2[:, i, :, 32:64], in0=hi, in1=cg, op=ALU.mult),
                              r=[tq.d, cosg.d], w=[tq2.d])
                        kb.op(pool, lambda: nc.gpsimd.tensor_tensor(out=tq3[:, i, :, 32:64], in0=lo, in1=sg, op=ALU.mult),
                              r=[tq.d, sing.d], w=[tq3.d])
                        yield
                        kb.op(dve, lambda: nc.vector.tensor_tensor(out=qk_bf[:, i, :, 0:32], in0=tq2[:, i, :, 0:32],
                                                                   in1=tq3[:, i, :, 0:32], op=ALU.subtract),
                              r=[tq2.d, tq3.d], w=[qk_bf.d])
                        yield
                        kb.op(dve, lambda: nc.vector.tensor_tensor(out=qk_bf[:, i, :, 32:64], in0=tq2[:, i, :, 32:64],
                                                                   in1=tq3[:, i, :, 32:64], op=ALU.add),
                              r=[tq2.d, tq3.d], w=[qk_bf.d])
                        yield
                    kb.op(act, lambda: nc.scalar.copy(out=vg_blk[:], in_=u_sb[:, :, 512:640]),
                          r=[u_sb.d], w=[vg_blk.d])
                    yield
                    kb.dma(sp, Vs[tsl, 0:128].rearrange("(i p) c -> p i c", p=128), vg_blk[:], r=[vg_blk.d], dw=[d_v])
                    kb.dma(sp, pin[tsl, :].rearrange("(i p) c -> p i c", p=128), u_sb[:, :, 1056:IN_W],
                           r=[u_sb.d], dw=[d_pin])
                    for i in range(NB):
                        for h in range(8):
                            kb.op(pe, lambda i=i, h=h: nc.tensor.transpose(pTg[0:64, h * 128:(h + 1) * 128],
                                                                           qk_bf[:, i, h, :], ident[:]),
                                  r=[qk_bf.d, ident.d], w=[pTg.d], sig=(h == 7))
                        yield
                        kb.op(act, lambda i=i: nc.scalar.copy(out=qkT_blk[:, :, i * 128:(i + 1) * 128],
                                                              in_=pTg[0:64, :].rearrange("p (h t) -> p h t", t=128)),
                              r=[pTg.d], w=[qkT_blk.d])
                        yield
                    kb.dma(sp, QTg[:, :, tsl].rearrange("h d t -> d h t"), qkT_blk[:, 0:6, :], r=[qkT_blk.d], dw=[d_q])
                    kb.dma(sp, KTg[:, :, tsl].rearrange("h d t -> d h t"), qkT_blk[:, 6:8, :], r=[qkT_blk.d], dw=[d_q])
                    yield

                def mla_norm_rope(src, G, dst, t0):
                    s3 = src[:].rearrange("p n h d -> p (n h) d")
                    kb.op(dve, lambda: nc.vector.tensor_tensor(
                        out=sq[:].rearrange("p n (h d) -> p (n h) d", d=96), in0=s3, in1=s3, op=ALU.mult),
                        r=[src.d], w=[sq.d])
                    yield
                    kb.op(dve, lambda: nc.vector.tensor_reduce(
                        out=ssq6[:, :], in_=sq[:].rearrange("p n (h d) -> p (n h) d", d=96),
                        axis=AX.X, op=ALU.add), r=[sq.d], w=[ssq6.d])
                    yield
                    rstd_chain(ssq6, rstd6, NB * 6, 1.0 / 96)
                    yield
                    kb.op(dve, lambda: nc.vector.tensor_tensor(
                        out=qg[:].rearrange("p n h d -> p (n h) d"), in0=s3,
                        in1=rstd6[:, :].unsqueeze(2).to_broadcast([128, NB * 6, 96]), op=ALU.mult),
                        r=[src.d, rstd6.d], w=[qg.d])
                    yield
                    for i in range(NB):
                        kb.op(pool, lambda i=i: nc.gpsimd.tensor_tensor(out=qg[:, i, :, :], in0=qg[:, i, :, :],
                                                                        in1=G[:], op=ALU.mult),
                              r=[qg.d, G.d], w=[qg.d])
                        yield
                    kb.op(act, lambda: nc.scalar.copy(out=dst[:, :, :, 0:64], in_=qg[:, :, :, 0:64]),
                          r=[qg.d], w=[dst.d])
                    yield
                    for i in range(NB):
                        cm_ = cosm[:, t0 + i, :].unsqueeze(1).to_broadcast([128, 6, 16])
                        sm_ = sinm[:, t0 + i, :].unsqueeze(1).to_broadcast([128, 6, 16])
                        lo = qg[:, i, :, 64:80]
                        hi = qg[:, i, :, 80:96]
                        kb.op(dve, lambda: nc.vector.tensor_tensor(out=rt[0][:, i, :, :], in0=lo, in1=cm_, op=ALU.mult),
                              r=[qg.d, cosm.d], w=[rt[0].d])
                        kb.op(pool, lambda: nc.gpsimd.tensor_tensor(out=rt[2][:, i, :, :], in0=lo, in1=sm_, op=ALU.mult),
                              r=[qg.d, sinm.d], w=[rt[2].d])
                        yield
                        kb.op(dve, lambda: nc.vector.tensor_tensor(out=rt[1][:, i, :, :], in0=hi, in1=sm_, op=ALU.mult),
                              r=[qg.d, sinm.d], w=[rt[1].d])
                        kb.op(pool, lambda: nc.gpsimd.tensor_tensor(out=rt[3][:, i, :, :], in0=hi, in1=cm_, op=ALU.mult),
                              r=[qg.d, cosm.d], w=[rt[3].d])
                        yield
                    kb.op(dve, lambda: nc.vector.tensor_tensor(out=dst[:, :, :, 64:80], in0=rt[0][:], in1=rt[1][:],
                                                               op=ALU.subtract), r=[rt[0].d, rt[1].d], w=[dst.d])
                    yield
                    kb.op(dve, lambda: nc.vector.tensor_tensor(out=dst[:, :, :, 80:96], in0=rt[2][:], in1=rt[3][:],
                                                               op=ALU.add), r=[rt[2].d, rt[3].d], w=[dst.d])
                    yield

                def sg2b(b):
                    t0 = b * NB
                    u_sb = u_sbs[b % 2]
                    tsl = slice(t0 * 128, (t0 + NB) * 128)
                    ucc = u_sb[:, :, 640:1024]
                    kb.op(dve, lambda: nc.vector.tensor_tensor(out=sq[:, :, 0:384], in0=ucc, in1=ucc, op=ALU.mult),
                          r=[u_sb.d], w=[sq.d])
                    yield
                    kb.op(dve, lambda: nc.vector.tensor_reduce(
                        out=ssqc[:, :].rearrange("p (n c) -> p n c", c=3),
                        in_=sq[:, :, 0:384].rearrange("p n (c d) -> p n c d", d=128),
                        axis=AX.X, op=ALU.add), r=[sq.d], w=[ssqc.d])
                    yield
                    sv = ssqc[:, :].rearrange("p (n c) -> p n c", c=3)
                    kb.op(dve, lambda: nc.vector.tensor_tensor(out=ssqcq[:, :], in0=sv[:, :, 0], in1=sv[:, :, 1],
                                                               op=ALU.add), r=[ssqc.d], w=[ssqcq.d])
                    kb.op(dve, lambda: nc.vector.tensor_copy(out=ssqckv[:, :], in_=sv[:, :, 2]),
                          r=[ssqc.d], w=[ssqckv.d])
                    yield
                    rstd_chain(ssqcq, rstdcq, NB, 1.0 / 256)
                    rstd_chain(ssqckv, rstdckv, NB, 1.0 / 128)
                    yield
                    kb.op(pool, lambda: nc.gpsimd.tensor_copy(out=cb[:], in_=ucc), r=[u_sb.d], w=[cb.d])
                    yield
                    for i in range(NB):
                        for k in range(3):
                            kb.op(pe, lambda i=i, k=k: nc.tensor.transpose(pTm[:, k * 128:(k + 1) * 128],
                                                                           cb[:, i, k * 128:(k + 1) * 128], ident[:]),
                                  r=[cb.d, ident.d], w=[pTm.d], sig=(k == 2))
                        yield
                        kb.op(act, lambda i=i: nc.scalar.copy(out=cT[:, i, :, :].rearrange("p k t -> p (k t)"),
                                                              in_=pTm[:, 0:384]), r=[pTm.d], w=[cT.d])
                        yield
                    for i in range(NB):
                        for (c, c0, cw) in ((0, 0, 512), (1, 512, 64)):
                            for k in range(2):
                                kb.op(pe, lambda i=i, c=c, c0=c0, cw=cw, k=k: nc.tensor.matmul(
                                    pkv[:, c, 0:cw], lhsT=cT[:, i, k, :], rhs=wuq[:, k, c0:c0 + cw],
                                    start=(k == 0), stop=(k == 1)), r=[cT.d, wuq.d], w=[pkv.d], sig=(k == 1))
                        yield
                        qmv = qm_sb[:, i, :, :].rearrange("p h d -> p (h d)")
                        kb.op(act, lambda i=i: nc.scalar.activation(out=qmv[:, 0:512], in_=pkv[:, 0, :],
                                                                    func=AF.Identity, scale=rstdcq[:, i:i + 1]),
                              r=[pkv.d, rstdcq.d], w=[qm_sb.d])
                        kb.op(act, lambda i=i: nc.scalar.activation(out=qmv[:, 512:576], in_=pkv[:, 1, 0:64],
                                                                    func=AF.Identity, scale=rstdcq[:, i:i + 1]),
                              r=[pkv.d, rstdcq.d], w=[qm_sb.d])
                        yield
                        for (c, c0, cw) in ((0, 0, 512), (1, 512, 256)):
                            kb.op(pe, lambda i=i, c=c, c0=c0, cw=cw: nc.tensor.matmul(
                                pkv[:, c, 0:cw], lhsT=cT[:, i, 2, :], rhs=wukv[:, c0:c0 + cw],
                                start=True, stop=True), r=[cT.d, wukv.d], w=[pkv.d])
                        yield
                        kvv = kv_sb[:, i, :, :].rearrange("p h d -> p (h d)")
                        kb.op(act, lambda i=i: nc.scalar.activation(out=kvv[:, 0:512], in_=pkv[:, 0, :],
                                                                    func=AF.Identity, scale=rstdckv[:, i:i + 1]),
                              r=[pkv.d, rstdckv.d], w=[kv_sb.d])
                        kb.op(act, lambda i=i: nc.scalar.activation(out=kvv[:, 512:768], in_=pkv[:, 1, 0:256],
                                                                    func=AF.Identity, scale=rstdckv[:, i:i + 1]),
                              r=[pkv.d, rstdckv.d], w=[kv_sb.d])
                        yield
                    kb.op(pool, lambda: nc.gpsimd.tensor_copy(out=kcat[:, :, :, 0:64], in_=kv_sb[:, :, :, 0:64]),
                          r=[kv_sb.d], w=[kcat.d])
                    yield
                    kb.op(pool, lambda: nc.gpsimd.tensor_copy(
                        out=kcat[:, :, :, 64:96],
                        in_=u_sb[:, :, 1024:1056].unsqueeze(2).to_broadcast([128, NB, 6, 32])),
                        r=[u_sb.d], w=[kcat.d])
                    yield
                    kb.op(act, lambda: nc.scalar.copy(out=vm_blk[:].rearrange("p n (h d) -> p n h d", d=64),
                                                      in_=kv_sb[:, :, :, 64:128]), r=[kv_sb.d], w=[vm_blk.d])
                    yield
                    kb.dma(sp, Vs[tsl, 128:512].rearrange("(i p) c -> p i c", p=128), vm_blk[:], r=[vm_blk.d], dw=[d_v])
                    yield from mla_norm_rope(qm_sb, GQM, qm_bf, t0)
                    yield from mla_norm_rope(kcat, GKM, km_bf, t0)
                    for (srcb, dstb, dram) in ((qm_bf, qmT_blk, QTm), (km_bf, kmT_blk, KTm)):
                        for i in range(NB):
                            for h in range(6):
                                kb.op(pe, lambda i=i, h=h: nc.tensor.transpose(pTm[0:96, h * 128:(h + 1) * 128],
                                                                               srcb[:, i, h, :], ident[:]),
                                      r=[srcb.d, ident.d], w=[pTm.d], sig=(h == 5))
                            yield
                            kb.op(act, lambda i=i: nc.scalar.copy(
                                out=dstb[:, :, i * 128:(i + 1) * 128],
                                in_=pTm[0:96, 0:768].rearrange("p (h t) -> p h t", t=128)), r=[pTm.d], w=[dstb.d])
                            yield
                        kb.dma(sp, dram[:, :, tsl].rearrange("h d t -> d h t"), dstb[:], r=[dstb.d], dw=[d_q])
                        yield

                def run_interleaved(gens, weights):
                    gens = [g for g in gens if g is not None]
                    if not OPT_ILV_A:
                        for g in gens:
                            for _ in g:
                                pass
                        return
                    alive = list(range(len(gens)))
                    while alive:
                        for gi in list(alive):
                            for _ in range(weights[gi]):
                                try:
                                    next(gens[gi])
                                except StopIteration:
                                    alive.remove(gi)
                                    break

                load_x(0)
                if not OPT_ILV_A:
                    for b in range(nblk):
                        for g_ in (sg1(b), sg2a(b), sg2b(b)):
                            for _ in g_:
                                pass
                else:
                    nxt = {"1": 0, "a": 0, "b": 0}
                    done = {"1": -1, "a": -1, "b": -1}
                    gen = {"1": None, "a": None, "b": None}
                    mk = {"1": sg1, "a": sg2a, "b": sg2b}
                    wts = (("1", 1), ("a", 1), ("b", 3))
                    while True:
                        if gen["1"] is None and nxt["1"] < nblk and min(done["a"], done["b"]) >= nxt["1"] - 2:
                            gen["1"] = mk["1"](nxt["1"])
                            nxt["1"] += 1
                        for c_ in ("a", "b"):
                            if gen[c_] is None and nxt[c_] < nblk and done["1"] >= nxt[c_]:
                                gen[c_] = mk[c_](nxt[c_])
                                nxt[c_] += 1
                        if gen["1"] is None and gen["a"] is None and gen["b"] is None:
                            break
                        for (c_, w_) in wts:
                            if gen[c_] is None:
                                continue
                            for _ in range(w_):
                                try:
                                    next(gen[c_])
                                except StopIteration:
                                    gen[c_] = None
                                    done[c_] = nxt[c_] - 1
                                    break
                kb.end_phase()

            with ExitStack() as es:
                bands = T(kb, es, "bands", [128, 5, 4, 128], F32)
                kb.dma(sp, bands[:], bands_in[:, :, :, :], w=[bands.d])
                wpool = T(kb, es, "wpool", [64, 4, 64], BF16)
                kb.dma(pool, wpool[:], w_pool[l].rearrange("g c d -> c g d"), w=[wpool.d])
                lsp = T(kb, es, "lsp", [64, 4], F32)
                for g in range(4):
                    kb.dma(sp, lsp[:, g:g + 1], ls_pool[l, g * 64:(g + 1) * 64].rearrange("(p o) -> p o", o=1),
                           w=[lsp.d])
                pw = [T(kb, es, f"pw{i}", [128, 3, 256], F32) for i in range(3)]
                py = [T(kb, es, f"py{i}", [64, 512], F32, psum=True) for i in range(2)]
                po = [T(kb, es, f"po{i}", [64, 512], F32, psum=True) for i in range(2)]
                yT = [T(kb, es, f"yT{i}", [64, 4, 128], BF16) for i in range(2)]
                oT = [T(kb, es, f"oT{i}", [64, 4, 128], BF16) for i in range(2)]
                seqs = [(NCT, NT)] if last else [(0, NCT), (NCT, NT)]

                def pool_tile(tt, s0, s1, it):
                    first, lastt = (tt == s0), (tt == s1 - 1)
                    j0 = 1 if first else 0
                    j1 = 2 if lastt else 3
                    pw_ = pw[it % 3]
                    kb.dma(sp, pw_[:, j0:j1, :],
                           pin[(tt - 1 + j0) * 128:(tt - 1 + j1) * 128, :].rearrange("(j p) c -> p j c", p=128),
                           w=[pw_.d], dr=[d_pin])
                    py_, po_, yT_, oT_ = py[it % 2], po[it % 2], yT[it % 2], oT[it % 2]
                    for g in range(4):
                        js = list(range(j0, j1))
                        for jj in js:
                            var = jj if jj != 1 else (3 if first else (4 if lastt else 1))
                            kb.op(pe, lambda g=g, jj=jj, var=var: nc.tensor.matmul(
                                py_[:, g * 128:(g + 1) * 128], lhsT=pw_[:, jj, g * 64:(g + 1) * 64],
                                rhs=bands[:, var, g, :], start=(jj == js[0]), stop=(jj == js[-1])),
                                r=[pw_.d, bands.d], w=[py_.d], sig=(jj == js[-1]))
                        yield
                    kb.op(act, lambda: nc.scalar.copy(out=yT_[:].rearrange("p g t -> p (g t)"), in_=py_[:, :]),
                          r=[py_.d], w=[yT_.d])
                    yield
                    for g in range(4):
                        kb.op(pe, lambda g=g: nc.tensor.matmul(po_[:, g * 128:(g + 1) * 128], lhsT=wpool[:, g, :],
                                                               rhs=yT_[:, g, :], start=True, stop=True),
                              r=[wpool.d, yT_.d], w=[po_.d], sig=(g == 3))
                    yield
                    kb.op(dve, lambda: nc.vector.tensor_tensor(
                        out=oT_[:], in0=po_[:, :].rearrange("p (g t) -> p g t", t=128),
                        in1=lsp[:, :].unsqueeze(2).to_broadcast([64, 4, 128]), op=ALU.mult),
                        r=[po_.d, lsp.d], w=[oT_.d])
                    yield
                    kb.dma(sp, mixT[768:1024, tt * 128:(tt + 1) * 128].rearrange("(g d) t -> d g t", d=64), oT_[:],
                           r=[oT_.d], dw=[d_mix])
                    yield

                tiles = [(tt, s0, s1) for (s0, s1) in seqs for tt in range(s0, s1)]
                gens = [pool_tile(tt, s0, s1, it) for it, (tt, s0, s1) in enumerate(tiles)]
                active = []
                gi = 0
                while gi < len(gens) or active:
                    if gi < len(gens) and len(active) < (2 if OPT_ILV_A2 else 1):
                        active.append(gens[gi])
                        gi += 1
                    for g_ in list(active):
                        try:
                            next(g_)
                        except StopIteration:
                            active.remove(g_)
                kb.end_phase()

            with ExitStack() as es:
                KS = 3
                S = [T(kb, es, f"S{i}", [128, KS * 512], F32, psum=True) for i in range(2)]
                O = [T(kb, es, f"O{i}", [128, 512], F32, psum=True) for i in range(2)]
                kT = [T(kb, es, f"kT{i}", [128, TT], BF16) for i in range(2)]
                vA = [T(kb, es, f"vA{i}", [128, NT, 128], BF16) for i in range(2)]
                qT = [T(kb, es, f"qT{i}", [128, TT], BF16) for i in range(2)]
                P = [T(kb, es, f"P{i}", [128, KS * 512], BF16) for i in range(3)]
                for z_ in kT + qT:
                    kb.op(pool, lambda z_=z_: nc.gpsimd.memset(z_[64:128, :], 0.0), w=[z_.d])
                rec = [T(kb, es, f"rec{i}", [128, 512], F32) for i in range(2)]
                ob = [T(kb, es, f"ob{i}", [64, 512], BF16) for i in range(3)]
                for v_ in vA:
                    kb.op(pool, lambda v_=v_: nc.gpsimd.memset(v_[:, :, 64:128], 1.0), w=[v_.d])
                groups = []
                for g in range(2):
                    groups.append((64, KTg[g], g * 64, [(QTg[3 * g + i], (3 * g + i) * 64) for i in range(3)]))
                for h in range(6):
                    groups.append((96, KTm[h], 128 + h * 64, [(QTm[h], 384 + h * 64)]))
                heads = []
                for gi, (dk, ksrc, vcol, qs) in enumerate(groups):
                    for (qsrc, row0) in qs:
                        heads.append((gi, dk, ksrc, vcol, qsrc, row0))

                loaded_g = {}

                def load_group(gi):
                    dk, ksrc, vcol, qs = groups[gi]
                    kt_, va_ = kT[gi % 2], vA[gi % 2]
                    kb.dma(sp, kt_[0:dk, :], ksrc[:, :], w=[kt_.d], dr=[d_q])
                    kb.dma(sp, va_[:, :, 0:64], Vs[:, vcol:vcol + 64].rearrange("(i p) c -> p i c", p=128),
                           w=[va_.d], dr=[d_v])

                def load_q(hi):
                    gi, dk, ksrc, vcol, qsrc, row0 = heads[hi]
                    kb.dma(sp, qT[hi % 2][0:dk, :], qsrc[:, :], w=[qT[hi % 2].d], dr=[d_q])

                steps = []
                for hi, (gi, dk, ksrc, vcol, qsrc, row0) in enumerate(heads):
                    blocks = []
                    if not last:
                        blocks.append((0, C, 0, NCT))
                    for qb in range(L // 512):
                        blocks.append((C + qb * 512, 512, 0, NT))
                    for (q0, nq, k0, nk) in blocks:
                        ks = KS if nk % KS == 0 else 2
                        assert nk % ks == 0
                        nst = nk // ks
                        for j in range(nst):
                            steps.append(dict(hi=hi, gi=gi, dk=dk, row0=row0, q0=q0, nq=nq, kt=k0 + ks * j, ks=ks,
                                              first=(j == 0), last=(j == nst - 1)))
                blk_id = -1
                for st in steps:
                    if st["first"]:
                        blk_id += 1
                    st["blk"] = blk_id

                def emit_qk(idx):
                    st = steps[idx]
                    s_ = S[idx % 2]
                    kt_, qt_ = kT[st["gi"] % 2], qT[st["hi"] % 2]
                    dk, nq, q0 = st["dk"], st["nq"], st["q0"]
                    for a in range(st["ks"]):
                        kti = st["kt"] + a
                        kb.op(pe, lambda a=a, kti=kti: nc.tensor.matmul(
                            s_[:, a * 512:a * 512 + nq], lhsT=kt_[:, kti * 128:(kti + 1) * 128],
                            rhs=qt_[:, q0:q0 + nq], start=True, stop=True),
                            r=[kt_.d, qt_.d], w=[s_.d], sig=(a == st["ks"] - 1))

                cur_g, cur_h = -1, -1
                load_group(0)
                load_q(0)
                ob_i = 0
                for idx, st in enumerate(steps):
                    if st["hi"] != cur_h:
                        cur_h = st["hi"]
                        if cur_h + 1 < len(heads):
                            nh = heads[cur_h + 1]
                            if nh[0] != st["gi"]:
                                load_group(nh[0])
                            load_q(cur_h + 1)
                    if idx == 0:
                        emit_qk(0)
                    if idx + 1 < len(steps):
                        emit_qk(idx + 1)
                    s_, p_ = S[idx % 2], P[idx % 3]
                    nq = st["nq"]
                    scale = float(st["dk"]) ** -0.5
                    va_ = vA[st["gi"] % 2]
                    o_ = O[st["blk"] % 2]
                    ks = st["ks"]
                    if nq == 512:
                        kb.op(act, lambda: nc.scalar.activation(out=p_[:, 0:ks * 512], in_=s_[:, 0:ks * 512],
                                                                func=AF.Exp, scale=scale),
                              r=[s_.d], w=[p_.d])
                    else:
                        kb.op(act, lambda: nc.scalar.activation(
                            out=p_[:, 0:ks * 512].rearrange("p (a n) -> p a n", a=ks)[:, :, 0:nq],
                            in_=s_[:, 0:ks * 512].rearrange("p (a n) -> p a n", a=ks)[:, :, 0:nq],
                            func=AF.Exp, scale=scale),
                            r=[s_.d], w=[p_.d])
                    for a in range(ks):
                        kti = st["kt"] + a
                        kb.op(pe, lambda a=a, kti=kti: nc.tensor.matmul(
                            o_[:, 0:nq], lhsT=va_[:, kti, :], rhs=p_[:, a * 512:a * 512 + nq],
                            start=(st["first"] and a == 0), stop=(st["last"] and a == ks - 1)),
                            r=[va_.d, p_.d], w=[o_.d], sig=(a == ks - 1))
                    if st["last"]:
                        rc = rec[st["blk"] % 2]
                        ob_ = ob[ob_i % 3]
                        ob_i += 1
                        kb.op(dve, lambda: nc.vector.reciprocal(out=rc[64:128, 0:nq], in_=o_[64:128, 0:nq]),
                              r=[o_.d], w=[rc.d])
                        kb.op(dve, lambda: nc.vector.tensor_tensor(out=ob_[:, 0:nq], in0=o_[0:64, 0:nq],
                                                                   in1=rc[64:128, 0:nq], op=ALU.mult),
                              r=[o_.d, rc.d], w=[ob_.d])
                        kb.dma(sp, mixT[st["row0"]:st["row0"] + 64, st["q0"]:st["q0"] + nq], ob_[:, 0:nq],
                               r=[ob_.d], dw=[d_mix])
                kb.end_phase()

            tile0 = NCT if last else 0
            with ExitStack() as es:
                NB = 2
                wout = T(kb, es, "wout", [128, KD, D], BF16)
                wv = w_out[l].rearrange("(k p) n -> p k n", p=128)
                for k in range(KD):
                    kb.dma(pool, wout[:, k, :], wv[:, k, :], w=[wout.d])
                g2t = T(kb, es, "g2t", [128, D], F32)
                kb.dma(sp, g2t[:], bc_row(g_norm2[l]), w=[g2t.d])
                GT1 = T(kb, es, "GT1", [128, D], F32)
                A2 = T(kb, es, "A2", [128, D], F32)
                B2 = T(kb, es, "B2", [128, D], F32)
                mT = [T(kb, es, f"mT{i}", [128, KD, NB * 128], BF16) for i in range(3)]
                xblk = [T(kb, es, f"xc{i}", [128, NB, D], F32) for i in range(3)]
                tmp = [T(kb, es, f"tmpc{i}", [128, D], F32) for i in range(2)]
                junk = [T(kb, es, f"junkC{i}", [128, D], BF16) for i in range(2)]
                ssq2s = [T(kb, es, f"ssq2{i}", [128, NB], F32) for i in range(2)]
                rstd2s = [T(kb, es, f"rstd2{i}", [128, NB], F32) for i in range(2)]
                hb2 = [T(kb, es, f"hb2{i}", [128, D], BF16) for i in range(2)]
                h2blk = [T(kb, es, f"h2blk{i}", [128, KD, NB * 128], BF16) for i in range(2)]
                pyo = [T(kb, es, f"pyo{i}", [128, 2, 512], F32, psum=True) for i in range(2)]
                pT = [T(kb, es, f"pTc{i}", [128, 1024], BF16, psum=True) for i in range(2)]
                blks = list(range(tile0 // NB, NT // NB))

                def load_c1(b):
                    t0 = b * NB
                    tsl = slice(t0 * 128, (t0 + NB) * 128)
                    kb.dma(sp, mT[b % 3][:], mixT[:, tsl].rearrange("(k p) t -> p k t", p=128), w=[mT[b % 3].d],
                           dr=[d_mix])
                    kb.dma(sp, xblk[b % 3][:], x_src(l, t0, NB), w=[xblk[b % 3].d], dr=[d_xs])

                cur_stream = [None]
                load_c1(blks[0])
                tic = [0]

                def c1_block(bi, b):
                    t0 = b * NB
                    j = 1 if t0 < NCT else 0
                    if cur_stream[0] != j:
                        load_mod_tile(GT1, l, j, 2)
                        make_scale_tile(A2, g2t, l, j, 4)
                        load_mod_tile(B2, l, j, 3)
                        cur_stream[0] = j
                    if bi + 1 < len(blks):
                        load_c1(blks[bi + 1])
                    m_, x_, h2_ = mT[b % 3], xblk[b % 3], h2blk[b % 2]
                    ssq2, rstd2 = ssq2s[b % 2], rstd2s[b % 2]
                    for i in range(NB):
                        ti = tic[0]
                        tic[0] += 1
                        py_ = pyo[ti % 2]
                        tm = tmp[ti % 2]
                        for hf in range(2):
                            for k in range(KD):
                                kb.op(pe, lambda i=i, hf=hf, k=k: nc.tensor.matmul(
                                    py_[:, hf, :], lhsT=m_[:, k, i * 128:(i + 1) * 128],
                                    rhs=wout[:, k, hf * 512:(hf + 1) * 512], start=(k == 0), stop=(k == KD - 1)),
                                    r=[m_.d, wout.d], w=[py_.d], sig=(k == KD - 1))
                            yield
                        kb.op(dve, lambda: nc.vector.tensor_tensor(out=tm[:], in0=py_[:].rearrange("p a n -> p (a n)"),
                                                                   in1=GT1[:], op=ALU.mult),
                              r=[py_.d, GT1.d], w=[tm.d])
                        yield
                        kb.op(pool, lambda i=i: nc.gpsimd.tensor_tensor(out=x_[:, i, :], in0=x_[:, i, :], in1=tm[:],
                                                                        op=ALU.add), r=[x_.d, tm.d], w=[x_.d])
                        yield
                        jt = junk[ti % 2]
                        kb.op(act, lambda i=i: nc.scalar.activation(out=jt[:], in_=x_[:, i, :], func=AF.Square,
                                                                    accum_out=ssq2[:, i:i + 1]),
                              r=[x_.d], w=[jt.d, ssq2.d])
                        yield
                    rstd_chain(ssq2, rstd2, NB, 1.0 / D)
                    yield
                    tsl = slice(t0 * 128, (t0 + NB) * 128)
                    kb.dma(sp, xs[tsl, :].rearrange("(i p) c -> p i c", p=128), x_[:], r=[x_.d], dw=[d_xs])
                    for i in range(NB):
                        ti = tic[0]
                        tic[0] += 1
                        tm = tmp[ti % 2]
                        hb_ = hb2[ti % 2]
                        p_ = pT[ti % 2]
                        kb.op(dve, lambda i=i: nc.vector.scalar_tensor_tensor(
                            out=tm[:], in0=x_[:, i, :], scalar=rstd2[:, i:i + 1], in1=A2[:],
                            op0=ALU.mult, op1=ALU.mult), r=[x_.d, rstd2.d, A2.d], w=[tm.d])
                        yield
                        kb.op(pool, lambda: nc.gpsimd.tensor_tensor(out=hb_[:], in0=tm[:], in1=B2[:], op=ALU.add),
                              r=[tm.d, B2.d], w=[hb_.d])
                        yield
                        for k in range(KD):
                            kb.op(pe, lambda k=k: nc.tensor.transpose(p_[:, k * 128:(k + 1) * 128],
                                                                      hb_[:, k * 128:(k + 1) * 128], ident[:]),
                                  r=[hb_.d, ident.d], w=[p_.d], sig=(k == KD - 1))
                        yield
                        kb.op(act, lambda i=i: nc.scalar.copy(out=h2_[:, :, i * 128:(i + 1) * 128],
                                                              in_=p_[:, :].rearrange("p (k t) -> p k t", t=128)),
                              r=[p_.d], w=[h2_.d])
                        yield
                    kb.dma(sp, h2T[:, tsl].rearrange("(k p) t -> p k t", p=128), h2_[:], r=[h2_.d], dw=[d_h2])
                    yield

                bi = 0
                if blks and blks[0] * NB < NCT:
                    for _ in c1_block(0, blks[0]):
                        pass
                    bi = 1
                active = []
                while bi < len(blks) or active:
                    if bi < len(blks) and len(active) < (2 if OPT_ILV_C1 else 1):
                        active.append(c1_block(bi, blks[bi]))
                        bi += 1
                    for g_ in list(active):
                        try:
                            next(g_)
                        except StopIteration:
                            active.remove(g_)
                kb.end_phase()

            with ExitStack() as es:
                wv1 = w_mlp1[l].rearrange("(k p) n -> p k n", p=128)
                wv2 = w_mlp2[l].rearrange("(k p) n -> p k n", p=128)
                w1c = [T(kb, es, f"w1c{i}", [128, KD, 512], BF16) for i in range(8)]
                w2c = [T(kb, es, f"w2c{i}", [128, 8, D], BF16) for i in range(4)]
                for c in range(8):
                    for kh in range(2):
                        kb.dma(pool, w1c[c][:, kh * 4:(kh + 1) * 4, :], wv1[:, kh * 4:(kh + 1) * 4, c * 512:(c + 1) * 512],
                               w=[w1c[c].d])
                for c in range(4):
                    for kh in range(4):
                        kb.dma(pool, w2c[c][:, kh * 2:(kh + 1) * 2, :], wv2[:, c * 8 + kh * 2:c * 8 + (kh + 1) * 2, :],
                               w=[w2c[c].d])
                GT2 = T(kb, es, "GT2", [128, D], F32)
                NBC = 4
                hblk = [T(kb, es, f"hblk{i}", [128, KD, NBC * 128], BF16) for i in range(2)]
                aT = T(kb, es, "aT", [128, 32, NBC * 128], BF16)
                rl = [T(kb, es, f"rl{i}", [128, NBC * 128], BF16) for i in range(2)]
                x1t = [T(kb, es, f"x1t{i}", [128, D], F32) for i in range(3)]
                tmp = [T(kb, es, f"tmpd{i}", [128, D], F32) for i in range(2)]
                pm1 = [T(kb, es, f"pm1{i}", [128, 512], F32, psum=True) for i in range(2)]
                py2 = [T(kb, es, f"py2{i}", [128, 2, 512], F32, psum=True) for i in range(2)]
                blocks = []
                if not last:
                    blocks.append((0, NCT))
                tcur = NCT
                while tcur < NT:
                    n = min(NBC, NT - tcur)
                    blocks.append((tcur, n))
                    tcur += n

                def load_c2(bi):
                    t0, n = blocks[bi]
                    kb.dma(sp, hblk[bi % 2][:, :, 0:n * 128],
                           h2T[:, t0 * 128:(t0 + n) * 128].rearrange("(k p) t -> p k t", p=128),
                           w=[hblk[bi % 2].d], dr=[d_h2])

                load_c2(0)
                cur_stream = None
                ci = 0
                xi = 0
                for bi, (t0, n) in enumerate(blocks):
                    j = 1 if t0 < NCT else 0
                    if cur_stream != j:
                        load_mod_tile(GT2, l, j, 5)
                        cur_stream = j
                    if bi + 1 < len(blocks):
                        load_c2(bi + 1)
                    h_ = hblk[bi % 2]
                    N = n * 128
                    for jf in range(32):
                        pm_ = pm1[ci % 2]
                        r_ = rl[ci % 2]
                        ci += 1
                        for k in range(KD):
                            kb.op(pe, lambda jf=jf, k=k: nc.tensor.matmul(
                                pm_[:, 0:N], lhsT=w1c[jf // 4][:, k, (jf % 4) * 128:(jf % 4 + 1) * 128], rhs=h_[:, k, 0:N],
                                start=(k == 0), stop=(k == KD - 1)), r=[w1c[jf // 4].d, h_.d], w=[pm_.d],
                                sig=(k == KD - 1))
                        kb.op(act, lambda: nc.scalar.activation(out=r_[:, 0:N], in_=pm_[:, 0:N], func=AF.Relu),
                              r=[pm_.d], w=[r_.d])
                        kb.op(dve, lambda jf=jf: nc.vector.tensor_tensor(out=aT[:, jf, 0:N], in0=r_[:, 0:N],
                                                                         in1=r_[:, 0:N], op=ALU.mult),
                              r=[r_.d], w=[aT.d])
                    for i in range(n):
                        tt = t0 + i
                        xt_ = x1t[xi % 3]
                        tm = tmp[xi % 2]
                        py_ = py2[xi % 2]
                        xi += 1
                        kb.dma(sp, xt_[:], xs[tt * 128:(tt + 1) * 128, :], w=[xt_.d], dr=[d_xs])
                        for hf in range(2):
                            for jf in range(32):
                                kb.op(pe, lambda i=i, hf=hf, jf=jf: nc.tensor.matmul(
                                    py_[:, hf, :], lhsT=aT[:, jf, i * 128:(i + 1) * 128],
                                    rhs=w2c[jf // 8][:, jf % 8, hf * 512:(hf + 1) * 512], start=(jf == 0), stop=(jf == 31)),
                                    r=[aT.d, w2c[jf // 8].d], w=[py_.d], sig=(jf == 31))
                        kb.op(dve, lambda: nc.vector.tensor_tensor(out=tm[:], in0=py_[:].rearrange("p a n -> p (a n)"),
                                                                   in1=GT2[:], op=ALU.mult),
                              r=[py_.d, GT2.d], w=[tm.d])
                        kb.op(pool, lambda: nc.gpsimd.tensor_tensor(out=xt_[:], in0=xt_[:], in1=tm[:], op=ALU.add),
                              r=[xt_.d, tm.d], w=[xt_.d])
                        if last:
                            kb.dma(sp, out[(tt - NCT) * 128:(tt - NCT + 1) * 128, :], xt_[:], r=[xt_.d], dw=[d_out])
                        else:
                            kb.dma(sp, xs[tt * 128:(tt + 1) * 128, :], xt_[:], r=[xt_.d], dw=[d_xs])
                kb.end_phase()
        for s, c in d_out.pending.items():
            kb._wait(sp, s, c)
    return nc


_CACHE = {}


def _prep_consts(cfg):
    cg, sg, cm, sm = _rope_tables(cfg["L"], cfg["C"])
    return dict(ident=np.eye(128, dtype=np.float32), cosg=cg, sing=sg, cosm=cm, sinm=sm, bands=_pool_bands())


def kernel(x, c, ctx, c_ctx, w_mod, b_mod, g_norm1, g_norm2, w_in, g_q_gqa, g_k_gqa, g_cq, g_ckv,
           w_uq, w_ukv, g_q_mla, g_k_mla, w_pool, ls_pool, w_out, w_mlp1, w_mlp2):
    cfg = dict(CFG)
    x = np.asarray(x, np.float32)
    B = x.shape[0]
    cfg["L"], cfg["C"], cfg["DEPTH"] = x.shape[1], np.asarray(ctx).shape[1], np.asarray(w_mod).shape[0]
    key = (cfg["L"], cfg["C"], cfg["DEPTH"], cfg["DEBUG"])
    if key not in _CACHE:
        _CACHE[key] = (build(cfg), _prep_consts(cfg))
    nc, consts = _CACHE[key]
    f = lambda a: np.ascontiguousarray(np.asarray(a, np.float32))
    shared = dict(w_mod=f(w_mod), b_mod=f(b_mod), g_norm1=f(g_norm1), g_norm2=f(g_norm2), w_in=f(w_in),
                  g_q_gqa=f(g_q_gqa), g_k_gqa=f(g_k_gqa), g_cq=f(g_cq), g_ckv=f(g_ckv), w_uq=f(w_uq),
                  w_ukv=f(w_ukv), g_q_mla=f(g_q_mla), g_k_mla=f(g_k_mla), w_pool=f(w_pool), ls_pool=f(ls_pool),
                  w_out=f(w_out), w_mlp1=f(w_mlp1), w_mlp2=f(w_mlp2), **consts)
    c = f(c)
    c_ctx = f(c_ctx)
    ctx = f(ctx)
    in_maps = []
    for b in range(B):
        cc = np.stack([c[b], c_ctx], axis=-1).reshape(KD, 128, 2).transpose(1, 0, 2)
        in_maps.append(dict(x=np.ascontiguousarray(x[b]), ctx=np.ascontiguousarray(ctx[b]),
                            cc=np.ascontiguousarray(cc), **shared))
    res = run_bass_kernel_spmd(nc, in_maps, core_ids=list(range(B)))
    kernel.last_results = res.results
    return np.stack([np.asarray(r["out"], np.float32) for r in res.results], axis=0)
```
